# Optimizing a Trainium2 kernel written in Bass

```python
import math
import jax, jax.numpy as jnp
from jax import lax
import numpy as np

D_MODEL = 1024
BATCH = 16
SEQ = 2048
DEPTH = 2
DEC_BATCH = 32
DEC_SEQ = 64
PAST_LEN = 2048

CHUNK = 64
PREV_CHUNKS = 8
BAND_PREV = PREV_CHUNKS * CHUNK
BAND = BAND_PREV + CHUNK
MAX_REL = 128
N_REL = 2 * MAX_REL + 1
HA = 8
DA = 64
HB = 4
DB = 64
EB = 2 * DB
GROUP_W = HA * DA
MIX_W = 2 * GROUP_W
D_IN = 6 * GROUP_W
D_FF = 2816
Q_BLOCK = 128
ROPE_THETA = 10000.0
EPS = 1e-6
NEG = -1e30

kernel_name = "hybrid_chunkband_diffattn_macaron_step"


def rms_norm(x, g):
    xf = x.astype(jnp.float32)
    y = xf * lax.rsqrt(jnp.mean(xf * xf, axis=-1, keepdims=True) + EPS)
    return (y * g.astype(jnp.float32)).astype(x.dtype)


def rope(x, pos):
    d = x.shape[-1]
    half = d // 2
    inv = ROPE_THETA ** (-jnp.arange(half, dtype=jnp.float32) * 2.0 / d)
    ang = pos.astype(jnp.float32)[:, None] * inv[None, :]
    cos = jnp.cos(ang)[:, None, :]
    sin = jnp.sin(ang)[:, None, :]
    xf = x.astype(jnp.float32)
    x1, x2 = xf[..., :half], xf[..., half:]
    return jnp.concatenate([x1 * cos - x2 * sin, x2 * cos + x1 * sin], axis=-1).astype(x.dtype)


def ffn_half(x, g, w_gate, w_up, w_down):
    h = rms_norm(x, g)
    a = jax.nn.silu(jnp.einsum('bsd,df->bsf', h, w_gate)) * jnp.einsum('bsd,df->bsf', h, w_up)
    return x + 0.5 * jnp.einsum('bsf,fd->bsd', a, w_down)


def project(h, w_in, g_qa, g_ka, g_qb, g_kb, pos):
    b, s, _ = h.shape
    z = jnp.einsum('bsd,de->bse', h, w_in)
    qa, ka, va, qb, kb, vb = jnp.split(z, 6, axis=-1)
    qa = rms_norm(qa.reshape(b, s, HA, DA), g_qa)
    ka = rms_norm(ka.reshape(b, s, HA, DA), g_ka)
    va = va.reshape(b, s, HA, DA)
    qb = rope(rms_norm(qb.reshape(b, s, 2 * HB, DB), g_qb), pos).reshape(b, s, HB, 2, DB)
    kb = rope(rms_norm(kb.reshape(b, s, 2 * HB, DB), g_kb), pos).reshape(b, s, HB, 2, DB)
    vb = vb.reshape(b, s, HB, EB)
    return qa, ka, va, qb, kb, vb


def rel_bias_lookup(table, qpos, kpos):
    idx = jnp.clip(qpos[:, None] - kpos[None, :], -MAX_REL, MAX_REL) + MAX_REL
    return table[:, idx]


def band_core(q, k, v, bias, valid):
    s = jnp.einsum('bqhd,bkhd->bhqk', q, k).astype(jnp.float32) * (DA ** -0.5)
    s = s + bias[None].astype(jnp.float32)
    s = jnp.where(valid[None, None, None, :], s, NEG)
    p = jax.nn.softmax(s, axis=-1).astype(v.dtype)
    return jnp.einsum('bhqk,bkhd->bqhd', p, v)


def band_attn_prompt(q, k, v, bias):
    b, s, h, d = q.shape
    nc = s // CHUNK
    pad = ((0, 0), (BAND_PREV, 0), (0, 0), (0, 0))
    kp = jnp.pad(k, pad)
    vp = jnp.pad(v, pad)
    qc = jnp.moveaxis(q.reshape(b, nc, CHUNK, h, d), 1, 0)
    rows = jnp.arange(BAND)

    def one_chunk(args):
        c, qch = args
        start = c * CHUNK
        kband = lax.dynamic_slice_in_dim(kp, start, BAND, axis=1)
        vband = lax.dynamic_slice_in_dim(vp, start, BAND, axis=1)
        valid = rows >= BAND_PREV - start
        return band_core(qch, kband, vband, bias, valid)

    out = lax.map(one_chunk, (jnp.arange(nc), qc))
    return jnp.moveaxis(out, 0, 1).reshape(b, s, h * d)


def diff_core(q, k, v, mask, lam):
    s = jnp.einsum('bqhmd,bkhmd->bhmqk', q, k).astype(jnp.float32) * (DB ** -0.5)
    s = jnp.where(mask[None, None, None], s, NEG)
    p = jax.nn.softmax(s, axis=-1)
    a = p[:, :, 0] - lam * p[:, :, 1]
    return jnp.einsum('bhqk,bkhe->bqhe', a.astype(v.dtype), v)


def diff_attn_prompt(q, k, v, lam):
    b, s = q.shape[0], q.shape[1]
    nb = s // Q_BLOCK
    qblk = jnp.moveaxis(q.reshape(b, nb, Q_BLOCK, HB, 2, DB), 1, 0)
    kchunk = jnp.arange(s) // CHUNK

    def one_block(args):
        i, qb = args
        qchunk = (i * Q_BLOCK + jnp.arange(Q_BLOCK)) // CHUNK
        mask = kchunk[None, :] <= qchunk[:, None]
        return diff_core(qb, k, v, mask, lam)

    out = lax.map(one_block, (jnp.arange(nb), qblk))
    return jnp.moveaxis(out, 0, 1).reshape(b, s, HB, EB)


def diff_post(o, g_sub, lam_init):
    b, s = o.shape[0], o.shape[1]
    return (rms_norm(o, g_sub) * (1.0 - lam_init)).reshape(b, s, HB * EB)


def out_proj(oa, ob, w_out):
    return jnp.einsum('bse,ed->bsd', jnp.concatenate([oa, ob], axis=-1), w_out)


def setup_inputs(seed: int = 0) -> dict:
    key = jax.random.key(seed)
    ks = jax.random.split(key, 32)
    f = jnp.float32
    nrm = lambda k, shape, scale: jax.random.normal(k, shape, f) * scale
    gain = lambda k, shape: 1.0 + 0.02 * jax.random.normal(k, shape, f)
    a_len = min(BAND_PREV, PAST_LEN)
    return {
        'x_prompt': nrm(ks[0], (BATCH, SEQ, D_MODEL), 1.0),
        'x_sample': nrm(ks[1], (DEC_BATCH, DEC_SEQ, D_MODEL), 1.0),
        'cache_a_k': nrm(ks[2], (DEPTH, DEC_BATCH, a_len, HA, DA), 1.0),
        'cache_a_v': nrm(ks[3], (DEPTH, DEC_BATCH, a_len, HA, DA), 1.0),
        'cache_b_k': nrm(ks[4], (DEPTH, DEC_BATCH, PAST_LEN, HB, EB), 1.0),
        'cache_b_v': nrm(ks[5], (DEPTH, DEC_BATCH, PAST_LEN, HB, EB), 1.0),
        'g_ffn1': gain(ks[6], (DEPTH, D_MODEL)),
        'w1_gate': nrm(ks[7], (DEPTH, D_MODEL, D_FF), D_MODEL ** -0.5),
        'w1_up': nrm(ks[8], (DEPTH, D_MODEL, D_FF), D_MODEL ** -0.5),
        'w1_down': nrm(ks[9], (DEPTH, D_FF, D_MODEL), D_FF ** -0.5),
        'g_mix': gain(ks[10], (DEPTH, D_MODEL)),
        'w_in': nrm(ks[11], (DEPTH, D_MODEL, D_IN), D_MODEL ** -0.5),
        'g_qa': gain(ks[12], (DEPTH, DA)),
        'g_ka': gain(ks[13], (DEPTH, DA)),
        'g_qb': gain(ks[14], (DEPTH, DB)),
        'g_kb': gain(ks[15], (DEPTH, DB)),
        'rel_bias': nrm(ks[16], (DEPTH, HA, N_REL), 0.1),
        'lam_q1': nrm(ks[17], (DEPTH, DB), 0.1),
        'lam_k1': nrm(ks[18], (DEPTH, DB), 0.1),
        'lam_q2': nrm(ks[19], (DEPTH, DB), 0.1),
        'lam_k2': nrm(ks[20], (DEPTH, DB), 0.1),
        'g_sub': gain(ks[21], (DEPTH, EB)),
        'w_out': nrm(ks[22], (DEPTH, MIX_W, D_MODEL), MIX_W ** -0.5),
        'g_ffn2': gain(ks[23], (DEPTH, D_MODEL)),
        'w2_gate': nrm(ks[24], (DEPTH, D_MODEL, D_FF), D_MODEL ** -0.5),
        'w2_up': nrm(ks[25], (DEPTH, D_MODEL, D_FF), D_MODEL ** -0.5),
        'w2_down': nrm(ks[26], (DEPTH, D_FF, D_MODEL), D_FF ** -0.5),
    }


def reference(x_prompt, x_sample, cache_a_k, cache_a_v, cache_b_k, cache_b_v,
              g_ffn1, w1_gate, w1_up, w1_down, g_mix, w_in, g_qa, g_ka, g_qb, g_kb,
              rel_bias, lam_q1, lam_k1, lam_q2, lam_k2, g_sub, w_out,
              g_ffn2, w2_gate, w2_up, w2_down):
    bp, s = x_prompt.shape[0], x_prompt.shape[1]
    bd, t = x_sample.shape[0], x_sample.shape[1]
    past = cache_b_k.shape[2]
    a_len = cache_a_k.shape[2]
    keep_p = min(BAND_PREV, s)

    pos_p = jnp.arange(s)
    pos_s = past + jnp.arange(t)
    band_q = BAND_PREV + jnp.arange(CHUNK)
    band_k = jnp.arange(BAND)
    kpos_a = jnp.concatenate([past - a_len + jnp.arange(a_len), pos_s])
    valid_s = jnp.ones((a_len + t,), dtype=bool)
    mask_s = jnp.ones((t, past + t), dtype=bool)

    yp, ys = x_prompt, x_sample
    pak, pav, pbk, pbv, sak, sav, sbk, sbv = [], [], [], [], [], [], [], []
    for l in range(DEPTH):
        lam_init = 0.8 - 0.6 * math.exp(-0.3 * l)
        lam = (jnp.exp(jnp.sum(lam_q1[l].astype(jnp.float32) * lam_k1[l].astype(jnp.float32)))
               - jnp.exp(jnp.sum(lam_q2[l].astype(jnp.float32) * lam_k2[l].astype(jnp.float32)))
               + lam_init)

        yp = ffn_half(yp, g_ffn1[l], w1_gate[l], w1_up[l], w1_down[l])
        qa, ka, va, qb, kb, vb = project(rms_norm(yp, g_mix[l]), w_in[l], g_qa[l], g_ka[l], g_qb[l], g_kb[l], pos_p)
        oa = band_attn_prompt(qa, ka, va, rel_bias_lookup(rel_bias[l], band_q, band_k))
        ob = diff_post(diff_attn_prompt(qb, kb, vb, lam), g_sub[l], lam_init)
        yp = yp + out_proj(oa, ob, w_out[l])
        yp = ffn_half(yp, g_ffn2[l], w2_gate[l], w2_up[l], w2_down[l])
        pak.append(ka[:, s - keep_p:])
        pav.append(va[:, s - keep_p:])
        pbk.append(kb.reshape(bp, s, HB, EB))
        pbv.append(vb)

        ys = ffn_half(ys, g_ffn1[l], w1_gate[l], w1_up[l], w1_down[l])
        qa, ka, va, qb, kb, vb = project(rms_norm(ys, g_mix[l]), w_in[l], g_qa[l], g_ka[l], g_qb[l], g_kb[l], pos_s)
        ka_all = jnp.concatenate([cache_a_k[l], ka], axis=1)
        va_all = jnp.concatenate([cache_a_v[l], va], axis=1)
        oa = band_core(qa, ka_all, va_all, rel_bias_lookup(rel_bias[l], pos_s, kpos_a), valid_s).reshape(bd, t, GROUP_W)
        kb_all = jnp.concatenate([cache_b_k[l].reshape(bd, past, HB, 2, DB), kb], axis=1)
        vb_all = jnp.concatenate([cache_b_v[l], vb], axis=1)
        ob = diff_post(diff_core(qb, kb_all, vb_all, mask_s, lam), g_sub[l], lam_init)
        ys = ys + out_proj(oa, ob, w_out[l])
        ys = ffn_half(ys, g_ffn2[l], w2_gate[l], w2_up[l], w2_down[l])
        sak.append(ka)
        sav.append(va)
        sbk.append(kb.reshape(bd, t, HB, EB))
        sbv.append(vb)

    return (yp, ys,
            jnp.stack(pak), jnp.stack(pav), jnp.stack(pbk), jnp.stack(pbv),
            jnp.stack(sak), jnp.stack(sav), jnp.stack(sbk), jnp.stack(sbv))
```

```python
import math
import types
import numpy as np
import concourse.bass as bass
import concourse.mybir as mybir
from concourse.bass_utils import run_bass_kernel_spmd

F32 = mybir.dt.float32
BF16 = mybir.dt.bfloat16
AF = mybir.ActivationFunctionType
ALU = mybir.AluOpType

D_MODEL = 1024
D_FF = 2816
NFC = 22
SEQ = 2048
PAST = 2048
NCORES = 8
EPS = 1e-6
WT = 2048
NSLOT = 4


def _freeze(fn):
    if fn.__closure__ is None:
        return fn
    cells = []
    for c in fn.__closure__:
        try:
            cells.append(types.CellType(c.cell_contents))
        except ValueError:
            cells.append(c)
    return types.FunctionType(fn.__code__, fn.__globals__, fn.__name__, fn.__defaults__, tuple(cells))


class Prog:
    def __init__(self, nc, same_engine_sync=True):
        self.nc = nc
        self.eng = {"pe": nc.tensor, "act": nc.scalar, "dve": nc.vector,
                    "pool": nc.gpsimd, "sp": nc.sync}
        self.ops = []
        self.last_w = {}
        self.readers = {}
        self.same_engine_sync = same_engine_sync
        self.dma_fill = {}

    def add(self, eng, emit, reads=(), writes=(), dma_key=None):
        idx = len(self.ops)
        deps = set()
        for r in reads:
            lw = self.last_w.get(r)
            if lw is not None:
                deps.add(lw)
            self.readers.setdefault(r, []).append(idx)
        for w in writes:
            lw = self.last_w.get(w)
            if lw is not None:
                deps.add(lw)
            rs = self.readers.get(w)
            if rs:
                deps.update(rs)
            self.last_w[w] = idx
            self.readers[w] = []
        deps.discard(idx)
        fill = None
        if dma_key is not None:
            fill = self.dma_fill.get(dma_key, 0) + 1
            self.dma_fill[dma_key] = fill
        self.ops.append([eng, _freeze(emit), deps, dma_key, fill, False, 0])
        return idx

    def emit_all(self):
        nc = self.nc
        ops = self.ops
        pruned = []
        for j, (eng, emit, deps, key, fill, _, _) in enumerate(ops):
            best = {}
            for i in deps:
                e_i, _, _, k_i, _, _, _ = ops[i]
                sid = ("dma", k_i) if k_i is not None else e_i
                if k_i is None and e_i == eng and key is None:
                    if eng == "pe" or not self.same_engine_sync:
                        continue
                if sid not in best or best[sid] < i:
                    best[sid] = i
            pruned.append(sorted(best.values()))
            for i in best.values():
                ops[i][5] = True
        cnt = {}
        for op in ops:
            if op[3] is None and op[5]:
                cnt[op[0]] = cnt.get(op[0], 0) + 1
                op[6] = cnt[op[0]]
        esem = {e: nc.alloc_semaphore("s_" + e) for e in ("pe", "act", "dve", "pool")}
        dsem = {}
        for n, k in enumerate(self.dma_fill):
            dsem[k] = nc.alloc_semaphore("d%d" % n)
        waited = {}
        n_wait = 0
        for j, (eng, emit, deps, key, fill, mark, c) in enumerate(ops):
            E = self.eng[eng]
            for i in pruned[j]:
                e_i, _, _, k_i, f_i, _, c_i = ops[i]
                if k_i is not None:
                    sem, val, sk = dsem[k_i], 16 * f_i, ("d", k_i)
                else:
                    sem, val, sk = esem[e_i], c_i, e_i
                wk = (eng, sk)
                if waited.get(wk, 0) >= val:
                    continue
                waited[wk] = val
                E.wait_ge(sem, val)
                n_wait += 1
            inst = emit()
            if key is not None:
                inst.then_inc(dsem[key], 16)
            elif mark:
                inst.then_inc(esem[eng], 1)
        for k, f in self.dma_fill.items():
            if waited.get(("sp", ("d", k)), 0) < 16 * f:
                nc.sync.wait_ge(dsem[k], 16 * f)
        return n_wait


def _weight_tiles(NL):
    cat = {}
    for l in range(NL):
        for f in (1, 2):
            for ft in range(11):
                cat[(l, "g%d" % f, ft)] = len(cat)
                cat[(l, "u%d" % f, ft)] = len(cat)
            for half in range(2):
                for t6 in range(6):
                    cat[(l, "d%d" % f, half * 6 + t6)] = len(cat)
            if f == 1:
                for ct in range(12):
                    cat[(l, "in", ct)] = len(cat)
                for ot in range(4):
                    cat[(l, "out", ot)] = len(cat)
    return cat


def build(NPS=2, NSS=4, NL=2, same_engine_sync=True, debug=None):
    nc = bass.Bass("TRN2", target_bir_lowering=False)
    P = Prog(nc, same_engine_sync)
    dbg_outs = {}

    def din(name, shape, dt=F32):
        return nc.dram_tensor(name, list(shape), dt, kind="ExternalInput").ap()

    def dout(name, shape, dt=F32):
        return nc.dram_tensor(name, list(shape), dt, kind="ExternalOutput").ap()

    xp = din("xp", [max(NPS, 1), SEQ, D_MODEL])
    xs = din("xs", [max(NSS, 1) * 64, D_MODEL])
    cak = din("cak", [NL, max(NSS, 1), 512, 512])
    cav = din("cav", [NL, max(NSS, 1), 512, 512])
    cbk = din("cbk", [NL, max(NSS, 1), PAST, 512])
    cbv = din("cbv", [NL, max(NSS, 1), PAST, 512])
    W = {}
    for f in (1, 2):
        W["g%d" % f] = din("w%d_gate" % f, [NL, D_MODEL, D_FF])
        W["u%d" % f] = din("w%d_up" % f, [NL, D_MODEL, D_FF])
        W["d%d" % f] = din("w%d_down" % f, [NL, D_FF, D_MODEL])
    W["in"] = din("w_in", [NL, D_MODEL, 3072])
    W["out"] = din("w_out", [NL, D_MODEL, D_MODEL])
    g_ffn1 = din("g_ffn1", [NL, D_MODEL])
    g_mix = din("g_mix", [NL, D_MODEL])
    g_ffn2 = din("g_ffn2", [NL, D_MODEL])
    g_qa = din("g_qa", [NL, 64])
    g_ka = din("g_ka", [NL, 64])
    g_qb = din("g_qb", [NL, 64])
    g_kb = din("g_kb", [NL, 64])
    g_sub = din("g_sub", [NL, 128])
    rel_bias = din("rel_bias", [NL, 8, 257])
    lam_in = {k: din(k, [NL, 64]) for k in ("lam_q1", "lam_k1", "lam_q2", "lam_k2")}
    c_ident = din("c_ident", [128, 128])
    c_anti = din("c_anti", [128, 128])
    c_blk = din("c_blk", [128, 128])
    c_rperm = din("c_rperm", [128, 128])
    c_cosp = din("c_cosp", [128, SEQ])
    c_sinp = din("c_sinp", [128, SEQ])
    c_coss = din("c_coss", [128, 256])
    c_sins = din("c_sins", [128, 256])

    yp = dout("yp", [max(NPS, 1), SEQ, D_MODEL])
    ys = dout("ys", [max(NSS, 1) * 64, D_MODEL])
    pak = dout("pak", [NL, max(NPS, 1), 512, 512])
    pav = dout("pav", [NL, max(NPS, 1), 512, 512])
    pbk = dout("pbk", [NL, max(NPS, 1), SEQ, 512])
    pbv = dout("pbv", [NL, max(NPS, 1), SEQ, 512])
    sak = dout("sak", [NL, max(NSS, 1), 64, 512])
    sav = dout("sav", [NL, max(NSS, 1), 64, 512])
    sbk = dout("sbk", [NL, max(NSS, 1), 64, 512])
    sbv = dout("sbv", [NL, max(NSS, 1), 64, 512])

    cat = _weight_tiles(NL)
    wsc = nc.dram_tensor("wsc", [len(cat), 128, WT], BF16, kind="Internal").ap()
    tpad = nc.dram_tensor("tpad", [NL * 8, 384], F32, kind="Internal").ap()

    A = nc.alloc_sbuf_tensor
    ident = A("ident", [128, 128], F32)
    identb = A("identb", [128, 128], BF16)
    antib = A("antib", [128, 128], BF16)
    onesb = A("onesb", [128, 128], BF16)
    blkb = A("blkb", [128, 128], BF16)
    rpermb = A("rpermb", [128, 128], BF16)
    maskB = A("maskB", [128, 64], BF16)
    maskA0 = A("maskA0", [128, 128], BF16)
    epsb = A("epsb", [128, 1], F32)
    lneps = A("lneps", [128, 1], F32)
    gx = A("gx", [128, 3 * NL * 8], F32)
    gh = A("gh", [128, 4 * NL], F32)
    gsub = A("gsub", [128, NL], F32)
    nlam = A("nlam", [128, NL], F32)
    bconst = A("bconst", [128, NL * 8], F32)
    btile = A("btile", [128, NL * 8 * 2, 128], BF16)
    xT = A("xT", [128, 8, 512], F32)
    hT = A("hT", [128, 8, 512], BF16)
    BIG = A("BIG", [128, NFC * 512], BF16)
    wring = A("wring", [128, NSLOT, WT], BF16)
    kbT = [A("kbT%d" % l, [128, 4, 2048], BF16) for l in range(2)]
    vbt = [A("vb%d" % l, [128, 16, 512], BF16) for l in range(2)]
    kaT = [A("kaT%d" % l, [128, 4, 1024], BF16) for l in range(2)]
    vat = [A("va%d" % l, [128, 8, 512], BF16) for l in range(2)]
    eT = A("eT", [128, 6, 512], BF16)
    tA = [A("tA%d" % i, [128, 512], F32) for i in range(2)]
    tB = [A("tB%d" % i, [128, 512], F32) for i in range(2)]
    tC = [A("tC%d" % i, [128, 512], F32) for i in range(2)]
    tz = [A("tz%d" % i, [128, 512], BF16) for i in range(2)]
    tq = [A("tq%d" % i, [128, 512], BF16) for i in range(2)]
    cosT = A("cosT", [128, 512], F32)
    sinT = A("sinT", [128, 512], F32)
    stg = A("stg", [128, 4, 512], F32)
    lamt = A("lamt", [128, 4, 64], F32)
    lamr = A("lamr", [128, 4], F32)
    rstdT = A("rstdT", [128, 512], F32)
    rtok = A("rtok", [128, 4], F32)
    epsv = lamt[:].rearrange("p a b -> p (a b)").bitcast(BF16)[:, 0:512]

    aT = BIG[:].rearrange("p (c t) -> p c t", c=NFC)
    xstg = BIG[:, 0:8192].bitcast(F32).rearrange("p (a b) -> p a b", a=4)
    sq8 = BIG[:, 0:4096].rearrange("p (c t) -> p c t", c=8)
    qpad = BIG[:, 0:8192].rearrange("p (c t) -> p c t", c=16)

    psb = [nc.alloc_psum_tensor("psb%d" % b, [128, 512], F32) for b in range(8)]
    rot_state = [0]

    def rot():
        b = rot_state[0] % 4
        rot_state[0] += 1
        return b

    def PS(b):
        return ("ps", b)

    def Bk(lo, hi):
        return [("B", c) for c in range(lo, hi)]

    ncd = nc.allow_non_contiguous_dma

    def dma(q, out, in_, reads, writes, key, nonc=False):
        E = nc.sync if q == "sp" else (nc.gpsimd if q == "pool" else nc.scalar)

        def emit():
            if nonc:
                with ncd(reason="tiny setup transfer"):
                    return E.dma_start(out=out, in_=in_)
            return E.dma_start(out=out, in_=in_)
        P.add(q, emit, reads, writes, dma_key=key)

    ukey = [0]

    def once_key():
        ukey[0] += 1
        return ("once", ukey[0] % 8)

    def dbg(name, ap, shape, reads):
        if debug is None or name not in debug:
            return
        o = dout("dbg_" + name, shape, ap.dtype)
        dbg_outs[name] = o
        dma("sp", o, ap, reads, [], ("dbg", name))

    def load_cast(dst_b, src):
        dma("sp", tA[0][:, 0:128], src, [], [("t", 0, "A")], ("once", 0))
        P.add("dve", lambda: nc.vector.tensor_copy(out=dst_b[:], in_=tA[0][:, 0:128]),
              [("t", 0, "A")], ["c"])

    dma("sp", ident[:], c_ident[:, :], [], ["c"], ("once", 1))
    load_cast(identb, c_ident[:, :])
    load_cast(antib, c_anti[:, :])
    load_cast(blkb, c_blk[:, :])
    load_cast(rpermb, c_rperm[:, :])
    P.add("pool", lambda: nc.gpsimd.memset(onesb[:], 1.0), [], ["c"])
    P.add("pool", lambda: nc.gpsimd.memset(epsb[:], EPS), [], ["c"])
    P.add("pool", lambda: nc.gpsimd.memset(lneps[:], float(math.log(EPS))), [], ["c"])
    P.add("pool", lambda: nc.gpsimd.memset(maskB[:], 0.0), [], ["c"])
    P.add("pool", lambda: nc.gpsimd.memset(maskB[64:128, :], -30000.0), [], ["c"])
    P.add("pool", lambda: nc.gpsimd.memset(maskA0[:], 0.0), [], ["c"])
    P.add("pool", lambda: nc.gpsimd.memset(maskA0[0:64, 64:128], -30000.0), [], ["c"])
    for wi, gsrc in enumerate((g_ffn1, g_mix, g_ffn2)):
        for l in range(NL):
            o = (wi * NL + l) * 8
            dma("sp", gx[:, o:o + 8], gsrc[l].rearrange("(c p) -> p c", p=128), [], ["g"], ("once", 2), nonc=True)
    for gi, gsrc in enumerate((g_qa, g_ka, g_qb, g_kb)):
        for l in range(NL):
            for hf in range(2):
                dma("sp", gh[hf * 64:(hf + 1) * 64, gi * NL + l:gi * NL + l + 1],
                    gsrc[l].rearrange("(p o) -> p o", o=1), [], ["g"], ("once", 3), nonc=True)
    for l in range(NL):
        lam_init = 0.8 - 0.6 * math.exp(-0.3 * l)
        dma("sp", tA[1][:, l:l + 1], g_sub[l].rearrange("(p o) -> p o", o=1), [], [("t", 1, "A")], ("once", 4), nonc=True)
        P.add("dve", lambda l=l, li=lam_init: nc.vector.tensor_scalar(
            out=gsub[:, l:l + 1], in0=tA[1][:, l:l + 1], scalar1=float(1.0 - li), scalar2=None, op0=ALU.mult),
            [("t", 1, "A")], ["g"])
        for k, nm in enumerate(("lam_q1", "lam_k1", "lam_q2", "lam_k2")):
            dma("sp", lamt[:, k, :], lam_in[nm][l:l + 1, :].partition_broadcast(128), [], ["lamt"], ("once", 5))
        for k in range(2):
            P.add("dve", lambda k=k: nc.vector.tensor_tensor(out=lamt[:, 2 * k, :], in0=lamt[:, 2 * k, :],
                                                             in1=lamt[:, 2 * k + 1, :], op=ALU.mult),
                  ["lamt"], ["lamt"])
            P.add("dve", lambda k=k: nc.vector.reduce_sum(out=lamr[:, k:k + 1], in_=lamt[:, 2 * k, :],
                                                          axis=mybir.AxisListType.X), ["lamt"], ["lamr"])
        P.add("act", lambda: nc.scalar.activation(out=lamr[:, 2:4], in_=lamr[:, 0:2], func=AF.Exp), ["lamr"], ["lamr"])
        P.add("dve", lambda l=l, li=lam_init: nc.vector.scalar_tensor_tensor(
            out=nlam[:, l:l + 1], in0=lamr[:, 3:4], scalar=float(-li), in1=lamr[:, 2:3], op0=ALU.add, op1=ALU.subtract),
            ["lamr"], ["g"])
        dma("sp", bconst[:, l * 8:(l + 1) * 8],
            bass.AP(tensor=rel_bias.tensor, offset=l * 8 * 257 + 256, ap=[[0, 128], [257, 8]]),
            [], ["bc"], ("once", 6), nonc=True)
        dma("sp", tB[0][0:8, 0:257], rel_bias[l], [], [("t", 0, "B")], ("once", 7))
        P.add("dve", lambda: nc.vector.tensor_copy(out=tB[0][0:8, 257:384],
                                                   in_=tB[0][0:8, 256:257].to_broadcast([8, 127])),
              [("t", 0, "B")], [("t", 0, "B")])
        dma("sp", tpad[l * 8:(l + 1) * 8, :], tB[0][0:8, 0:384], [("t", 0, "B")], ["tpad"], ("tpadw", l))
        for h in range(8):
            for di, Dv in enumerate((128, 0)):
                s = (h * 2 + di) % 2
                src = bass.AP(tensor=tpad.tensor, offset=(l * 8 + h) * 384 + Dv + 1, ap=[[1, 128], [1, 128]])
                dma("sp", tC[s][:, 0:128], src, ["tpad"], [("t", s, "C")], ("btl", s))
                bi = (l * 8 + h) * 2 + di
                P.add("dve", lambda s=s, bi=bi, l=l, h=h: nc.vector.tensor_scalar(
                    out=btile[:, bi, :], in0=tC[s][:, 0:128], scalar1=bconst[:, l * 8 + h:l * 8 + h + 1],
                    scalar2=8.0, op0=ALU.subtract, op1=ALU.mult), [("t", s, "C"), "bc"], ["bt"])
                if di == 1:
                    P.add("pool", lambda bi=bi: nc.gpsimd.memset(btile[0:64, bi, 0:64], -30000.0), [], ["bt"])

    cat_list = list(cat.items())
    cv_done = [0]
    CV_LEAD = 36
    CV_KEYS = 8

    def convert_upto(n_hi):
        while cv_done[0] < min(n_hi, len(cat_list)):
            n = cv_done[0]
            cv_done[0] += 1
            (l, name, idx), tid = cat_list[n]
            if name[0] in "gu" or name in ("in", "out"):
                src = W[name][l][:, idx * 256:(idx + 1) * 256].rearrange("(kc p) f -> p kc f", p=128)
                dst = wsc[tid].rearrange("p (kc f) -> p kc f", kc=8)
            else:
                half, t6 = idx // 6, idx % 6
                nfc = min(4, NFC - 4 * t6)
                src = W[name][l][t6 * 512:t6 * 512 + nfc * 128, half * 512:(half + 1) * 512].rearrange(
                    "(fc p) d -> p fc d", p=128)
                dst = wsc[tid][:, 0:nfc * 512].rearrange("p (fc d) -> p fc d", fc=nfc)
            dma("pool", dst, src, [], [("ws", tid), ("cvk", n % CV_KEYS)], ("cv", n % CV_KEYS))

    wcount = [0]

    def wload(l, name, idx, nel=WT):
        tid = cat[(l, name, idx)]
        convert_upto(tid + 1 + CV_LEAD)
        s = wcount[0] % NSLOT
        wcount[0] += 1
        dma("sp", wring[:, s, 0:nel], wsc[tid][:, 0:nel], [("ws", tid)], [("w", s)], ("w", s))
        return s

    nstate = {"bank": None, "pend": []}
    sqbuf = [(tq[0], ("t", 0, "q")), (tq[1], ("t", 1, "q")), (tz[0], ("t", 0, "z")), (tz[1], ("t", 1, "z"))]

    def norm_flush():
        for (b, dc, T) in nstate["pend"]:
            sb, stok = sqbuf[dc % 4]
            P.add("pe", lambda: nc.tensor.matmul(psb[b][:, 0:T], onesb[:], sb[:, 0:T],
                                                 start=(dc == 0), stop=(dc == 7)), [stok, "c"], [PS(b)])
        nstate["pend"] = []

    def norm_feed(dc, T, gcol, bank=None):
        if gcol is None:
            return
        if dc == 0:
            nstate["bank"] = rot() if bank is None else bank
        b = nstate["bank"]
        sb, stok = sqbuf[dc % 4]
        P.add("act", lambda: nc.scalar.activation(out=sb[:, 0:T], in_=xT[:, dc, 0:T], func=AF.Square),
              [("xT", dc)], [stok])
        P.add("dve", lambda: nc.vector.tensor_scalar(
            out=hT[:, dc, 0:T], in0=xT[:, dc, 0:T], scalar1=gx[:, gcol + dc:gcol + dc + 1], scalar2=None,
            op0=ALU.mult), [("xT", dc), "g"], [("hT", dc)])
        nstate["pend"].append((b, dc, T))

    def norm_finish(T, for_mix=False, blocks=None):
        norm_flush()
        b = nstate["bank"]
        P.add("act", lambda: nc.scalar.activation(out=rstdT[:, 0:T], in_=psb[b][:, 0:T], func=AF.Ln,
                                                  scale=1.0 / D_MODEL, bias=epsb[:, 0:1]), ["c"], [PS(b), "rstd"])
        if for_mix:
            P.add("act", lambda: nc.scalar.activation(out=epsv[:, 0:T], in_=rstdT[:, 0:T], func=AF.Exp,
                                                      bias=lneps[:, 0:1]), ["rstd", "c"], ["epsv", "lamt"])
        P.add("act", lambda: nc.scalar.activation(out=rstdT[:, 0:T], in_=rstdT[:, 0:T], func=AF.Exp, scale=-0.5),
              [], ["rstd"])
        if for_mix:
            nstate["rtok"] = (T, blocks)

    def rtok_part():
        T, blocks = nstate["rtok"]
        if True:
            bt = rot8()
            for k, (c0, n) in enumerate(blocks):
                P.add("pe", lambda k=k, c0=c0, n=n: nc.tensor.transpose(
                    psb[bt][0:n, k * 128:(k + 1) * 128], rstdT[:, c0:c0 + n], ident[:]), ["rstd", "c"], [PS(bt)])
            nb = len(blocks)
            n0 = blocks[0][1]
            P.add("dve", lambda: nc.vector.tensor_copy(
                out=rtok[0:n0, 0:nb], in_=psb[bt][0:n0, 0:nb * 128].rearrange("p (k f) -> p k f", k=nb)[:, :, 0]),
                [], [PS(bt), "rtok"])

    def ffn(l, f, T, pre=None):
        if f == 1:
            gnext = (1 * NL + l) * 8
        else:
            gnext = (0 * NL + l + 1) * 8 if l + 1 < NL else None
        HT = [("hT", c) for c in range(8)]
        for ft in range(11):
            sg = wload(l, "g%d" % f, ft)
            su = wload(l, "u%d" % f, ft)
            wg = wring[:, sg, :].rearrange("p (kc f) -> p kc f", kc=8)
            wu = wring[:, su, :].rearrange("p (kc f) -> p kc f", kc=8)
            for fl in range(2):
                fc = 2 * ft + fl
                bg, bu = rot(), rot()
                for kc in range(8):
                    P.add("pe", lambda kc=kc, bg=bg, wg=wg, fl=fl: nc.tensor.matmul(
                        psb[bg][:, 0:T], wg[:, kc, fl * 128:(fl + 1) * 128], hT[:, kc, 0:T],
                        start=(kc == 0), stop=(kc == 7)), [("w", sg), ("hT", kc)], [PS(bg)])
                for kc in range(8):
                    P.add("pe", lambda kc=kc, bu=bu, wu=wu, fl=fl: nc.tensor.matmul(
                        psb[bu][:, 0:T], wu[:, kc, fl * 128:(fl + 1) * 128], hT[:, kc, 0:T],
                        start=(kc == 0), stop=(kc == 7)), [("w", su), ("hT", kc)], [PS(bu)])
                ts = fc % 2
                if fc == 0:
                    norm_finish(T)
                P.add("dve", lambda bg=bg, ts=ts: nc.vector.tensor_tensor(
                    out=tA[ts][:, 0:T], in0=psb[bg][:, 0:T], in1=rstdT[:, 0:T], op=ALU.mult),
                    ["rstd"], [PS(bg), ("t", ts, "A")])
                P.add("act", lambda ts=ts: nc.scalar.activation(out=tA[ts][:, 0:T], in_=tA[ts][:, 0:T],
                                                                func=AF.Silu), [], [("t", ts, "A")])
                P.add("dve", lambda bu=bu, ts=ts, fc=fc: nc.vector.tensor_tensor(
                    out=aT[:, fc, 0:T], in0=psb[bu][:, 0:T], in1=tA[ts][:, 0:T], op=ALU.mult),
                    [("t", ts, "A")], [PS(bu), ("B", fc)])
        if pre is not None:
            pre()
        for half in range(2):
            for t6 in range(6):
                nfc = min(4, NFC - 4 * t6)
                sd = wload(l, "d%d" % f, half * 6 + t6, nel=nfc * 512)
                wd = wring[:, sd, 0:nfc * 512].rearrange("p (fc d) -> p fc d", fc=nfc)
                if half == 1 and t6 == 2:
                    norm_flush()
                for fl in range(nfc):
                    fc = 4 * t6 + fl
                    for dcl in range(4):
                        ab = (4 + dcl) if half == 0 else dcl
                        P.add("pe", lambda fl=fl, fc=fc, dcl=dcl, wd=wd, ab=ab: nc.tensor.matmul(
                            psb[ab][:, 0:T], wd[:, fl, dcl * 128:(dcl + 1) * 128], aT[:, fc, 0:T],
                            start=(fc == 0), stop=(fc == NFC - 1)), [("w", sd), ("B", fc)], [PS(ab)])
            for dcl in range(4):
                dc = half * 4 + dcl
                tb_ = tB[dcl % 2]
                ab = (4 + dcl) if half == 0 else dcl
                P.add("dve", lambda dcl=dcl, tb_=tb_, ab=ab: nc.vector.tensor_tensor(
                    out=tb_[:, 0:T], in0=psb[ab][:, 0:T], in1=rstdT[:, 0:T], op=ALU.mult),
                    ["rstd"], [PS(ab), ("t", dcl % 2, "B")])
                P.add("dve", lambda dc=dc, tb_=tb_: nc.vector.scalar_tensor_tensor(
                    out=xT[:, dc, 0:T], in0=tb_[:, 0:T], scalar=0.5, in1=xT[:, dc, 0:T],
                    op0=ALU.mult, op1=ALU.add), [("t", dcl % 2, "B")], [("xT", dc)])
            for dcl in range(4):
                norm_feed(half * 4 + dcl, T, gnext, bank=4)

    stgx = stg[:].rearrange("p a b -> p (a b)").rearrange("p (t d) -> p t d", t=2)
    XPIECE = {2: [(tA[0], ("t", 0, "A")), (tA[1], ("t", 1, "A"))],
              3: [(tC[0], ("t", 0, "C")), (tC[1], ("t", 1, "C"))]}

    def prefetch_x(src_rows, T):
        ntt = T // 128
        dma("sp", stgx[:, 0:2, :], src_rows[0:256, :].rearrange("(t p) d -> p t d", p=128), [],
            [("stg", k) for k in range(4)], "xin0")
        for tt in range(2, ntt):
            for hf in range(2):
                buf, tok = XPIECE[tt][hf]
                dma("sp", buf[:, :], src_rows[tt * 128:(tt + 1) * 128, hf * 512:(hf + 1) * 512], [], [tok],
                    "xin%d" % (1 + (tt - 2) * 2 + hf))

    def load_x(T):
        ntt = T // 128
        for dc in range(8):
            b = rot()
            for tt in range(ntt):
                if tt < 2:
                    src, toks = stgx[:, tt, dc * 128:(dc + 1) * 128], [("stg", 2 * tt), ("stg", 2 * tt + 1)]
                else:
                    buf, tok = XPIECE[tt][dc // 4]
                    src, toks = buf[:, (dc % 4) * 128:(dc % 4 + 1) * 128], [tok]
                P.add("pe", lambda tt=tt, src=src: nc.tensor.transpose(
                    psb[b][:, tt * 128:(tt + 1) * 128], src, ident[:]), toks + ["c"], [PS(b)])
            norm_flush()
            if dc % 2 == 0:
                P.add("act", lambda: nc.scalar.copy(out=xT[:, dc, 0:T], in_=psb[b][:, 0:T]),
                      [], [PS(b), ("xT", dc)])
            else:
                P.add("dve", lambda: nc.vector.tensor_copy(out=xT[:, dc, 0:T], in_=psb[b][:, 0:T]),
                      [], [PS(b), ("xT", dc)])
            norm_feed(dc, T, 0, bank=7)

    def store_y(dst_rows, T):
        ntt = T // 128
        for tt in range(ntt):
            for hf in range(2):
                b = rot()
                for d4 in range(4):
                    dc = hf * 4 + d4
                    P.add("pe", lambda dc=dc, d4=d4, tt=tt, b=b: nc.tensor.transpose(
                        psb[b][:, d4 * 128:(d4 + 1) * 128], xT[:, dc, tt * 128:(tt + 1) * 128], ident[:]),
                        [("xT", dc), "c"], [PS(b)])
                if hf == 0:
                    P.add("act", lambda tt=tt, hf=hf, b=b: nc.scalar.copy(
                        out=xstg[:, tt, hf * 512:(hf + 1) * 512], in_=psb[b][:, :]), [], [PS(b)] + Bk(0, 16))
                else:
                    P.add("dve", lambda tt=tt, hf=hf, b=b: nc.vector.tensor_copy(
                        out=xstg[:, tt, hf * 512:(hf + 1) * 512], in_=psb[b][:, :]), [], [PS(b)] + Bk(0, 16))
        dma("pool", dst_rows.rearrange("(t p) d -> p t d", p=128), xstg[:, 0:ntt, :], Bk(0, 16), [], "yst")

    def store_rows(src_f32, ntok_p, dst, reads, slot):
        dma("pool", dst, src_f32, reads, [], ("stg", slot))

    rot8_state = [0]

    def rot8():
        b = rot8_state[0] % 8
        rot8_state[0] += 1
        return b

    def mix(l, grp):
        T = grp["T"]
        ntt = T // 128
        isP = grp["kind"] == "p"
        if isP:
            vblocks = [(tb * 128, 128) for tb in range(ntt)]
        else:
            vblocks = [(tb * 64, 64) for tb in range(grp["nseq"])]
        norm_finish(T, for_mix=True, blocks=vblocks)
        if isP:
            i = grp["i"]
            sq_ = grp["seq"]
            tok0 = i * 512
            dma("sp", cosT[:, 0:T], c_cosp[:, i * 512:(i + 1) * 512], [], ["cs"], "cs")
            dma("sp", sinT[:, 0:T], c_sinp[:, i * 512:(i + 1) * 512], [], ["cs"], "cs")
        else:
            dma("sp", cosT[:, 0:T], c_coss[:, 0:T], [], ["cs"], "cs")
            dma("sp", sinT[:, 0:T], c_sins[:, 0:T], [], ["cs"], "cs")
        P.add("pool", lambda: nc.gpsimd.memset(BIG[:, 0:8192], 0.0), [], Bk(0, 16))
        oth = 1 - l
        wtl = {}

        def get_w(gidx):
            if gidx not in wtl:
                s0 = wload(l, "in", 2 * gidx)
                s1 = wload(l, "in", 2 * gidx + 1)
                wtl[gidx] = ([wring[:, s0, :].rearrange("p (kc f) -> p kc f", kc=8),
                              wring[:, s1, :].rearrange("p (kc f) -> p kc f", kc=8)], [s0, s1])
            return wtl[gidx]

        GIDX = {"qa": 0, "ka": 1, "va": 2, "qb": 3, "kb": 4, "vb": 5}
        jobs = []

        def v_job(kind, tb):
            isBv = kind == "vb"
            bsz = 128 if isP else 64

            def Pst():
                wv, ws = get_w(GIDX[kind])
                b = rot8()
                for half in range(2):
                    for kc in range(8):
                        P.add("pe", lambda kc=kc, half=half: nc.tensor.matmul(
                            psb[b][0:bsz, half * 256:(half + 1) * 256], hT[:, kc, tb * bsz:(tb + 1) * bsz],
                            wv[half][:, kc, :], start=(kc == 0), stop=(kc == 7)),
                            [("w", ws[half]), ("hT", kc)], [PS(b)])
                slot = tb % 4
                P.add("act", lambda: nc.scalar.activation(out=stg[0:bsz, slot, :], in_=psb[b][0:bsz, :], func=AF.Copy,
                                                          scale=rtok[0:bsz, tb:tb + 1]), ["rtok"], [PS(b), ("stg", slot)])
                if isP:
                    kt = (i * 4 + tb)
                    if isBv:
                        dstt, tok = vbt[l][:, kt, :], ("vb", l, kt)
                    else:
                        dstt, tok = vat[l][:, kt % 8, :], ("va", l, kt % 8)
                    P.add("pool", lambda: nc.gpsimd.tensor_copy(out=dstt, in_=stg[:, slot, :]), [("stg", slot)], [tok])
                    if isBv:
                        store_rows(stg[:, slot, :], 128, pbv[l, sq_, tok0 + tb * 128:tok0 + (tb + 1) * 128, :],
                                   [("stg", slot)], slot)
                    elif i == 3:
                        store_rows(stg[:, slot, :], 128, pav[l, sq_, tb * 128:(tb + 1) * 128, :],
                                   [("stg", slot)], slot)
                else:
                    if isBv:
                        dstt, tok = vbt[oth][0:64, tb, :], ("vb", oth, tb)
                    else:
                        dstt, tok = vat[oth][0:64, tb, :], ("va", oth, tb)
                    P.add("pool", lambda: nc.gpsimd.tensor_copy(out=dstt, in_=stg[0:64, slot, :]), [("stg", slot)], [tok])
                    store_rows(stg[0:64, slot, :], 64, (sbv if isBv else sav)[l, tb, :, :], [("stg", slot)], slot)
            return [Pst, None, None, None]

        def qk_job(kind, c, ts):
            isB = kind[1] == "b"
            gcol = {"qa": 0, "ka": 1, "qb": 2, "kb": 3}[kind] * NL + l
            GC = gh[:, gcol:gcol + 1]
            need_rows = (kind == "kb") or (kind == "ka" and (not isP or grp["i"] == 3))
            st = {}
            FT = ("t", ts, "B")
            if kind == "ka":
                if isP:
                    dstk, ktok = kaT[l][:, c, (i % 2) * 512:(i % 2) * 512 + T], ("ka", l, c)
                else:
                    dstk, ktok = kaT[oth][:, c, 0:T], ("ka", oth, c)
            elif kind == "kb":
                if isP:
                    dstk, ktok = kbT[l][:, c, tok0:tok0 + T], ("kb", l, c)
                else:
                    dstk, ktok = kbT[oth][:, c, 0:T], ("kb", oth, c)
            qbase = 0 if kind == "qa" else 8

            def Pst():
                wv, ws = get_w(GIDX[kind])
                half, cl = c // 2, c % 2
                bz = rot8()
                st["bz"] = bz
                for kc in range(8):
                    P.add("pe", lambda kc=kc: nc.tensor.matmul(
                        psb[bz][:, 0:T], wv[half][:, kc, cl * 128:(cl + 1) * 128], hT[:, kc, 0:T],
                        start=(kc == 0), stop=(kc == 7)), [("w", ws[half]), ("hT", kc)], [PS(bz)])
                P.add("act", lambda: nc.scalar.activation(out=tq[ts][:, 0:T], in_=psb[bz][:, 0:T], func=AF.Square),
                      [], [PS(bz), ("t", ts, "q")])
                if isB:
                    P.add("act", lambda: nc.scalar.activation(out=tz[ts][:, 0:T], in_=psb[bz][:, 0:T], func=AF.Copy,
                                                              scale=GC), ["g"], [PS(bz), ("t", ts, "z")])

            def Bst():
                bz = st["bz"]
                if isB:
                    P.add("dve", lambda: nc.vector.scalar_tensor_tensor(
                        out=tB[ts][:, 0:T], in0=psb[bz][:, 0:T], scalar=GC, in1=cosT[:, 0:T],
                        op0=ALU.mult, op1=ALU.mult), ["cs", "g"], [PS(bz), FT])
                bn = rot8()
                P.add("pe", lambda: nc.tensor.matmul(psb[bn][:, 0:T], blkb[:], tq[ts][:, 0:T], start=True, stop=False),
                      [("t", ts, "q"), "c"], [PS(bn)])
                P.add("pe", lambda: nc.tensor.matmul(psb[bn][:, 0:T], blkb[:], epsv[:, 0:T], start=False, stop=True),
                      ["epsv", "c"], [PS(bn)])
                P.add("act", lambda: nc.scalar.activation(out=tA[ts][:, 0:T], in_=psb[bn][:, 0:T], func=AF.Ln,
                                                          scale=1.0 / 64), [], [PS(bn), ("t", ts, "A")])
                P.add("act", lambda: nc.scalar.activation(out=tA[ts][:, 0:T], in_=tA[ts][:, 0:T], func=AF.Exp, scale=-0.5),
                      [], [("t", ts, "A")])
                if isB:
                    br = rot8()
                    P.add("pe", lambda: nc.tensor.matmul(psb[br][:, 0:T], rpermb[:], tz[ts][:, 0:T], start=True, stop=True),
                          [("t", ts, "z"), "c"], [PS(br)])
                    P.add("dve", lambda: nc.vector.tensor_tensor(out=tC[ts][:, 0:T], in0=psb[br][:, 0:T], in1=sinT[:, 0:T],
                                                                 op=ALU.mult), ["cs"], [PS(br), ("t", ts, "C")])
                elif kind == "qa":
                    for hf in range(2):
                        lo, hi = hf * 64, hf * 64 + 64
                        P.add("dve", lambda lo=lo, hi=hi, hf=hf: nc.vector.scalar_tensor_tensor(
                            out=qpad[lo:hi, 2 * c + hf, 0:T], in0=psb[bz][lo:hi, 0:T], scalar=gh[lo:hi, gcol:gcol + 1],
                            in1=tA[ts][lo:hi, 0:T], op0=ALU.mult, op1=ALU.mult),
                            [("t", ts, "A"), "g"], [PS(bz), ("B", 2 * c + hf)])
                elif need_rows:
                    P.add("dve", lambda: nc.vector.scalar_tensor_tensor(
                        out=tB[ts][:, 0:T], in0=psb[bz][:, 0:T], scalar=GC, in1=tA[ts][:, 0:T],
                        op0=ALU.mult, op1=ALU.mult), [("t", ts, "A"), "g"], [PS(bz), FT])
                    P.add("pool", lambda: nc.gpsimd.tensor_copy(out=dstk, in_=tB[ts][:, 0:T]), [FT], [ktok])
                else:
                    P.add("dve", lambda: nc.vector.scalar_tensor_tensor(
                        out=dstk, in0=psb[bz][:, 0:T], scalar=GC, in1=tA[ts][:, 0:T],
                        op0=ALU.mult, op1=ALU.mult), [("t", ts, "A"), "g"], [PS(bz), ktok])

            def Rst():
                P.add("dve", lambda: nc.vector.tensor_tensor(out=tB[ts][:, 0:T], in0=tB[ts][:, 0:T], in1=tC[ts][:, 0:T],
                                                             op=ALU.add), [("t", ts, "C")], [FT])
                if kind == "qb":
                    for m in range(2):
                        lo, hi = m * 64, m * 64 + 64
                        P.add("dve", lambda lo=lo, hi=hi, m=m: nc.vector.tensor_tensor(
                            out=qpad[lo:hi, 8 + 2 * c + m, 0:T], in0=tB[ts][lo:hi, 0:T], in1=tA[ts][lo:hi, 0:T],
                            op=ALU.mult), [FT, ("t", ts, "A")], [("B", 8 + 2 * c + m)])
                else:
                    P.add("dve", lambda: nc.vector.tensor_tensor(out=tB[ts][:, 0:T], in0=tB[ts][:, 0:T], in1=tA[ts][:, 0:T],
                                                                 op=ALU.mult), [("t", ts, "A")], [FT])
                    P.add("pool", lambda: nc.gpsimd.tensor_copy(out=dstk, in_=tB[ts][:, 0:T]), [FT], [ktok])

            def Xst():
                fin = tB[ts]
                bt = rot8()
                for tt in range(ntt):
                    P.add("pe", lambda tt=tt: nc.tensor.transpose(
                        psb[bt][:, tt * 128:(tt + 1) * 128], fin[:, tt * 128:(tt + 1) * 128], ident[:]),
                        [FT, "c"], [PS(bt)])
                P.add("dve", lambda: nc.vector.tensor_copy(
                    out=stg[:, 0:ntt, c * 128:(c + 1) * 128],
                    in_=psb[bt][:, 0:T].rearrange("p (t f) -> p t f", t=ntt)),
                    [], [PS(bt)] + [("stg", s_) for s_ in range(ntt)])
                if c == 3:
                    SR = [("stg", s_) for s_ in range(ntt)]
                    if isP:
                        if kind == "kb":
                            dst = pbk[l, sq_, tok0:tok0 + T, :].rearrange("(t p) f -> p t f", p=128)
                        else:
                            dst = pak[l, sq_, :, :].rearrange("(t p) f -> p t f", p=128)
                        dma("pool", dst, stg[:, 0:ntt, :], SR, [], ("stg", 0))
                    else:
                        dd = sbk if kind == "kb" else sak
                        for tt in range(ntt):
                            dst = dd[l, 2 * tt:2 * tt + 2, :, :].rearrange("s j f -> (s j) f")
                            dma("pool", dst, stg[:, tt, :], [("stg", tt)], [], ("stg", tt))
            return [Pst, Bst, Rst if isB else None, Xst if need_rows else None]

        nblk = ntt if isP else grp["nseq"]
        nq = 0
        for kind in ("qa", "va", "qb", "vb", "ka", "kb"):
            if kind[0] == "v":
                for tb in range(nblk):
                    jobs.append(v_job(kind, tb))
            else:
                for c in range(4):
                    jobs.append(qk_job(kind, c, nq % 2))
                    nq += 1
        nj = len(jobs)
        for step in range(nj + 3):
            for k in (3, 2, 0, 1):
                ci = step - k
                if 0 <= ci < nj and jobs[ci][k] is not None:
                    jobs[ci][k]()
            if step == 1:
                rtok_part()
        if isP:
            attn_prompt(l, grp)
        else:
            attn_sample(l, grp)
        for ot in range(4):
            so = wload(l, "out", ot)
            wo = wring[:, so, :].rearrange("p (ec f) -> p ec f", ec=8)
            for dcl in range(2):
                dc = 2 * ot + dcl
                b = rot()
                for n_, ec in enumerate((4, 5, 6, 7, 0, 1, 2, 3)):
                    oc = 8 + ec if ec < 4 else 12 + ec
                    P.add("pe", lambda n_=n_, ec=ec, oc=oc, dcl=dcl, b=b, wo=wo: nc.tensor.matmul(
                        psb[b][:, 0:T], wo[:, ec, dcl * 128:(dcl + 1) * 128], aT[:, oc, 0:T],
                        start=(n_ == 0), stop=(n_ == 7)), [("w", so), ("B", oc)], [PS(b)])
                norm_flush()
                P.add("dve", lambda dc=dc, b=b: nc.vector.tensor_tensor(
                    out=xT[:, dc, 0:T], in0=psb[b][:, 0:T], in1=xT[:, dc, 0:T], op=ALU.add),
                    [], [PS(b), ("xT", dc)])
                norm_feed(dc, T, (2 * NL + l) * 8, bank=7)

    ering = [0]

    def eslot():
        s = ering[0] % 6
        ering[0] += 1
        return s

    def zero_block(ap, etok):
        P.add("act", lambda: nc.scalar.mul(out=ap, in_=ap, mul=0.0), [], [etok])

    def attn_prompt(l, grp):
        i = grp["i"]
        T = 512
        nk = 4 * (i + 1)
        LA = 3
        tiles = [(h, m, j) for h in range(4) for m in range(2) for j in range(nk)]
        bo = [4, 6]
        bZ = [5, 7]
        pend = {}
        deferred = []

        def b_norm_m(h, m):
            P.add("act", lambda: nc.scalar.activation(out=tA[m][:, 0:T], in_=psb[bZ[m]][:, 0:T], func=AF.Ln),
                  [], [PS(bZ[m]), ("t", m, "A")])
            P.add("act", lambda: nc.scalar.activation(out=tA[m][:, 0:T], in_=tA[m][:, 0:T], func=AF.Exp, scale=-1.0),
                  [], [("t", m, "A")])
            P.add("dve", lambda: nc.vector.tensor_tensor(
                out=tB[m][:, 0:T], in0=psb[bo[m]][:, 0:T], in1=tA[m][:, 0:T], op=ALU.mult),
                [("t", m, "A")], [PS(bo[m]), ("t", m, "B")])

        def b_combine(h):
            P.add("dve", lambda: nc.vector.scalar_tensor_tensor(
                out=tC[0][:, 0:T], in0=tB[1][:, 0:T], scalar=nlam[:, l:l + 1], in1=tB[0][:, 0:T],
                op0=ALU.mult, op1=ALU.add), [("t", 0, "B"), ("t", 1, "B"), "g"], [("t", 0, "C")])
            P.add("act", lambda: nc.scalar.activation(out=tq[0][:, 0:T], in_=tC[0][:, 0:T], func=AF.Square),
                  [("t", 0, "C")], [("t", 0, "q")])

        def b_subnorm(h):
            bn = rot()
            P.add("pe", lambda: nc.tensor.matmul(psb[bn][:, 0:T], onesb[:], tq[0][:, 0:T], start=True, stop=True),
                  [("t", 0, "q"), "c"], [PS(bn)])
            P.add("act", lambda: nc.scalar.activation(out=tC[1][:, 0:T], in_=psb[bn][:, 0:T], func=AF.Ln,
                                                      scale=1.0 / 128, bias=epsb[:, 0:1]), ["c"], [PS(bn), ("t", 1, "C")])
            P.add("act", lambda: nc.scalar.activation(out=tC[1][:, 0:T], in_=tC[1][:, 0:T], func=AF.Exp, scale=-0.5),
                  [], [("t", 1, "C")])
            P.add("dve", lambda: nc.vector.scalar_tensor_tensor(
                out=aT[:, 16 + h, 0:T], in0=tC[0][:, 0:T], scalar=gsub[:, l:l + 1], in1=tC[1][:, 0:T],
                op0=ALU.mult, op1=ALU.mult), [("t", 0, "C"), ("t", 1, "C"), "g"], [("B", 16 + h)])

        def b1(idx):
            h, m, j = tiles[idx]
            qv = qpad[:, 8 + 2 * h + m, :]
            QT = ("B", 8 + 2 * h + m)
            r = j - 4 * i
            q0 = 128 * r if r > 0 else 0
            bs = rot()
            P.add("pe", lambda: nc.tensor.matmul(
                psb[bs][:, q0:512], kbT[l][:, h, j * 128:(j + 1) * 128], qv[:, q0:512],
                start=True, stop=(r < 0)), [("kb", l, h), QT], [PS(bs)])
            if r >= 0:
                P.add("pe", lambda: nc.tensor.matmul(psb[bs][:, q0:q0 + 64], identb[:], maskB[:],
                                                     start=False, stop=True), ["c"], [PS(bs)])
            es = eslot()
            P.add("act", lambda: nc.scalar.activation(
                out=eT[:, es, q0:512], in_=psb[bs][:, q0:512], func=AF.Exp, scale=0.125),
                [], [PS(bs), ("e", es)])
            pend[idx] = (es, q0)

        def b2(idx):
            h, m, j = tiles[idx]
            es, q0 = pend.pop(idx)
            P.add("pe", lambda: nc.tensor.matmul(
                psb[bo[m]][:, q0:512], vbt[l][:, j, h * 128:(h + 1) * 128], eT[:, es, q0:512],
                start=(j == 0), stop=(j == nk - 1)), [("vb", l, j), ("e", es)], [PS(bo[m])])
            P.add("pe", lambda: nc.tensor.matmul(
                psb[bZ[m]][:, q0:512], onesb[:], eT[:, es, q0:512],
                start=(j == 0), stop=(j == nk - 1)), [("e", es), "c"], [PS(bZ[m])])
            if j == nk - 1:
                deferred.append([idx + 2, b_norm_m, (h, m)])
                if m == 1:
                    deferred.append([idx + 3, b_combine, (h,)])
                    deferred.append([idx + 6, b_subnorm, (h,)])

        def flush(upto):
            while deferred and deferred[0][0] <= upto:
                _, fn, args = deferred.pop(0)
                fn(*args)

        nt = len(tiles)
        for idx in range(nt + LA):
            if idx < nt:
                b1(idx)
            if idx - LA >= 0:
                b2(idx - LA)
                flush(idx - LA)
        units = [(h, u) for h in range(8) for u in range(4)]
        pendA = {}

        def a1(idx):
            h, u = units[idx]
            hp = h // 2
            bi0 = (l * 8 + h) * 2
            ug = 4 * i + u
            t4 = [t for t in range(4) if ug - 4 + t >= 0]
            BC = bconst[:, l * 8 + h:l * 8 + h + 1]
            qv = qpad[:, h, u * 128:(u + 1) * 128]
            j = ug
            kcol = ((j // 4) % 2) * 512 + (j % 4) * 128
            b2_ = rot()
            P.add("pe", lambda: nc.tensor.matmul(psb[b2_][:, 0:128], antib[:], btile[:, bi0 + 1, :],
                                                 start=True, stop=False), ["bt", "c"], [PS(b2_)])
            P.add("pe", lambda: nc.tensor.matmul(
                psb[b2_][:, 0:128], kaT[l][:, hp, kcol:kcol + 128], qv, start=False, stop=True),
                [("ka", l, hp), ("B", h)], [PS(b2_)])
            es2 = eslot()
            P.add("act", lambda: nc.scalar.activation(
                out=eT[:, es2, 0:128], in_=psb[b2_][:, 0:128], func=AF.Exp, scale=0.125, bias=BC),
                ["bc"], [PS(b2_), ("e", es2)])
            es = None
            if t4:
                b1_ = rot()
                for t in t4:
                    j = ug - 4 + t
                    kcol = ((j // 4) % 2) * 512 + (j % 4) * 128
                    first = True
                    if t == 3:
                        P.add("pe", lambda t=t: nc.tensor.matmul(
                            psb[b1_][:, t * 128:(t + 1) * 128], antib[:], btile[:, bi0, :], start=True, stop=False),
                            ["bt", "c"], [PS(b1_)])
                        first = False
                    if t == 0:
                        P.add("pe", lambda t=t: nc.tensor.matmul(
                            psb[b1_][:, 0:128], identb[:], maskA0[:], start=True, stop=False), ["c"], [PS(b1_)])
                        first = False
                    P.add("pe", lambda t=t, kcol=kcol, first=first: nc.tensor.matmul(
                        psb[b1_][:, t * 128:(t + 1) * 128], kaT[l][:, hp, kcol:kcol + 128], qv,
                        start=first, stop=True), [("ka", l, hp), ("B", h)], [PS(b1_)])
                es = eslot()
                c0 = t4[0] * 128
                P.add("act", lambda: nc.scalar.activation(
                    out=eT[:, es, c0:512], in_=psb[b1_][:, c0:512], func=AF.Exp, scale=0.125, bias=BC),
                    ["bc"], [PS(b1_), ("e", es)])
            pendA[idx] = [(4, es2, 0)] + [(t, es, t * 128) for t in t4]

        def a2(idx):
            h, u = units[idx]
            hp, hf = h // 2, h % 2
            ug = 4 * i + u
            bo_, bZ_ = (4, 5) if h % 2 == 0 else (6, 7)
            seq_mm = pendA.pop(idx)
            for n_, (t, e_, ecol) in enumerate(seq_mm):
                j = ug - 4 + t
                P.add("pe", lambda n_=n_, j=j, e_=e_, ecol=ecol: nc.tensor.matmul(
                    psb[bo_][:, u * 128:(u + 1) * 128], vat[l][:, j % 8, hp * 128:(hp + 1) * 128],
                    eT[:, e_, ecol:ecol + 128], start=(n_ == 0), stop=(n_ == len(seq_mm) - 1)),
                    [("va", l, j % 8), ("e", e_)], [PS(bo_)])
                P.add("pe", lambda n_=n_, e_=e_, ecol=ecol: nc.tensor.matmul(
                    psb[bZ_][:, u * 128:(u + 1) * 128], onesb[:], eT[:, e_, ecol:ecol + 128],
                    start=(n_ == 0), stop=(n_ == len(seq_mm) - 1)), [("e", e_), "c"], [PS(bZ_)])
            if u == 3:
                deferred.append([idx + 1, a_finish, (h,)])

        def a_finish(h):
            hp, hf = h // 2, h % 2
            bo_, bZ_ = (4, 5) if h % 2 == 0 else (6, 7)
            ts = h % 2
            lo, hi = hf * 64, hf * 64 + 64
            P.add("act", lambda: nc.scalar.activation(
                out=tA[ts][lo:hi, 0:T], in_=psb[bZ_][lo:hi, 0:T], func=AF.Ln), [], [PS(bZ_), ("t", ts, "A")])
            P.add("act", lambda: nc.scalar.activation(
                out=tA[ts][lo:hi, 0:T], in_=tA[ts][lo:hi, 0:T], func=AF.Exp, scale=-1.0), [], [("t", ts, "A")])
            P.add("dve", lambda: nc.vector.tensor_tensor(
                out=aT[lo:hi, 8 + hp, 0:T], in0=psb[bo_][lo:hi, 0:T], in1=tA[ts][lo:hi, 0:T], op=ALU.mult),
                [("t", ts, "A")], [PS(bo_), ("B", 8 + hp)])

        nu = len(units)
        for idx in range(nu + 1):
            if idx < nu:
                a1(idx)
            if idx >= 1:
                a2(idx - 1)
                if idx >= 2:
                    flush(idx - 1)
            if idx == 1:
                flush(10 ** 9)
        flush(10 ** 9)

    def attn_sample(l, grp):
        T = grp["T"]
        nseq = grp["nseq"]
        oth = 1 - l
        cstg = [vbt[oth][:, 4 + 4 * s_:8 + 4 * s_, :].rearrange("p a b -> p (a b)").bitcast(F32).rearrange(
            "p (k f) -> p k f", k=2) for s_ in range(3)]
        CT = [[("vb", oth, 4 + 4 * s_ + k) for k in range(4)] for s_ in range(3)]
        ev = [0]
        vc = [0]

        def evac(out, in_, reads, writes):
            ev[0] += 1
            if ev[0] % 2 == 0:
                P.add("act", lambda: nc.scalar.copy(out=out, in_=in_), reads, writes)
            else:
                P.add("dve", lambda: nc.vector.tensor_copy(out=out, in_=in_), reads, writes)

        def KBS(h, half):
            return ("kbs", l, h, half)

        pieces = []
        for s_ in range(nseq):
            for half in range(2):
                for pc in range(4):
                    pieces.append((s_, cbk, half * 4 + pc, True, True))
                for pc in range(4):
                    pieces.append((s_, cbv, half * 4 + pc, False, True))
            for pc in range(2):
                pieces.append((s_, cak, pc, True, False))
            for pc in range(2):
                pieces.append((s_, cav, pc, False, False))
        pstate = {"dma": 0, "conv": 0}

        def piece_dma():
            n = pstate["dma"]
            if n >= len(pieces):
                return
            pstate["dma"] += 1
            s_, csrc, pc, isK, isBc = pieces[n]
            sl = n % 3
            dma("sp", cstg[sl], csrc[l, s_, pc * 256:(pc + 1) * 256, :].rearrange("(k p) f -> p k f", p=128),
                [], CT[sl], ("cst", sl))

        def piece_conv():
            n = pstate["conv"]
            if n >= len(pieces):
                return
            while pstate["dma"] < min(n + 3, len(pieces)):
                piece_dma()
            pstate["conv"] += 1
            s_, csrc, pc, isK, isBc = pieces[n]
            sl = n % 3
            for k in range(2):
                kt = pc * 2 + k
                if isK:
                    b = rot()
                    for c in range(4):
                        P.add("pe", lambda c=c: nc.tensor.transpose(
                            psb[b][:, c * 128:(c + 1) * 128], cstg[sl][:, k, c * 128:(c + 1) * 128], ident[:]),
                            CT[sl] + ["c"], [PS(b)])
                    if isBc:
                        dst = kbT[l][:, :, kt * 128:(kt + 1) * 128]
                        toks = [KBS(c, kt // 8) for c in range(4)]
                        if s_ == 0:
                            toks = toks + [("kb", l, c) for c in range(4)]
                    else:
                        dst = kaT[l][:, :, kt * 128:(kt + 1) * 128]
                        toks = [("ka", l, c) for c in range(4)]
                    evac(dst, psb[b][:, :].rearrange("p (c k) -> p c k", c=4), [], [PS(b)] + toks)
                else:
                    if isBc:
                        dst, tok = vbt[l][:, kt, :], ("vb", l, kt)
                    else:
                        dst, tok = vat[l][:, kt, :], ("va", l, kt)
                    vc[0] += 1
                    if vc[0] % 3 == 0:
                        P.add("pool", lambda: nc.gpsimd.tensor_copy(out=dst, in_=cstg[sl][:, k, :]), CT[sl], [tok])
                    elif vc[0] % 3 == 1:
                        P.add("dve", lambda: nc.vector.tensor_copy(out=dst, in_=cstg[sl][:, k, :]), CT[sl], [tok])
                    else:
                        P.add("act", lambda: nc.scalar.copy(out=dst, in_=cstg[sl][:, k, :]), CT[sl], [tok])

        for _ in range(8):
            piece_conv()

        sunits = [(h, m) for h in range(4) for m in range(2)]
        BO = [4, 6]
        BZ = [5, 7]

        for s in range(nseq):
            qc0 = s * 64
            for half in range(2):
                bo, bZ = BO[half], BZ[half]
                spend = {}

                def sb1(n):
                    h, m = sunits[n]
                    qv = qpad[:, 8 + 2 * h + m, qc0:qc0 + 64]
                    QT = ("B", 8 + 2 * h + m)
                    bs = rot()
                    for k8 in range(8):
                        j = half * 8 + k8
                        P.add("pe", lambda j=j, k8=k8: nc.tensor.matmul(
                            psb[bs][:, k8 * 64:(k8 + 1) * 64], kbT[l][:, h, j * 128:(j + 1) * 128], qv,
                            start=True, stop=True), [KBS(h, half), QT], [PS(bs)])
                    es = eslot()
                    P.add("act", lambda: nc.scalar.activation(
                        out=eT[:, es, :], in_=psb[bs][:, :], func=AF.Exp, scale=0.125), [], [PS(bs), ("e", es)])
                    es3 = None
                    if half == 1:
                        bs3 = rot()
                        P.add("pe", lambda: nc.tensor.matmul(
                            psb[bs3][0:64, 0:64], kbT[oth][:, h, qc0:qc0 + 64], qv, start=True, stop=True),
                            [("kb", oth, h), QT], [PS(bs3)])
                        es3 = eslot()
                        P.add("act", lambda: nc.scalar.activation(
                            out=eT[0:64, es3, 0:64], in_=psb[bs3][0:64, 0:64], func=AF.Exp, scale=0.125),
                            [], [PS(bs3), ("e", es3)])
                    spend[n] = (es, es3)

                def sb2(n):
                    h, m = sunits[n]
                    col = (2 * h + m) * 64
                    es, es3 = spend.pop(n)
                    nt = 9 if half == 1 else 8
                    for jj in range(nt):
                        if jj < 8:
                            j = half * 8 + jj
                            va_, e_, tokv = vbt[l][:, j, h * 128:(h + 1) * 128], eT[:, es, jj * 64:(jj + 1) * 64], ("vb", l, j)
                            on_ = onesb[:]
                            et = ("e", es)
                        else:
                            va_, e_, tokv = vbt[oth][0:64, s, h * 128:(h + 1) * 128], eT[0:64, es3, 0:64], ("vb", oth, s)
                            on_ = onesb[0:64, :]
                            et = ("e", es3)
                        P.add("pe", lambda jj=jj, va_=va_, e_=e_: nc.tensor.matmul(
                            psb[bo][:, col:col + 64], va_, e_, start=(jj == 0), stop=(jj == nt - 1)),
                            [tokv, et], [PS(bo)])
                        P.add("pe", lambda jj=jj, on_=on_, e_=e_: nc.tensor.matmul(
                            psb[bZ][:, col:col + 64], on_, e_, start=(jj == 0), stop=(jj == nt - 1)),
                            [et, "c"], [PS(bZ)])

                for n in range(len(sunits) + 1):
                    if n < len(sunits):
                        sb1(n)
                    if n >= 1:
                        sb2(n - 1)
                        piece_conv()
            P.add("dve", lambda: nc.vector.tensor_copy(out=tA[0][:, :], in_=psb[BZ[0]][:, :]), [], [PS(BZ[0]), ("t", 0, "A")])
            P.add("dve", lambda: nc.vector.tensor_tensor(out=tA[0][:, :], in0=psb[BZ[1]][:, :], in1=tA[0][:, :], op=ALU.add),
                  [], [PS(BZ[1]), ("t", 0, "A")])
            P.add("act", lambda: nc.scalar.activation(out=tA[0][:, :], in_=tA[0][:, :], func=AF.Ln), [], [("t", 0, "A")])
            P.add("act", lambda: nc.scalar.activation(out=tA[0][:, :], in_=tA[0][:, :], func=AF.Exp, scale=-1.0), [], [("t", 0, "A")])
            P.add("dve", lambda: nc.vector.tensor_copy(out=tB[0][:, :], in_=psb[BO[0]][:, :]), [], [PS(BO[0]), ("t", 0, "B")])
            P.add("dve", lambda: nc.vector.tensor_tensor(out=tB[0][:, :], in0=psb[BO[1]][:, :], in1=tB[0][:, :], op=ALU.add),
                  [], [PS(BO[1]), ("t", 0, "B")])
            P.add("dve", lambda: nc.vector.tensor_tensor(out=tB[0][:, :], in0=tB[0][:, :], in1=tA[0][:, :], op=ALU.mult),
                  [("t", 0, "A")], [("t", 0, "B")])
            onv = tB[0][:, :].rearrange("p (h m q) -> p h m q", h=4, m=2)
            obv = tC[0][:, 0:256].rearrange("p (h q) -> p h q", h=4)
            P.add("dve", lambda: nc.vector.scalar_tensor_tensor(
                out=obv, in0=onv[:, :, 1, :], scalar=nlam[:, l:l + 1], in1=onv[:, :, 0, :],
                op0=ALU.mult, op1=ALU.add), [("t", 0, "B"), "g"], [("t", 0, "C")])
            P.add("act", lambda: nc.scalar.activation(out=tq[0][:, 0:256], in_=tC[0][:, 0:256], func=AF.Square),
                  [("t", 0, "C")], [("t", 0, "q")])
            bn = rot()
            P.add("pe", lambda bn=bn: nc.tensor.matmul(psb[bn][:, 0:256], onesb[:], tq[0][:, 0:256], start=True, stop=True),
                  [("t", 0, "q"), "c"], [PS(bn)])
            P.add("act", lambda bn=bn: nc.scalar.activation(out=tC[1][:, 0:256], in_=psb[bn][:, 0:256], func=AF.Ln,
                                                            scale=1.0 / 128, bias=epsb[:, 0:1]), ["c"], [PS(bn), ("t", 1, "C")])
            P.add("act", lambda: nc.scalar.activation(out=tC[1][:, 0:256], in_=tC[1][:, 0:256], func=AF.Exp, scale=-0.5),
                  [], [("t", 1, "C")])
            P.add("dve", lambda: nc.vector.scalar_tensor_tensor(
                out=aT[:, 16:20, qc0:qc0 + 64], in0=obv, scalar=gsub[:, l:l + 1],
                in1=tC[1][:, 0:256].rearrange("p (h q) -> p h q", h=4), op0=ALU.mult, op1=ALU.mult),
                [("t", 0, "C"), ("t", 1, "C"), "g"], [("B", c) for c in range(16, 20)])
            bo, bZ = 6, 7
            for h in range(8):
                hp, hf = h // 2, h % 2
                bi0 = (l * 8 + h) * 2
                qv = qpad[:, h, qc0:qc0 + 64]
                col = h * 64
                b1 = rot()
                for t in range(4):
                    first = True
                    if t == 3:
                        P.add("pe", lambda b1=b1, t=t: nc.tensor.matmul(
                            psb[b1][:, t * 64:(t + 1) * 64], antib[:], btile[:, bi0, 0:64], start=True, stop=False),
                            ["bt", "c"], [PS(b1)])
                        first = False
                    P.add("pe", lambda b1=b1, t=t, first=first, qv=qv: nc.tensor.matmul(
                        psb[b1][:, t * 64:(t + 1) * 64], kaT[l][:, hp, t * 128:(t + 1) * 128], qv,
                        start=first, stop=True), [("ka", l, hp), ("B", h)], [PS(b1)])
                es = eslot()
                P.add("act", lambda b1=b1, es=es: nc.scalar.activation(
                    out=eT[:, es, 0:256], in_=psb[b1][:, 0:256], func=AF.Exp, scale=0.125,
                    bias=bconst[:, l * 8 + h:l * 8 + h + 1]), ["bc"], [PS(b1), ("e", es)])
                b2 = rot()
                P.add("pe", lambda b2=b2: nc.tensor.matmul(
                    psb[b2][0:64, 0:64], antib[64:128, 0:64], btile[64:128, bi0 + 1, 0:64], start=True, stop=False),
                    ["bt", "c"], [PS(b2)])
                P.add("pe", lambda b2=b2, qv=qv: nc.tensor.matmul(
                    psb[b2][0:64, 0:64], kaT[oth][:, hp, qc0:qc0 + 64], qv, start=False, stop=True),
                    [("ka", oth, hp), ("B", h)], [PS(b2)])
                es2 = eslot()
                P.add("act", lambda b2=b2, es2=es2: nc.scalar.activation(
                    out=eT[0:64, es2, 0:64], in_=psb[b2][0:64, 0:64], func=AF.Exp, scale=0.125,
                    bias=bconst[0:64, l * 8 + h:l * 8 + h + 1]), ["bc"], [PS(b2), ("e", es2)])
                for j in range(5):
                    if j < 4:
                        va_, e_, tokv, on_, et = vat[l][:, j, hp * 128:(hp + 1) * 128], eT[:, es, j * 64:(j + 1) * 64], ("va", l, j), onesb[:], ("e", es)
                    else:
                        va_, e_, tokv, on_, et = vat[oth][0:64, s, hp * 128:(hp + 1) * 128], eT[0:64, es2, 0:64], ("va", oth, s), onesb[0:64, :], ("e", es2)
                    P.add("pe", lambda j=j, va_=va_, e_=e_, col=col: nc.tensor.matmul(
                        psb[bo][:, col:col + 64], va_, e_, start=(j == 0), stop=(j == 4)), [tokv, et], [PS(bo)])
                    P.add("pe", lambda j=j, on_=on_, e_=e_, col=col: nc.tensor.matmul(
                        psb[bZ][:, col:col + 64], on_, e_, start=(j == 0), stop=(j == 4)), [et, "c"], [PS(bZ)])
                if h % 2 == 1:
                    piece_conv()
            for hf in range(2):
                lo, hi = hf * 64, hf * 64 + 64
                zv = psb[bZ][lo:hi, :].rearrange("p (hp f q) -> p hp f q", hp=4, f=2)[:, :, hf, :]
                ov = psb[bo][lo:hi, :].rearrange("p (hp f q) -> p hp f q", hp=4, f=2)[:, :, hf, :]
                tv = tA[1][lo:hi, 0:256].rearrange("p (hp q) -> p hp q", hp=4)
                P.add("act", lambda tv=tv, zv=zv: nc.scalar.activation(out=tv, in_=zv, func=AF.Ln), [], [PS(bZ), ("t", 1, "A")])
                P.add("act", lambda tv=tv: nc.scalar.activation(out=tv, in_=tv, func=AF.Exp, scale=-1.0), [], [("t", 1, "A")])
                P.add("dve", lambda tv=tv, ov=ov, lo=lo, hi=hi: nc.vector.tensor_tensor(
                    out=aT[lo:hi, 8:12, qc0:qc0 + 64], in0=ov, in1=tv, op=ALU.mult),
                    [("t", 1, "A")], [PS(bo)] + [("B", c) for c in range(8, 12)])

    groups = []
    for sq_ in range(NPS):
        for i in range(4):
            groups.append(dict(kind="p", seq=sq_, i=i, T=512))
    if NSS:
        groups.append(dict(kind="s", T=NSS * 64, nseq=NSS))
    def grp_rows(grp, dram_p, dram_s):
        if grp["kind"] == "p":
            return dram_p[grp["seq"], grp["i"] * 512:(grp["i"] + 1) * 512, :]
        return dram_s[0:grp["T"], :]

    prefetch_x(grp_rows(groups[0], xp, xs), groups[0]["T"])
    for gi_, grp in enumerate(groups):
        T = grp["T"]
        load_x(T)
        nxt = groups[gi_ + 1] if gi_ + 1 < len(groups) else None
        for l in range(NL):
            ffn(l, 1, T)
            mix(l, grp)
            pre = None
            if l == NL - 1 and nxt is not None:
                pre = (lambda nxt=nxt: prefetch_x(grp_rows(nxt, xp, xs), nxt["T"]))
            ffn(l, 2, T, pre=pre)
        store_y(grp_rows(grp, yp, ys), T)
    nw = P.emit_all()
    return nc, dict(n_ops=len(P.ops), n_wait=nw, dbg=dbg_outs)


def _constants():
    ident = np.eye(128, dtype=np.float32)
    anti = np.ascontiguousarray(ident[::-1])
    blk = np.zeros((128, 128), np.float32)
    blk[:64, :64] = 1.0
    blk[64:, 64:] = 1.0
    rperm = np.zeros((128, 128), np.float32)
    for m in range(128):
        if (m % 64) < 32:
            rperm[m + 32, m] = -1.0
        else:
            rperm[m - 32, m] = 1.0
    inv = (10000.0 ** (-np.arange(32, dtype=np.float32) * 2.0 / 64)).astype(np.float32)
    fidx = np.arange(128) % 32

    def tab(pos):
        ang = pos.astype(np.float32)[None, :] * inv[fidx][:, None]
        return np.cos(ang).astype(np.float32), np.sin(ang).astype(np.float32)
    cosp, sinp = tab(np.arange(SEQ))
    coss, sins = tab(np.tile(PAST + np.arange(64), 4))
    return dict(c_ident=ident, c_anti=anti, c_blk=blk, c_rperm=rperm, c_cosp=cosp, c_sinp=sinp,
                c_coss=coss, c_sins=sins)


_CACHE = {}


def kernel(x_prompt, x_sample, cache_a_k, cache_a_v, cache_b_k, cache_b_v,
           g_ffn1, w1_gate, w1_up, w1_down, g_mix, w_in, g_qa, g_ka, g_qb, g_kb,
           rel_bias, lam_q1, lam_k1, lam_q2, lam_k2, g_sub, w_out,
           g_ffn2, w2_gate, w2_up, w2_down):
    f = lambda a: np.ascontiguousarray(np.asarray(a, dtype=np.float32))
    NL = 2
    if "nc" not in _CACHE:
        _CACHE["nc"] = build()[0]
    nc = _CACHE["nc"]
    consts = _constants()
    shared = dict(w1_gate=f(w1_gate), w1_up=f(w1_up), w1_down=f(w1_down), w_in=f(w_in), w_out=f(w_out),
                  w2_gate=f(w2_gate), w2_up=f(w2_up), w2_down=f(w2_down),
                  g_ffn1=f(g_ffn1), g_mix=f(g_mix), g_ffn2=f(g_ffn2), g_qa=f(g_qa), g_ka=f(g_ka),
                  g_qb=f(g_qb), g_kb=f(g_kb), g_sub=f(g_sub), rel_bias=f(rel_bias),
                  lam_q1=f(lam_q1), lam_k1=f(lam_k1), lam_q2=f(lam_q2), lam_k2=f(lam_k2))
    shared.update(consts)
    xpn, xsn = f(x_prompt), f(x_sample)
    cak, cav = f(cache_a_k).reshape(NL, 32, 512, 512), f(cache_a_v).reshape(NL, 32, 512, 512)
    cbk, cbv = f(cache_b_k).reshape(NL, 32, PAST, 512), f(cache_b_v).reshape(NL, 32, PAST, 512)
    in_maps = []
    for c in range(NCORES):
        m = dict(shared)
        m["xp"] = xpn[2 * c:2 * c + 2]
        m["xs"] = xsn[4 * c:4 * c + 4].reshape(256, D_MODEL)
        m["cak"] = np.ascontiguousarray(cak[:, 4 * c:4 * c + 4])
        m["cav"] = np.ascontiguousarray(cav[:, 4 * c:4 * c + 4])
        m["cbk"] = np.ascontiguousarray(cbk[:, 4 * c:4 * c + 4])
        m["cbv"] = np.ascontiguousarray(cbv[:, 4 * c:4 * c + 4])
        in_maps.append(m)
    res = run_bass_kernel_spmd(nc, in_maps, core_ids=list(range(NCORES)))
    R = res.results
    yp = np.concatenate([r["yp"] for r in R], axis=0)
    ys = np.concatenate([r["ys"].reshape(4, 64, D_MODEL) for r in R], axis=0)
    cat1 = lambda k: np.concatenate([r[k] for r in R], axis=1)
    pak = cat1("pak").reshape(NL, 16, 512, 8, 64)
    pav = cat1("pav").reshape(NL, 16, 512, 8, 64)
    pbk = cat1("pbk").reshape(NL, 16, SEQ, 4, 128)
    pbv = cat1("pbv").reshape(NL, 16, SEQ, 4, 128)
    sak = cat1("sak").reshape(NL, 32, 64, 8, 64)
    sav = cat1("sav").reshape(NL, 32, 64, 8, 64)
    sbk = cat1("sbk").reshape(NL, 32, 64, 4, 128)
    sbv = cat1("sbv").reshape(NL, 32, 64, 4, 128)
    return (yp.astype(np.float32), ys.astype(np.float32), pak, pav, pbk, pbv, sak, sav, sbk, sbv)
```

```python
import math
import types
import numpy as np
import concourse.bass as bass
import concourse.mybir as mybir
from concourse.bass_utils import run_bass_kernel_spmd

F32 = mybir.dt.float32
BF16 = mybir.dt.bfloat16
AF = mybir.ActivationFunctionType
ALU = mybir.AluOpType

D_MODEL = 1024
D_FF = 2816
NFC = 22
SEQ = 2048
PAST = 2048
NCORES = 8
EPS = 1e-6
WT = 2048
NSLOT = 4


def _freeze(fn):
    if fn.__closure__ is None:
        return fn
    cells = []
    for c in fn.__closure__:
        try:
            cells.append(types.CellType(c.cell_contents))
        except ValueError:
            cells.append(c)
    return types.FunctionType(fn.__code__, fn.__globals__, fn.__name__, fn.__defaults__, tuple(cells))


class Prog:
    def __init__(self, nc, same_engine_sync=True):
        self.nc = nc
        self.eng = {"pe": nc.tensor, "act": nc.scalar, "dve": nc.vector,
                    "pool": nc.gpsimd, "sp": nc.sync}
        self.ops = []
        self.last_w = {}
        self.readers = {}
        self.same_engine_sync = same_engine_sync
        self.dma_fill = {}

    def add(self, eng, emit, reads=(), writes=(), dma_key=None):
        idx = len(self.ops)
        deps = set()
        for r in reads:
            lw = self.last_w.get(r)
            if lw is not None:
                deps.add(lw)
            self.readers.setdefault(r, []).append(idx)
        for w in writes:
            lw = self.last_w.get(w)
            if lw is not None:
                deps.add(lw)
            rs = self.readers.get(w)
            if rs:
                deps.update(rs)
            self.last_w[w] = idx
            self.readers[w] = []
        deps.discard(idx)
        fill = None
        if dma_key is not None:
            fill = self.dma_fill.get(dma_key, 0) + 1
            self.dma_fill[dma_key] = fill
        self.ops.append([eng, _freeze(emit), deps, dma_key, fill, False, 0])
        return idx

    def emit_all(self):
        nc = self.nc
        ops = self.ops
        pruned = []
        for j, (eng, emit, deps, key, fill, _, _) in enumerate(ops):
            best = {}
            for i in deps:
                e_i, _, _, k_i, _, _, _ = ops[i]
                sid = ("dma", k_i) if k_i is not None else e_i
                if k_i is None and e_i == eng and key is None:
                    if eng == "pe" or not self.same_engine_sync:
                        continue
                if sid not in best or best[sid] < i:
                    best[sid] = i
            pruned.append(sorted(best.values()))
            for i in best.values():
                ops[i][5] = True
        cnt = {}
        for op in ops:
            if op[3] is None and op[5]:
                cnt[op[0]] = cnt.get(op[0], 0) + 1
                op[6] = cnt[op[0]]
        esem = {e: nc.alloc_semaphore("s_" + e) for e in ("pe", "act", "dve", "pool")}
        dsem = {}
        for n, k in enumerate(self.dma_fill):
            dsem[k] = nc.alloc_semaphore("d%d" % n)
        know = {e: {} for e in self.eng}
        snap = {}
        n_wait = 0
        for j, (eng, emit, deps, key, fill, mark, c) in enumerate(ops):
            E = self.eng[eng]
            K = know[eng]
            for i in pruned[j]:
                e_i, _, _, k_i, f_i, _, c_i = ops[i]
                if k_i is not None:
                    sem, val, sk = dsem[k_i], 16 * f_i, ("d", k_i)
                else:
                    sem, val, sk = esem[e_i], c_i, e_i
                if K.get(sk, 0) >= val:
                    continue
                E.wait_ge(sem, val)
                n_wait += 1
                K[sk] = val
                si = snap.get(i)
                if si:
                    for k2, v2 in si.items():
                        if K.get(k2, 0) < v2:
                            K[k2] = v2
            if key is not None or mark:
                snap[j] = dict(K)
            inst = emit()
            if key is not None:
                inst.then_inc(dsem[key], 16)
            elif mark:
                inst.then_inc(esem[eng], 1)
        for k, f in self.dma_fill.items():
            if know["sp"].get(("d", k), 0) < 16 * f:
                nc.sync.wait_ge(dsem[k], 16 * f)
        return n_wait


def _weight_tiles(NL):
    cat = {}
    for l in range(NL):
        for f in (1, 2):
            for ft in range(11):
                cat[(l, "g%d" % f, ft)] = len(cat)
                cat[(l, "u%d" % f, ft)] = len(cat)
            for half in range(2):
                for t6 in range(6):
                    cat[(l, "d%d" % f, half * 6 + t6)] = len(cat)
            if f == 1:
                for ct in range(12):
                    cat[(l, "in", ct)] = len(cat)
                for ot in range(4):
                    cat[(l, "out", ot)] = len(cat)
    return cat


def build(NPS=2, NSS=4, NL=2, same_engine_sync=True, debug=None):
    nc = bass.Bass("TRN2", target_bir_lowering=False)
    P = Prog(nc, same_engine_sync)
    dbg_outs = {}

    def din(name, shape, dt=F32):
        return nc.dram_tensor(name, list(shape), dt, kind="ExternalInput").ap()

    def dout(name, shape, dt=F32):
        return nc.dram_tensor(name, list(shape), dt, kind="ExternalOutput").ap()

    xp = din("xp", [max(NPS, 1), SEQ, D_MODEL])
    xs = din("xs", [max(NSS, 1) * 64, D_MODEL])
    cak = din("cak", [NL, max(NSS, 1), 512, 512])
    cav = din("cav", [NL, max(NSS, 1), 512, 512])
    cbk = din("cbk", [NL, max(NSS, 1), PAST, 512])
    cbv = din("cbv", [NL, max(NSS, 1), PAST, 512])
    W = {}
    for f in (1, 2):
        W["g%d" % f] = din("w%d_gate" % f, [NL, D_MODEL, D_FF])
        W["u%d" % f] = din("w%d_up" % f, [NL, D_MODEL, D_FF])
        W["d%d" % f] = din("w%d_down" % f, [NL, D_FF, D_MODEL])
    W["in"] = din("w_in", [NL, D_MODEL, 3072])
    W["out"] = din("w_out", [NL, D_MODEL, D_MODEL])
    g_ffn1 = din("g_ffn1", [NL, D_MODEL])
    g_mix = din("g_mix", [NL, D_MODEL])
    g_ffn2 = din("g_ffn2", [NL, D_MODEL])
    g_qa = din("g_qa", [NL, 64])
    g_ka = din("g_ka", [NL, 64])
    g_qb = din("g_qb", [NL, 64])
    g_kb = din("g_kb", [NL, 64])
    g_sub = din("g_sub", [NL, 128])
    rel_bias = din("rel_bias", [NL, 8, 257])
    lam_in = {k: din(k, [NL, 64]) for k in ("lam_q1", "lam_k1", "lam_q2", "lam_k2")}
    c_ident = din("c_ident", [128, 128])
    c_anti = din("c_anti", [128, 128])
    c_blk = din("c_blk", [128, 128])
    c_rperm = din("c_rperm", [128, 128])
    c_cosp = din("c_cosp", [128, SEQ])
    c_sinp = din("c_sinp", [128, SEQ])
    c_coss = din("c_coss", [128, 256])
    c_sins = din("c_sins", [128, 256])

    yp = dout("yp", [max(NPS, 1), SEQ, D_MODEL])
    ys = dout("ys", [max(NSS, 1) * 64, D_MODEL])
    pak = dout("pak", [NL, max(NPS, 1), 512, 512])
    pav = dout("pav", [NL, max(NPS, 1), 512, 512])
    pbk = dout("pbk", [NL, max(NPS, 1), SEQ, 512])
    pbv = dout("pbv", [NL, max(NPS, 1), SEQ, 512])
    sak = dout("sak", [NL, max(NSS, 1), 64, 512])
    sav = dout("sav", [NL, max(NSS, 1), 64, 512])
    sbk = dout("sbk", [NL, max(NSS, 1), 64, 512])
    sbv = dout("sbv", [NL, max(NSS, 1), 64, 512])

    cat = _weight_tiles(NL)
    wsc = nc.dram_tensor("wsc", [len(cat), 128, WT], BF16, kind="Internal").ap()
    tpad = nc.dram_tensor("tpad", [NL * 8, 384], F32, kind="Internal").ap()

    A = nc.alloc_sbuf_tensor
    ident = A("ident", [128, 128], F32)
    identb = A("identb", [128, 128], BF16)
    antib = A("antib", [128, 128], BF16)
    onesb = A("onesb", [128, 128], BF16)
    blkb = A("blkb", [128, 128], BF16)
    rpermb = A("rpermb", [128, 128], BF16)
    maskB = A("maskB", [128, 64], BF16)
    maskA0 = A("maskA0", [128, 128], BF16)
    epsb = A("epsb", [128, 1], F32)
    lneps = A("lneps", [128, 1], F32)
    gx = A("gx", [128, 3 * NL * 8], F32)
    gh = A("gh", [128, 4 * NL], F32)
    gsub = A("gsub", [128, NL], F32)
    nlam = A("nlam", [128, NL], F32)
    bconst = A("bconst", [128, NL * 8], F32)
    btile = A("btile", [128, NL * 8 * 2, 128], BF16)
    xT = A("xT", [128, 8, 512], F32)
    hT = A("hT", [128, 8, 512], BF16)
    BIG = A("BIG", [128, NFC * 512], BF16)
    wring = A("wring", [128, NSLOT, WT], BF16)
    kbT = [A("kbT%d" % l, [128, 4, 2048], BF16) for l in range(2)]
    vbt = [A("vb%d" % l, [128, 16, 512], BF16) for l in range(2)]
    kaT = [A("kaT%d" % l, [128, 4, 1024], BF16) for l in range(2)]
    vat = [A("va%d" % l, [128, 8, 512], BF16) for l in range(2)]
    eT = A("eT", [128, 6, 512], BF16)
    tA = [A("tA%d" % i, [128, 512], F32) for i in range(2)]
    tB = [A("tB%d" % i, [128, 512], F32) for i in range(2)]
    tC = [A("tC%d" % i, [128, 512], F32) for i in range(2)]
    tz = [A("tz%d" % i, [128, 512], BF16) for i in range(2)]
    tq = [A("tq%d" % i, [128, 512], BF16) for i in range(2)]
    cosT = A("cosT", [128, 512], F32)
    sinT = A("sinT", [128, 512], F32)
    stg = A("stg", [128, 4, 512], F32)
    lamt = A("lamt", [128, 4, 64], F32)
    lamr = A("lamr", [128, 4], F32)
    rstdT = A("rstdT", [128, 512], F32)
    rtok = A("rtok", [128, 4], F32)
    epsv = lamt[:].rearrange("p a b -> p (a b)").bitcast(BF16)[:, 0:512]

    aT = BIG[:].rearrange("p (c t) -> p c t", c=NFC)
    xstg = BIG[:, 0:8192].bitcast(F32).rearrange("p (a b) -> p a b", a=4)
    sq8 = BIG[:, 0:4096].rearrange("p (c t) -> p c t", c=8)
    qpad = BIG[:, 0:8192].rearrange("p (c t) -> p c t", c=16)

    psb = [nc.alloc_psum_tensor("psb%d" % b, [128, 512], F32) for b in range(8)]
    rot_state = [0]

    def rot():
        b = rot_state[0] % 4
        rot_state[0] += 1
        return b

    def PS(b):
        return ("ps", b)

    def Bk(lo, hi):
        return [("B", c) for c in range(lo, hi)]

    ncd = nc.allow_non_contiguous_dma

    def dma(q, out, in_, reads, writes, key, nonc=False):
        E = nc.sync if q == "sp" else (nc.gpsimd if q == "pool" else nc.scalar)

        def emit():
            if nonc:
                with ncd(reason="tiny setup transfer"):
                    return E.dma_start(out=out, in_=in_)
            return E.dma_start(out=out, in_=in_)
        P.add(q, emit, reads, writes, dma_key=key)

    ukey = [0]

    def once_key():
        ukey[0] += 1
        return ("once", ukey[0] % 8)

    def dbg(name, ap, shape, reads):
        if debug is None or name not in debug:
            return
        o = dout("dbg_" + name, shape, ap.dtype)
        dbg_outs[name] = o
        dma("sp", o, ap, reads, [], ("dbg", name))

    def load_cast(dst_b, src):
        dma("sp", tA[0][:, 0:128], src, [], [("t", 0, "A")], ("once", 0))
        P.add("dve", lambda: nc.vector.tensor_copy(out=dst_b[:], in_=tA[0][:, 0:128]),
              [("t", 0, "A")], ["c"])

    dma("sp", ident[:], c_ident[:, :], [], ["c"], ("once", 1))
    load_cast(identb, c_ident[:, :])
    load_cast(antib, c_anti[:, :])
    load_cast(blkb, c_blk[:, :])
    load_cast(rpermb, c_rperm[:, :])
    P.add("pool", lambda: nc.gpsimd.memset(onesb[:], 1.0), [], ["c"])
    P.add("pool", lambda: nc.gpsimd.memset(epsb[:], EPS), [], ["c"])
    P.add("pool", lambda: nc.gpsimd.memset(lneps[:], float(math.log(EPS))), [], ["c"])
    P.add("pool", lambda: nc.gpsimd.memset(maskB[:], 0.0), [], ["c"])
    P.add("pool", lambda: nc.gpsimd.memset(maskB[64:128, :], -30000.0), [], ["c"])
    P.add("pool", lambda: nc.gpsimd.memset(maskA0[:], 0.0), [], ["c"])
    P.add("pool", lambda: nc.gpsimd.memset(maskA0[0:64, 64:128], -30000.0), [], ["c"])
    for wi, gsrc in enumerate((g_ffn1, g_mix, g_ffn2)):
        for l in range(NL):
            o = (wi * NL + l) * 8
            dma("sp", gx[:, o:o + 8], gsrc[l].rearrange("(c p) -> p c", p=128), [], ["g"], ("once", 2), nonc=True)
    for gi, gsrc in enumerate((g_qa, g_ka, g_qb, g_kb)):
        for l in range(NL):
            for hf in range(2):
                dma("sp", gh[hf * 64:(hf + 1) * 64, gi * NL + l:gi * NL + l + 1],
                    gsrc[l].rearrange("(p o) -> p o", o=1), [], ["g"], ("once", 3), nonc=True)
    for l in range(NL):
        lam_init = 0.8 - 0.6 * math.exp(-0.3 * l)
        dma("sp", tA[1][:, l:l + 1], g_sub[l].rearrange("(p o) -> p o", o=1), [], [("t", 1, "A")], ("once", 4), nonc=True)
        P.add("dve", lambda l=l, li=lam_init: nc.vector.tensor_scalar(
            out=gsub[:, l:l + 1], in0=tA[1][:, l:l + 1], scalar1=float(1.0 - li), scalar2=None, op0=ALU.mult),
            [("t", 1, "A")], ["g"])
        for k, nm in enumerate(("lam_q1", "lam_k1", "lam_q2", "lam_k2")):
            dma("sp", lamt[:, k, :], lam_in[nm][l:l + 1, :].partition_broadcast(128), [], ["lamt"], ("once", 5))
        for k in range(2):
            P.add("dve", lambda k=k: nc.vector.tensor_tensor(out=lamt[:, 2 * k, :], in0=lamt[:, 2 * k, :],
                                                             in1=lamt[:, 2 * k + 1, :], op=ALU.mult),
                  ["lamt"], ["lamt"])
            P.add("dve", lambda k=k: nc.vector.reduce_sum(out=lamr[:, k:k + 1], in_=lamt[:, 2 * k, :],
                                                          axis=mybir.AxisListType.X), ["lamt"], ["lamr"])
        P.add("act", lambda: nc.scalar.activation(out=lamr[:, 2:4], in_=lamr[:, 0:2], func=AF.Exp), ["lamr"], ["lamr"])
        P.add("dve", lambda l=l, li=lam_init: nc.vector.scalar_tensor_tensor(
            out=nlam[:, l:l + 1], in0=lamr[:, 3:4], scalar=float(-li), in1=lamr[:, 2:3], op0=ALU.add, op1=ALU.subtract),
            ["lamr"], ["g"])
        dma("sp", bconst[:, l * 8:(l + 1) * 8],
            bass.AP(tensor=rel_bias.tensor, offset=l * 8 * 257 + 256, ap=[[0, 128], [257, 8]]),
            [], ["bc"], ("once", 6), nonc=True)
        dma("sp", tB[0][0:8, 0:257], rel_bias[l], [], [("t", 0, "B")], ("once", 7))
        P.add("dve", lambda: nc.vector.tensor_copy(out=tB[0][0:8, 257:384],
                                                   in_=tB[0][0:8, 256:257].to_broadcast([8, 127])),
              [("t", 0, "B")], [("t", 0, "B")])
        dma("sp", tpad[l * 8:(l + 1) * 8, :], tB[0][0:8, 0:384], [("t", 0, "B")], ["tpad"], ("tpadw", l))
        for h in range(8):
            for di, Dv in enumerate((128, 0)):
                s = (h * 2 + di) % 2
                src = bass.AP(tensor=tpad.tensor, offset=(l * 8 + h) * 384 + Dv + 1, ap=[[1, 128], [1, 128]])
                dma("sp", tC[s][:, 0:128], src, ["tpad"], [("t", s, "C")], ("btl", s))
                bi = (l * 8 + h) * 2 + di
                P.add("dve", lambda s=s, bi=bi, l=l, h=h: nc.vector.tensor_scalar(
                    out=btile[:, bi, :], in0=tC[s][:, 0:128], scalar1=bconst[:, l * 8 + h:l * 8 + h + 1],
                    scalar2=8.0, op0=ALU.subtract, op1=ALU.mult), [("t", s, "C"), "bc"], ["bt"])
                if di == 1:
                    P.add("pool", lambda bi=bi: nc.gpsimd.memset(btile[0:64, bi, 0:64], -30000.0), [], ["bt"])

    cat_list = list(cat.items())
    cv_done = [0]
    CV_LEAD = 32
    CV_KEYS = 8

    def convert_upto(n_hi):
        while cv_done[0] < min(n_hi, len(cat_list)):
            n = cv_done[0]
            cv_done[0] += 1
            (l, name, idx), tid = cat_list[n]
            if name[0] in "gu" or name in ("in", "out"):
                src = W[name][l][:, idx * 256:(idx + 1) * 256].rearrange("(kc p) f -> p kc f", p=128)
                dst = wsc[tid].rearrange("p (kc f) -> p kc f", kc=8)
            else:
                half, t6 = idx // 6, idx % 6
                nfc = min(4, NFC - 4 * t6)
                src = W[name][l][t6 * 512:t6 * 512 + nfc * 128, half * 512:(half + 1) * 512].rearrange(
                    "(fc p) d -> p fc d", p=128)
                dst = wsc[tid][:, 0:nfc * 512].rearrange("p (fc d) -> p fc d", fc=nfc)
            dma("pool", dst, src, [], [("ws", tid), ("cvk", n % CV_KEYS)], ("cv", n % CV_KEYS))

    wcount = [0]

    def wload(l, name, idx, nel=WT):
        tid = cat[(l, name, idx)]
        convert_upto(tid + 1 + CV_LEAD)
        s = wcount[0] % NSLOT
        wcount[0] += 1
        dma("sp", wring[:, s, 0:nel], wsc[tid][:, 0:nel], [("ws", tid)], [("w", s)], ("w", s))
        return s

    nstate = {"bank": None, "pend": []}
    sqbuf = [(tq[0], ("t", 0, "q")), (tq[1], ("t", 1, "q")), (tz[0], ("t", 0, "z")), (tz[1], ("t", 1, "z"))]

    def norm_flush():
        for (b, dc, T) in nstate["pend"]:
            sb, stok = sqbuf[dc % 4]
            P.add("pe", lambda: nc.tensor.matmul(psb[b][:, 0:T], onesb[:], sb[:, 0:T],
                                                 start=(dc == 0), stop=(dc == 7)), [stok, "c"], [PS(b)])
        nstate["pend"] = []

    def norm_feed(dc, T, gcol, bank=None):
        if gcol is None:
            return
        if dc == 0:
            nstate["bank"] = rot() if bank is None else bank
        b = nstate["bank"]
        sb, stok = sqbuf[dc % 4]
        P.add("act", lambda: nc.scalar.activation(out=sb[:, 0:T], in_=xT[:, dc, 0:T], func=AF.Square),
              [("xT", dc)], [stok])
        P.add("dve", lambda: nc.vector.tensor_scalar(
            out=hT[:, dc, 0:T], in0=xT[:, dc, 0:T], scalar1=gx[:, gcol + dc:gcol + dc + 1], scalar2=None,
            op0=ALU.mult), [("xT", dc), "g"], [("hT", dc)])
        nstate["pend"].append((b, dc, T))

    def norm_finish(T, for_mix=False, blocks=None):
        norm_flush()
        b = nstate["bank"]
        P.add("act", lambda: nc.scalar.activation(out=rstdT[:, 0:T], in_=psb[b][:, 0:T], func=AF.Ln,
                                                  scale=1.0 / D_MODEL, bias=epsb[:, 0:1]), ["c"], [PS(b), "rstd"])
        if for_mix:
            P.add("act", lambda: nc.scalar.activation(out=epsv[:, 0:T], in_=rstdT[:, 0:T], func=AF.Exp,
                                                      bias=lneps[:, 0:1]), ["rstd", "c"], ["epsv", "lamt"])
        P.add("act", lambda: nc.scalar.activation(out=rstdT[:, 0:T], in_=rstdT[:, 0:T], func=AF.Exp, scale=-0.5),
              [], ["rstd"])
        if for_mix:
            nstate["rtok"] = (T, blocks)

    def rtok_part():
        T, blocks = nstate["rtok"]
        if True:
            bt = rot8()
            for k, (c0, n) in enumerate(blocks):
                P.add("pe", lambda k=k, c0=c0, n=n: nc.tensor.transpose(
                    psb[bt][0:n, k * 128:(k + 1) * 128], rstdT[:, c0:c0 + n], ident[:]), ["rstd", "c"], [PS(bt)])
            nb = len(blocks)
            n0 = blocks[0][1]
            P.add("dve", lambda: nc.vector.tensor_copy(
                out=rtok[0:n0, 0:nb], in_=psb[bt][0:n0, 0:nb * 128].rearrange("p (k f) -> p k f", k=nb)[:, :, 0]),
                [], [PS(bt), "rtok"])

    def ffn(l, f, T, pre=None):
        if f == 1:
            gnext = (1 * NL + l) * 8
        else:
            gnext = (0 * NL + l + 1) * 8 if l + 1 < NL else None
        HT = [("hT", c) for c in range(8)]
        for ft in range(11):
            sg = wload(l, "g%d" % f, ft)
            su = wload(l, "u%d" % f, ft)
            wg = wring[:, sg, :].rearrange("p (kc f) -> p kc f", kc=8)
            wu = wring[:, su, :].rearrange("p (kc f) -> p kc f", kc=8)
            for fl in range(2):
                fc = 2 * ft + fl
                bg, bu = rot(), rot()
                for kc in range(8):
                    P.add("pe", lambda kc=kc, bg=bg, wg=wg, fl=fl: nc.tensor.matmul(
                        psb[bg][:, 0:T], wg[:, kc, fl * 128:(fl + 1) * 128], hT[:, kc, 0:T],
                        start=(kc == 0), stop=(kc == 7)), [("w", sg), ("hT", kc)], [PS(bg)])
                for kc in range(8):
                    P.add("pe", lambda kc=kc, bu=bu, wu=wu, fl=fl: nc.tensor.matmul(
                        psb[bu][:, 0:T], wu[:, kc, fl * 128:(fl + 1) * 128], hT[:, kc, 0:T],
                        start=(kc == 0), stop=(kc == 7)), [("w", su), ("hT", kc)], [PS(bu)])
                ts = fc % 2
                if fc == 0:
                    norm_finish(T)
                P.add("dve", lambda bg=bg, ts=ts: nc.vector.tensor_tensor(
                    out=tA[ts][:, 0:T], in0=psb[bg][:, 0:T], in1=rstdT[:, 0:T], op=ALU.mult),
                    ["rstd"], [PS(bg), ("t", ts, "A")])
                P.add("act", lambda ts=ts: nc.scalar.activation(out=tA[ts][:, 0:T], in_=tA[ts][:, 0:T],
                                                                func=AF.Silu), [], [("t", ts, "A")])
                P.add("dve", lambda bu=bu, ts=ts, fc=fc: nc.vector.tensor_tensor(
                    out=aT[:, fc, 0:T], in0=psb[bu][:, 0:T], in1=tA[ts][:, 0:T], op=ALU.mult),
                    [("t", ts, "A")], [PS(bu), ("B", fc)])
        if pre is not None:
            pre()
        for half in range(2):
            for t6 in range(6):
                nfc = min(4, NFC - 4 * t6)
                sd = wload(l, "d%d" % f, half * 6 + t6, nel=nfc * 512)
                wd = wring[:, sd, 0:nfc * 512].rearrange("p (fc d) -> p fc d", fc=nfc)
                if half == 1 and t6 == 2:
                    norm_flush()
                for fl in range(nfc):
                    fc = 4 * t6 + fl
                    for dcl in range(4):
                        ab = (4 + dcl) if half == 0 else dcl
                        P.add("pe", lambda fl=fl, fc=fc, dcl=dcl, wd=wd, ab=ab: nc.tensor.matmul(
                            psb[ab][:, 0:T], wd[:, fl, dcl * 128:(dcl + 1) * 128], aT[:, fc, 0:T],
                            start=(fc == 0), stop=(fc == NFC - 1)), [("w", sd), ("B", fc)], [PS(ab)])
            for dcl in range(4):
                dc = half * 4 + dcl
                tb_ = tB[dcl % 2]
                ab = (4 + dcl) if half == 0 else dcl
                P.add("dve", lambda dcl=dcl, tb_=tb_, ab=ab: nc.vector.tensor_tensor(
                    out=tb_[:, 0:T], in0=psb[ab][:, 0:T], in1=rstdT[:, 0:T], op=ALU.mult),
                    ["rstd"], [PS(ab), ("t", dcl % 2, "B")])
                P.add("dve", lambda dc=dc, tb_=tb_: nc.vector.scalar_tensor_tensor(
                    out=xT[:, dc, 0:T], in0=tb_[:, 0:T], scalar=0.5, in1=xT[:, dc, 0:T],
                    op0=ALU.mult, op1=ALU.add), [("t", dcl % 2, "B")], [("xT", dc)])
            for dcl in range(4):
                norm_feed(half * 4 + dcl, T, gnext, bank=4)

    stgx = stg[:].rearrange("p a b -> p (a b)").rearrange("p (t d) -> p t d", t=2)
    XPIECE = {2: [(tA[0], ("t", 0, "A")), (tA[1], ("t", 1, "A"))],
              3: [(tC[0], ("t", 0, "C")), (tC[1], ("t", 1, "C"))]}

    def prefetch_x(src_rows, T):
        ntt = T // 128
        dma("sp", stgx[:, 0:2, :], src_rows[0:256, :].rearrange("(t p) d -> p t d", p=128), [],
            [("stg", k) for k in range(4)], "xin0")
        for tt in range(2, ntt):
            for hf in range(2):
                buf, tok = XPIECE[tt][hf]
                dma("sp", buf[:, :], src_rows[tt * 128:(tt + 1) * 128, hf * 512:(hf + 1) * 512], [], [tok],
                    "xin%d" % (1 + (tt - 2) * 2 + hf))

    def load_x(T):
        ntt = T // 128
        for dc in range(8):
            b = rot()
            for tt in range(ntt):
                if tt < 2:
                    src, toks = stgx[:, tt, dc * 128:(dc + 1) * 128], [("stg", 2 * tt), ("stg", 2 * tt + 1)]
                else:
                    buf, tok = XPIECE[tt][dc // 4]
                    src, toks = buf[:, (dc % 4) * 128:(dc % 4 + 1) * 128], [tok]
                P.add("pe", lambda tt=tt, src=src: nc.tensor.transpose(
                    psb[b][:, tt * 128:(tt + 1) * 128], src, ident[:]), toks + ["c"], [PS(b)])
            norm_flush()
            if dc % 2 == 0:
                P.add("act", lambda: nc.scalar.copy(out=xT[:, dc, 0:T], in_=psb[b][:, 0:T]),
                      [], [PS(b), ("xT", dc)])
            else:
                P.add("dve", lambda: nc.vector.tensor_copy(out=xT[:, dc, 0:T], in_=psb[b][:, 0:T]),
                      [], [PS(b), ("xT", dc)])
            norm_feed(dc, T, 0, bank=7)

    def store_y(dst_rows, T):
        ntt = T // 128
        for tt in range(ntt):
            for hf in range(2):
                b = rot()
                for d4 in range(4):
                    dc = hf * 4 + d4
                    P.add("pe", lambda dc=dc, d4=d4, tt=tt, b=b: nc.tensor.transpose(
                        psb[b][:, d4 * 128:(d4 + 1) * 128], xT[:, dc, tt * 128:(tt + 1) * 128], ident[:]),
                        [("xT", dc), "c"], [PS(b)])
                if hf == 0:
                    P.add("act", lambda tt=tt, hf=hf, b=b: nc.scalar.copy(
                        out=xstg[:, tt, hf * 512:(hf + 1) * 512], in_=psb[b][:, :]), [], [PS(b)] + Bk(0, 16))
                else:
                    P.add("dve", lambda tt=tt, hf=hf, b=b: nc.vector.tensor_copy(
                        out=xstg[:, tt, hf * 512:(hf + 1) * 512], in_=psb[b][:, :]), [], [PS(b)] + Bk(0, 16))
        dma("pool", dst_rows.rearrange("(t p) d -> p t d", p=128), xstg[:, 0:ntt, :], Bk(0, 16), [], "yst")

    def store_rows(src_f32, ntok_p, dst, reads, slot):
        dma("pool", dst, src_f32, reads, [], ("stg", slot))

    rot8_state = [0]

    def rot8():
        b = rot8_state[0] % 8
        rot8_state[0] += 1
        return b

    def mix(l, grp):
        T = grp["T"]
        ntt = T // 128
        isP = grp["kind"] == "p"
        if isP:
            vblocks = [(tb * 128, 128) for tb in range(ntt)]
        else:
            vblocks = [(tb * 64, 64) for tb in range(grp["nseq"])]
        norm_finish(T, for_mix=True, blocks=vblocks)
        if isP:
            i = grp["i"]
            sq_ = grp["seq"]
            tok0 = i * 512
            dma("sp", cosT[:, 0:T], c_cosp[:, i * 512:(i + 1) * 512], [], ["cs"], "cs")
            dma("sp", sinT[:, 0:T], c_sinp[:, i * 512:(i + 1) * 512], [], ["cs"], "cs")
        else:
            dma("sp", cosT[:, 0:T], c_coss[:, 0:T], [], ["cs"], "cs")
            dma("sp", sinT[:, 0:T], c_sins[:, 0:T], [], ["cs"], "cs")
        P.add("pool", lambda: nc.gpsimd.memset(BIG[:, 0:8192], 0.0), [], Bk(0, 16))
        oth = 1 - l
        wtl = {}

        def get_w(gidx):
            if gidx not in wtl:
                s0 = wload(l, "in", 2 * gidx)
                s1 = wload(l, "in", 2 * gidx + 1)
                wtl[gidx] = ([wring[:, s0, :].rearrange("p (kc f) -> p kc f", kc=8),
                              wring[:, s1, :].rearrange("p (kc f) -> p kc f", kc=8)], [s0, s1])
            return wtl[gidx]

        GIDX = {"qa": 0, "ka": 1, "va": 2, "qb": 3, "kb": 4, "vb": 5}
        jobs = []

        def v_job(kind, tb):
            isBv = kind == "vb"
            bsz = 128 if isP else 64

            def Pst():
                wv, ws = get_w(GIDX[kind])
                b = rot8()
                for half in range(2):
                    for kc in range(8):
                        P.add("pe", lambda kc=kc, half=half: nc.tensor.matmul(
                            psb[b][0:bsz, half * 256:(half + 1) * 256], hT[:, kc, tb * bsz:(tb + 1) * bsz],
                            wv[half][:, kc, :], start=(kc == 0), stop=(kc == 7)),
                            [("w", ws[half]), ("hT", kc)], [PS(b)])
                slot = tb % 4
                P.add("act", lambda: nc.scalar.activation(out=stg[0:bsz, slot, :], in_=psb[b][0:bsz, :], func=AF.Copy,
                                                          scale=rtok[0:bsz, tb:tb + 1]), ["rtok"], [PS(b), ("stg", slot)])
                if isP:
                    kt = (i * 4 + tb)
                    if isBv:
                        dstt, tok = vbt[l][:, kt, :], ("vb", l, kt)
                    else:
                        dstt, tok = vat[l][:, kt % 8, :], ("va", l, kt % 8)
                    P.add("pool", lambda: nc.gpsimd.tensor_copy(out=dstt, in_=stg[:, slot, :]), [("stg", slot)], [tok])
                    if isBv:
                        store_rows(stg[:, slot, :], 128, pbv[l, sq_, tok0 + tb * 128:tok0 + (tb + 1) * 128, :],
                                   [("stg", slot)], slot)
                    elif i == 3:
                        store_rows(stg[:, slot, :], 128, pav[l, sq_, tb * 128:(tb + 1) * 128, :],
                                   [("stg", slot)], slot)
                else:
                    if isBv:
                        dstt, tok = vbt[oth][0:64, tb, :], ("vb", oth, tb)
                    else:
                        dstt, tok = vat[oth][0:64, tb, :], ("va", oth, tb)
                    P.add("pool", lambda: nc.gpsimd.tensor_copy(out=dstt, in_=stg[0:64, slot, :]), [("stg", slot)], [tok])
                    store_rows(stg[0:64, slot, :], 64, (sbv if isBv else sav)[l, tb, :, :], [("stg", slot)], slot)
            return [Pst, None, None, None]

        def qk_job(kind, c, ts):
            isB = kind[1] == "b"
            gcol = {"qa": 0, "ka": 1, "qb": 2, "kb": 3}[kind] * NL + l
            GC = gh[:, gcol:gcol + 1]
            need_rows = (kind == "kb") or (kind == "ka" and (not isP or grp["i"] == 3))
            st = {}
            FT = ("t", ts, "B")
            if kind == "ka":
                if isP:
                    dstk, ktok = kaT[l][:, c, (i % 2) * 512:(i % 2) * 512 + T], ("ka", l, c)
                else:
                    dstk, ktok = kaT[oth][:, c, 0:T], ("ka", oth, c)
            elif kind == "kb":
                if isP:
                    dstk, ktok = kbT[l][:, c, tok0:tok0 + T], ("kb", l, c)
                else:
                    dstk, ktok = kbT[oth][:, c, 0:T], ("kb", oth, c)
            qbase = 0 if kind == "qa" else 8

            def Pst():
                wv, ws = get_w(GIDX[kind])
                half, cl = c // 2, c % 2
                bz = rot8()
                st["bz"] = bz
                for kc in range(8):
                    P.add("pe", lambda kc=kc: nc.tensor.matmul(
                        psb[bz][:, 0:T], wv[half][:, kc, cl * 128:(cl + 1) * 128], hT[:, kc, 0:T],
                        start=(kc == 0), stop=(kc == 7)), [("w", ws[half]), ("hT", kc)], [PS(bz)])
                P.add("act", lambda: nc.scalar.activation(out=tq[ts][:, 0:T], in_=psb[bz][:, 0:T], func=AF.Square),
                      [], [PS(bz), ("t", ts, "q")])
                if isB:
                    P.add("act", lambda: nc.scalar.activation(out=tz[ts][:, 0:T], in_=psb[bz][:, 0:T], func=AF.Copy,
                                                              scale=GC), ["g"], [PS(bz), ("t", ts, "z")])

            def Bst():
                bz = st["bz"]
                if isB:
                    P.add("dve", lambda: nc.vector.scalar_tensor_tensor(
                        out=tB[ts][:, 0:T], in0=psb[bz][:, 0:T], scalar=GC, in1=cosT[:, 0:T],
                        op0=ALU.mult, op1=ALU.mult), ["cs", "g"], [PS(bz), FT])
                bn = rot8()
                P.add("pe", lambda: nc.tensor.matmul(psb[bn][:, 0:T], blkb[:], tq[ts][:, 0:T], start=True, stop=False),
                      [("t", ts, "q"), "c"], [PS(bn)])
                P.add("pe", lambda: nc.tensor.matmul(psb[bn][:, 0:T], blkb[:], epsv[:, 0:T], start=False, stop=True),
                      ["epsv", "c"], [PS(bn)])
                P.add("act", lambda: nc.scalar.activation(out=tA[ts][:, 0:T], in_=psb[bn][:, 0:T], func=AF.Ln,
                                                          scale=1.0 / 64), [], [PS(bn), ("t", ts, "A")])
                P.add("act", lambda: nc.scalar.activation(out=tA[ts][:, 0:T], in_=tA[ts][:, 0:T], func=AF.Exp, scale=-0.5),
                      [], [("t", ts, "A")])
                if isB:
                    br = rot8()
                    P.add("pe", lambda: nc.tensor.matmul(psb[br][:, 0:T], rpermb[:], tz[ts][:, 0:T], start=True, stop=True),
                          [("t", ts, "z"), "c"], [PS(br)])
                    P.add("dve", lambda: nc.vector.tensor_tensor(out=tC[ts][:, 0:T], in0=psb[br][:, 0:T], in1=sinT[:, 0:T],
                                                                 op=ALU.mult), ["cs"], [PS(br), ("t", ts, "C")])
                elif kind == "qa":
                    for hf in range(2):
                        lo, hi = hf * 64, hf * 64 + 64
                        P.add("dve", lambda lo=lo, hi=hi, hf=hf: nc.vector.scalar_tensor_tensor(
                            out=qpad[lo:hi, 2 * c + hf, 0:T], in0=psb[bz][lo:hi, 0:T], scalar=gh[lo:hi, gcol:gcol + 1],
                            in1=tA[ts][lo:hi, 0:T], op0=ALU.mult, op1=ALU.mult),
                            [("t", ts, "A"), "g"], [PS(bz), ("B", 2 * c + hf)])
                elif need_rows:
                    P.add("dve", lambda: nc.vector.scalar_tensor_tensor(
                        out=tB[ts][:, 0:T], in0=psb[bz][:, 0:T], scalar=GC, in1=tA[ts][:, 0:T],
                        op0=ALU.mult, op1=ALU.mult), [("t", ts, "A"), "g"], [PS(bz), FT])
                    P.add("pool", lambda: nc.gpsimd.tensor_copy(out=dstk, in_=tB[ts][:, 0:T]), [FT], [ktok])
                else:
                    P.add("dve", lambda: nc.vector.scalar_tensor_tensor(
                        out=dstk, in0=psb[bz][:, 0:T], scalar=GC, in1=tA[ts][:, 0:T],
                        op0=ALU.mult, op1=ALU.mult), [("t", ts, "A"), "g"], [PS(bz), ktok])

            def Rst():
                P.add("dve", lambda: nc.vector.tensor_tensor(out=tB[ts][:, 0:T], in0=tB[ts][:, 0:T], in1=tC[ts][:, 0:T],
                                                             op=ALU.add), [("t", ts, "C")], [FT])
                if kind == "qb":
                    for m in range(2):
                        lo, hi = m * 64, m * 64 + 64
                        P.add("dve", lambda lo=lo, hi=hi, m=m: nc.vector.tensor_tensor(
                            out=qpad[lo:hi, 8 + 2 * c + m, 0:T], in0=tB[ts][lo:hi, 0:T], in1=tA[ts][lo:hi, 0:T],
                            op=ALU.mult), [FT, ("t", ts, "A")], [("B", 8 + 2 * c + m)])
                else:
                    P.add("dve", lambda: nc.vector.tensor_tensor(out=tB[ts][:, 0:T], in0=tB[ts][:, 0:T], in1=tA[ts][:, 0:T],
                                                                 op=ALU.mult), [("t", ts, "A")], [FT])
                    P.add("pool", lambda: nc.gpsimd.tensor_copy(out=dstk, in_=tB[ts][:, 0:T]), [FT], [ktok])

            def Xst():
                fin = tB[ts]
                bt = rot8()
                for tt in range(ntt):
                    P.add("pe", lambda tt=tt: nc.tensor.transpose(
                        psb[bt][:, tt * 128:(tt + 1) * 128], fin[:, tt * 128:(tt + 1) * 128], ident[:]),
                        [FT, "c"], [PS(bt)])
                P.add("dve", lambda: nc.vector.tensor_copy(
                    out=stg[:, 0:ntt, c * 128:(c + 1) * 128],
                    in_=psb[bt][:, 0:T].rearrange("p (t f) -> p t f", t=ntt)),
                    [], [PS(bt)] + [("stg", s_) for s_ in range(ntt)])
                if c == 3:
                    SR = [("stg", s_) for s_ in range(ntt)]
                    if isP:
                        if kind == "kb":
                            dst = pbk[l, sq_, tok0:tok0 + T, :].rearrange("(t p) f -> p t f", p=128)
                        else:
                            dst = pak[l, sq_, :, :].rearrange("(t p) f -> p t f", p=128)
                        dma("pool", dst, stg[:, 0:ntt, :], SR, [], ("stg", 0))
                    else:
                        dd = sbk if kind == "kb" else sak
                        for tt in range(ntt):
                            dst = dd[l, 2 * tt:2 * tt + 2, :, :].rearrange("s j f -> (s j) f")
                            dma("pool", dst, stg[:, tt, :], [("stg", tt)], [], ("stg", tt))
            return [Pst, Bst, Rst if isB else None, Xst if need_rows else None]

        nblk = ntt if isP else grp["nseq"]
        nq = 0
        for kind in ("qa", "va", "qb", "vb", "ka", "kb"):
            if kind[0] == "v":
                for tb in range(nblk):
                    jobs.append(v_job(kind, tb))
            else:
                for c in range(4):
                    jobs.append(qk_job(kind, c, nq % 2))
                    nq += 1
        nj = len(jobs)
        for step in range(nj + 3):
            for k in (3, 2, 0, 1):
                ci = step - k
                if 0 <= ci < nj and jobs[ci][k] is not None:
                    jobs[ci][k]()
            if step == 1:
                rtok_part()
        if isP:
            attn_prompt(l, grp)
        else:
            attn_sample(l, grp)
        for ot in range(4):
            so = wload(l, "out", ot)
            wo = wring[:, so, :].rearrange("p (ec f) -> p ec f", ec=8)
            for dcl in range(2):
                dc = 2 * ot + dcl
                b = rot()
                for n_, ec in enumerate((4, 5, 6, 7, 0, 1, 2, 3)):
                    oc = 8 + ec if ec < 4 else 12 + ec
                    P.add("pe", lambda n_=n_, ec=ec, oc=oc, dcl=dcl, b=b, wo=wo: nc.tensor.matmul(
                        psb[b][:, 0:T], wo[:, ec, dcl * 128:(dcl + 1) * 128], aT[:, oc, 0:T],
                        start=(n_ == 0), stop=(n_ == 7)), [("w", so), ("B", oc)], [PS(b)])
                norm_flush()
                P.add("dve", lambda dc=dc, b=b: nc.vector.tensor_tensor(
                    out=xT[:, dc, 0:T], in0=psb[b][:, 0:T], in1=xT[:, dc, 0:T], op=ALU.add),
                    [], [PS(b), ("xT", dc)])
                norm_feed(dc, T, (2 * NL + l) * 8, bank=7)

    ering = [0]

    def eslot():
        s = ering[0] % 6
        ering[0] += 1
        return s

    def zero_block(ap, etok):
        P.add("act", lambda: nc.scalar.mul(out=ap, in_=ap, mul=0.0), [], [etok])

    def attn_prompt(l, grp):
        i = grp["i"]
        T = 512
        nk = 4 * (i + 1)
        LA = 3
        tiles = [(h, m, j) for h in range(4) for m in range(2) for j in range(nk)]
        bo = [4, 6]
        bZ = [5, 7]
        pend = {}
        deferred = []

        def b_norm_m(h, m):
            P.add("act", lambda: nc.scalar.activation(out=tA[m][:, 0:T], in_=psb[bZ[m]][:, 0:T], func=AF.Ln),
                  [], [PS(bZ[m]), ("t", m, "A")])
            P.add("act", lambda: nc.scalar.activation(out=tA[m][:, 0:T], in_=tA[m][:, 0:T], func=AF.Exp, scale=-1.0),
                  [], [("t", m, "A")])
            P.add("dve", lambda: nc.vector.tensor_tensor(
                out=tB[m][:, 0:T], in0=psb[bo[m]][:, 0:T], in1=tA[m][:, 0:T], op=ALU.mult),
                [("t", m, "A")], [PS(bo[m]), ("t", m, "B")])

        def b_combine(h):
            P.add("dve", lambda: nc.vector.scalar_tensor_tensor(
                out=tC[0][:, 0:T], in0=tB[1][:, 0:T], scalar=nlam[:, l:l + 1], in1=tB[0][:, 0:T],
                op0=ALU.mult, op1=ALU.add), [("t", 0, "B"), ("t", 1, "B"), "g"], [("t", 0, "C")])
            P.add("act", lambda: nc.scalar.activation(out=tq[0][:, 0:T], in_=tC[0][:, 0:T], func=AF.Square),
                  [("t", 0, "C")], [("t", 0, "q")])

        def b_subnorm(h):
            bn = rot()
            P.add("pe", lambda: nc.tensor.matmul(psb[bn][:, 0:T], onesb[:], tq[0][:, 0:T], start=True, stop=True),
                  [("t", 0, "q"), "c"], [PS(bn)])
            P.add("act", lambda: nc.scalar.activation(out=tC[1][:, 0:T], in_=psb[bn][:, 0:T], func=AF.Ln,
                                                      scale=1.0 / 128, bias=epsb[:, 0:1]), ["c"], [PS(bn), ("t", 1, "C")])
            P.add("act", lambda: nc.scalar.activation(out=tC[1][:, 0:T], in_=tC[1][:, 0:T], func=AF.Exp, scale=-0.5),
                  [], [("t", 1, "C")])
            P.add("dve", lambda: nc.vector.scalar_tensor_tensor(
                out=aT[:, 16 + h, 0:T], in0=tC[0][:, 0:T], scalar=gsub[:, l:l + 1], in1=tC[1][:, 0:T],
                op0=ALU.mult, op1=ALU.mult), [("t", 0, "C"), ("t", 1, "C"), "g"], [("B", 16 + h)])

        def b1(idx):
            h, m, j = tiles[idx]
            qv = qpad[:, 8 + 2 * h + m, :]
            QT = ("B", 8 + 2 * h + m)
            r = j - 4 * i
            q0 = 128 * r if r > 0 else 0
            bs = rot()
            P.add("pe", lambda: nc.tensor.matmul(
                psb[bs][:, q0:512], kbT[l][:, h, j * 128:(j + 1) * 128], qv[:, q0:512],
                start=True, stop=(r < 0)), [("kb", l, h), QT], [PS(bs)])
            if r >= 0:
                P.add("pe", lambda: nc.tensor.matmul(psb[bs][:, q0:q0 + 64], identb[:], maskB[:],
                                                     start=False, stop=True), ["c"], [PS(bs)])
            es = eslot()
            P.add("act", lambda: nc.scalar.activation(
                out=eT[:, es, q0:512], in_=psb[bs][:, q0:512], func=AF.Exp, scale=0.125),
                [], [PS(bs), ("e", es)])
            pend[idx] = (es, q0)

        def b2(idx):
            h, m, j = tiles[idx]
            es, q0 = pend.pop(idx)
            P.add("pe", lambda: nc.tensor.matmul(
                psb[bo[m]][:, q0:512], vbt[l][:, j, h * 128:(h + 1) * 128], eT[:, es, q0:512],
                start=(j == 0), stop=(j == nk - 1)), [("vb", l, j), ("e", es)], [PS(bo[m])])
            P.add("pe", lambda: nc.tensor.matmul(
                psb[bZ[m]][:, q0:512], onesb[:], eT[:, es, q0:512],
                start=(j == 0), stop=(j == nk - 1)), [("e", es), "c"], [PS(bZ[m])])
            if j == nk - 1:
                deferred.append([idx + 2, b_norm_m, (h, m)])
                if m == 1:
                    deferred.append([idx + 3, b_combine, (h,)])
                    deferred.append([idx + 6, b_subnorm, (h,)])

        def flush(upto):
            while deferred and deferred[0][0] <= upto:
                _, fn, args = deferred.pop(0)
                fn(*args)

        nt = len(tiles)
        for idx in range(nt + LA):
            if idx < nt:
                b1(idx)
            if idx - LA >= 0:
                b2(idx - LA)
                flush(idx - LA)
        units = [(h, u) for h in range(8) for u in range(4)]
        pendA = {}

        def a1(idx):
            h, u = units[idx]
            hp = h // 2
            bi0 = (l * 8 + h) * 2
            ug = 4 * i + u
            t4 = [t for t in range(4) if ug - 4 + t >= 0]
            BC = bconst[:, l * 8 + h:l * 8 + h + 1]
            qv = qpad[:, h, u * 128:(u + 1) * 128]
            j = ug
            kcol = ((j // 4) % 2) * 512 + (j % 4) * 128
            b2_ = rot()
            P.add("pe", lambda: nc.tensor.matmul(psb[b2_][:, 0:128], antib[:], btile[:, bi0 + 1, :],
                                                 start=True, stop=False), ["bt", "c"], [PS(b2_)])
            P.add("pe", lambda: nc.tensor.matmul(
                psb[b2_][:, 0:128], kaT[l][:, hp, kcol:kcol + 128], qv, start=False, stop=True),
                [("ka", l, hp), ("B", h)], [PS(b2_)])
            es2 = eslot()
            P.add("act", lambda: nc.scalar.activation(
                out=eT[:, es2, 0:128], in_=psb[b2_][:, 0:128], func=AF.Exp, scale=0.125, bias=BC),
                ["bc"], [PS(b2_), ("e", es2)])
            es = None
            if t4:
                b1_ = rot()
                for t in t4:
                    j = ug - 4 + t
                    kcol = ((j // 4) % 2) * 512 + (j % 4) * 128
                    first = True
                    if t == 3:
                        P.add("pe", lambda t=t: nc.tensor.matmul(
                            psb[b1_][:, t * 128:(t + 1) * 128], antib[:], btile[:, bi0, :], start=True, stop=False),
                            ["bt", "c"], [PS(b1_)])
                        first = False
                    if t == 0:
                        P.add("pe", lambda t=t: nc.tensor.matmul(
                            psb[b1_][:, 0:128], identb[:], maskA0[:], start=True, stop=False), ["c"], [PS(b1_)])
                        first = False
                    P.add("pe", lambda t=t, kcol=kcol, first=first: nc.tensor.matmul(
                        psb[b1_][:, t * 128:(t + 1) * 128], kaT[l][:, hp, kcol:kcol + 128], qv,
                        start=first, stop=True), [("ka", l, hp), ("B", h)], [PS(b1_)])
                es = eslot()
                c0 = t4[0] * 128
                P.add("act", lambda: nc.scalar.activation(
                    out=eT[:, es, c0:512], in_=psb[b1_][:, c0:512], func=AF.Exp, scale=0.125, bias=BC),
                    ["bc"], [PS(b1_), ("e", es)])
            pendA[idx] = [(4, es2, 0)] + [(t, es, t * 128) for t in t4]

        def a2(idx):
            h, u = units[idx]
            hp, hf = h // 2, h % 2
            ug = 4 * i + u
            bo_, bZ_ = (4, 5) if h % 2 == 0 else (6, 7)
            seq_mm = pendA.pop(idx)
            for n_, (t, e_, ecol) in enumerate(seq_mm):
                j = ug - 4 + t
                P.add("pe", lambda n_=n_, j=j, e_=e_, ecol=ecol: nc.tensor.matmul(
                    psb[bo_][:, u * 128:(u + 1) * 128], vat[l][:, j % 8, hp * 128:(hp + 1) * 128],
                    eT[:, e_, ecol:ecol + 128], start=(n_ == 0), stop=(n_ == len(seq_mm) - 1)),
                    [("va", l, j % 8), ("e", e_)], [PS(bo_)])
                P.add("pe", lambda n_=n_, e_=e_, ecol=ecol: nc.tensor.matmul(
                    psb[bZ_][:, u * 128:(u + 1) * 128], onesb[:], eT[:, e_, ecol:ecol + 128],
                    start=(n_ == 0), stop=(n_ == len(seq_mm) - 1)), [("e", e_), "c"], [PS(bZ_)])
            if u == 3:
                deferred.append([idx + 1, a_finish, (h,)])

        def a_finish(h):
            hp, hf = h // 2, h % 2
            bo_, bZ_ = (4, 5) if h % 2 == 0 else (6, 7)
            ts = h % 2
            lo, hi = hf * 64, hf * 64 + 64
            P.add("act", lambda: nc.scalar.activation(
                out=tA[ts][lo:hi, 0:T], in_=psb[bZ_][lo:hi, 0:T], func=AF.Ln), [], [PS(bZ_), ("t", ts, "A")])
            P.add("act", lambda: nc.scalar.activation(
                out=tA[ts][lo:hi, 0:T], in_=tA[ts][lo:hi, 0:T], func=AF.Exp, scale=-1.0), [], [("t", ts, "A")])
            P.add("dve", lambda: nc.vector.tensor_tensor(
                out=aT[lo:hi, 8 + hp, 0:T], in0=psb[bo_][lo:hi, 0:T], in1=tA[ts][lo:hi, 0:T], op=ALU.mult),
                [("t", ts, "A")], [PS(bo_), ("B", 8 + hp)])

        nu = len(units)
        for idx in range(nu + 1):
            if idx < nu:
                a1(idx)
            if idx >= 1:
                a2(idx - 1)
                if idx >= 2:
                    flush(idx - 1)
            if idx == 1:
                flush(10 ** 9)
        flush(10 ** 9)

    def attn_sample(l, grp):
        T = grp["T"]
        nseq = grp["nseq"]
        oth = 1 - l
        cstg = [vbt[oth][:, 4 + 4 * s_:8 + 4 * s_, :].rearrange("p a b -> p (a b)").bitcast(F32).rearrange(
            "p (k f) -> p k f", k=2) for s_ in range(3)]
        CT = [[("vb", oth, 4 + 4 * s_ + k) for k in range(4)] for s_ in range(3)]
        ev = [0]
        vc = [0]

        def evac(out, in_, reads, writes):
            ev[0] += 1
            if ev[0] % 2 == 0:
                P.add("act", lambda: nc.scalar.copy(out=out, in_=in_), reads, writes)
            else:
                P.add("dve", lambda: nc.vector.tensor_copy(out=out, in_=in_), reads, writes)

        def KBS(h, half):
            return ("kbs", l, h, half)

        pieces = []
        for s_ in range(nseq):
            for half in range(2):
                for pc in range(4):
                    pieces.append((s_, cbk, half * 4 + pc, True, True))
                for pc in range(4):
                    pieces.append((s_, cbv, half * 4 + pc, False, True))
            for pc in range(2):
                pieces.append((s_, cak, pc, True, False))
            for pc in range(2):
                pieces.append((s_, cav, pc, False, False))
        pstate = {"dma": 0, "conv": 0}

        def piece_dma():
            n = pstate["dma"]
            if n >= len(pieces):
                return
            pstate["dma"] += 1
            s_, csrc, pc, isK, isBc = pieces[n]
            sl = n % 3
            dma("sp", cstg[sl], csrc[l, s_, pc * 256:(pc + 1) * 256, :].rearrange("(k p) f -> p k f", p=128),
                [], CT[sl], ("cst", sl))

        def piece_conv():
            n = pstate["conv"]
            if n >= len(pieces):
                return
            while pstate["dma"] < min(n + 3, len(pieces)):
                piece_dma()
            pstate["conv"] += 1
            s_, csrc, pc, isK, isBc = pieces[n]
            sl = n % 3
            for k in range(2):
                kt = pc * 2 + k
                if isK:
                    b = rot()
                    for c in range(4):
                        P.add("pe", lambda c=c: nc.tensor.transpose(
                            psb[b][:, c * 128:(c + 1) * 128], cstg[sl][:, k, c * 128:(c + 1) * 128], ident[:]),
                            CT[sl] + ["c"], [PS(b)])
                    if isBc:
                        dst = kbT[l][:, :, kt * 128:(kt + 1) * 128]
                        toks = [KBS(c, kt // 8) for c in range(4)]
                        if s_ == 0:
                            toks = toks + [("kb", l, c) for c in range(4)]
                    else:
                        dst = kaT[l][:, :, kt * 128:(kt + 1) * 128]
                        toks = [("ka", l, c) for c in range(4)]
                    evac(dst, psb[b][:, :].rearrange("p (c k) -> p c k", c=4), [], [PS(b)] + toks)
                else:
                    if isBc:
                        dst, tok = vbt[l][:, kt, :], ("vb", l, kt)
                    else:
                        dst, tok = vat[l][:, kt, :], ("va", l, kt)
                    vc[0] += 1
                    if vc[0] % 3 == 0:
                        P.add("pool", lambda: nc.gpsimd.tensor_copy(out=dst, in_=cstg[sl][:, k, :]), CT[sl], [tok])
                    elif vc[0] % 3 == 1:
                        P.add("dve", lambda: nc.vector.tensor_copy(out=dst, in_=cstg[sl][:, k, :]), CT[sl], [tok])
                    else:
                        P.add("act", lambda: nc.scalar.copy(out=dst, in_=cstg[sl][:, k, :]), CT[sl], [tok])

        for _ in range(8):
            piece_conv()

        sunits = [(h, m) for h in range(4) for m in range(2)]
        BO = [4, 6]
        BZ = [5, 7]

        for s in range(nseq):
            qc0 = s * 64
            for half in range(2):
                bo, bZ = BO[half], BZ[half]
                spend = {}

                def sb1(n):
                    h, m = sunits[n]
                    qv = qpad[:, 8 + 2 * h + m, qc0:qc0 + 64]
                    QT = ("B", 8 + 2 * h + m)
                    bs = rot()
                    for k8 in range(8):
                        j = half * 8 + k8
                        P.add("pe", lambda j=j, k8=k8: nc.tensor.matmul(
                            psb[bs][:, k8 * 64:(k8 + 1) * 64], kbT[l][:, h, j * 128:(j + 1) * 128], qv,
                            start=True, stop=True), [KBS(h, half), QT], [PS(bs)])
                    es = eslot()
                    P.add("act", lambda: nc.scalar.activation(
                        out=eT[:, es, :], in_=psb[bs][:, :], func=AF.Exp, scale=0.125), [], [PS(bs), ("e", es)])
                    es3 = None
                    if half == 1:
                        bs3 = rot()
                        P.add("pe", lambda: nc.tensor.matmul(
                            psb[bs3][0:64, 0:64], kbT[oth][:, h, qc0:qc0 + 64], qv, start=True, stop=True),
                            [("kb", oth, h), QT], [PS(bs3)])
                        es3 = eslot()
                        P.add("act", lambda: nc.scalar.activation(
                            out=eT[0:64, es3, 0:64], in_=psb[bs3][0:64, 0:64], func=AF.Exp, scale=0.125),
                            [], [PS(bs3), ("e", es3)])
                    spend[n] = (es, es3)

                def sb2(n):
                    h, m = sunits[n]
                    col = (2 * h + m) * 64
                    es, es3 = spend.pop(n)
                    nt = 9 if half == 1 else 8
                    for jj in range(nt):
                        if jj < 8:
                            j = half * 8 + jj
                            va_, e_, tokv = vbt[l][:, j, h * 128:(h + 1) * 128], eT[:, es, jj * 64:(jj + 1) * 64], ("vb", l, j)
                            on_ = onesb[:]
                            et = ("e", es)
                        else:
                            va_, e_, tokv = vbt[oth][0:64, s, h * 128:(h + 1) * 128], eT[0:64, es3, 0:64], ("vb", oth, s)
                            on_ = onesb[0:64, :]
                            et = ("e", es3)
                        P.add("pe", lambda jj=jj, va_=va_, e_=e_: nc.tensor.matmul(
                            psb[bo][:, col:col + 64], va_, e_, start=(jj == 0), stop=(jj == nt - 1)),
                            [tokv, et], [PS(bo)])
                        P.add("pe", lambda jj=jj, on_=on_, e_=e_: nc.tensor.matmul(
                            psb[bZ][:, col:col + 64], on_, e_, start=(jj == 0), stop=(jj == nt - 1)),
                            [et, "c"], [PS(bZ)])

                for n in range(len(sunits) + 1):
                    if n < len(sunits):
                        sb1(n)
                    if n >= 1:
                        sb2(n - 1)
                        piece_conv()
            P.add("dve", lambda: nc.vector.tensor_copy(out=tA[0][:, :], in_=psb[BZ[0]][:, :]), [], [PS(BZ[0]), ("t", 0, "A")])
            P.add("dve", lambda: nc.vector.tensor_tensor(out=tA[0][:, :], in0=psb[BZ[1]][:, :], in1=tA[0][:, :], op=ALU.add),
                  [], [PS(BZ[1]), ("t", 0, "A")])
            P.add("act", lambda: nc.scalar.activation(out=tA[0][:, :], in_=tA[0][:, :], func=AF.Ln), [], [("t", 0, "A")])
            P.add("act", lambda: nc.scalar.activation(out=tA[0][:, :], in_=tA[0][:, :], func=AF.Exp, scale=-1.0), [], [("t", 0, "A")])
            P.add("dve", lambda: nc.vector.tensor_copy(out=tB[0][:, :], in_=psb[BO[0]][:, :]), [], [PS(BO[0]), ("t", 0, "B")])
            P.add("dve", lambda: nc.vector.tensor_tensor(out=tB[0][:, :], in0=psb[BO[1]][:, :], in1=tB[0][:, :], op=ALU.add),
                  [], [PS(BO[1]), ("t", 0, "B")])
            P.add("dve", lambda: nc.vector.tensor_tensor(out=tB[0][:, :], in0=tB[0][:, :], in1=tA[0][:, :], op=ALU.mult),
                  [("t", 0, "A")], [("t", 0, "B")])
            onv = tB[0][:, :].rearrange("p (h m q) -> p h m q", h=4, m=2)
            obv = tC[0][:, 0:256].rearrange("p (h q) -> p h q", h=4)
            P.add("dve", lambda: nc.vector.scalar_tensor_tensor(
                out=obv, in0=onv[:, :, 1, :], scalar=nlam[:, l:l + 1], in1=onv[:, :, 0, :],
                op0=ALU.mult, op1=ALU.add), [("t", 0, "B"), "g"], [("t", 0, "C")])
            P.add("act", lambda: nc.scalar.activation(out=tq[0][:, 0:256], in_=tC[0][:, 0:256], func=AF.Square),
                  [("t", 0, "C")], [("t", 0, "q")])
            bn = rot()
            P.add("pe", lambda bn=bn: nc.tensor.matmul(psb[bn][:, 0:256], onesb[:], tq[0][:, 0:256], start=True, stop=True),
                  [("t", 0, "q"), "c"], [PS(bn)])
            P.add("act", lambda bn=bn: nc.scalar.activation(out=tC[1][:, 0:256], in_=psb[bn][:, 0:256], func=AF.Ln,
                                                            scale=1.0 / 128, bias=epsb[:, 0:1]), ["c"], [PS(bn), ("t", 1, "C")])
            P.add("act", lambda: nc.scalar.activation(out=tC[1][:, 0:256], in_=tC[1][:, 0:256], func=AF.Exp, scale=-0.5),
                  [], [("t", 1, "C")])
            P.add("dve", lambda: nc.vector.scalar_tensor_tensor(
                out=aT[:, 16:20, qc0:qc0 + 64], in0=obv, scalar=gsub[:, l:l + 1],
                in1=tC[1][:, 0:256].rearrange("p (h q) -> p h q", h=4), op0=ALU.mult, op1=ALU.mult),
                [("t", 0, "C"), ("t", 1, "C"), "g"], [("B", c) for c in range(16, 20)])
            bo, bZ = 6, 7
            for h in range(8):
                hp, hf = h // 2, h % 2
                bi0 = (l * 8 + h) * 2
                qv = qpad[:, h, qc0:qc0 + 64]
                col = h * 64
                b1 = rot()
                for t in range(4):
                    first = True
                    if t == 3:
                        P.add("pe", lambda b1=b1, t=t: nc.tensor.matmul(
                            psb[b1][:, t * 64:(t + 1) * 64], antib[:], btile[:, bi0, 0:64], start=True, stop=False),
                            ["bt", "c"], [PS(b1)])
                        first = False
                    P.add("pe", lambda b1=b1, t=t, first=first, qv=qv: nc.tensor.matmul(
                        psb[b1][:, t * 64:(t + 1) * 64], kaT[l][:, hp, t * 128:(t + 1) * 128], qv,
                        start=first, stop=True), [("ka", l, hp), ("B", h)], [PS(b1)])
                es = eslot()
                P.add("act", lambda b1=b1, es=es: nc.scalar.activation(
                    out=eT[:, es, 0:256], in_=psb[b1][:, 0:256], func=AF.Exp, scale=0.125,
                    bias=bconst[:, l * 8 + h:l * 8 + h + 1]), ["bc"], [PS(b1), ("e", es)])
                b2 = rot()
                P.add("pe", lambda b2=b2: nc.tensor.matmul(
                    psb[b2][0:64, 0:64], antib[64:128, 0:64], btile[64:128, bi0 + 1, 0:64], start=True, stop=False),
                    ["bt", "c"], [PS(b2)])
                P.add("pe", lambda b2=b2, qv=qv: nc.tensor.matmul(
                    psb[b2][0:64, 0:64], kaT[oth][:, hp, qc0:qc0 + 64], qv, start=False, stop=True),
                    [("ka", oth, hp), ("B", h)], [PS(b2)])
                es2 = eslot()
                P.add("act", lambda b2=b2, es2=es2: nc.scalar.activation(
                    out=eT[0:64, es2, 0:64], in_=psb[b2][0:64, 0:64], func=AF.Exp, scale=0.125,
                    bias=bconst[0:64, l * 8 + h:l * 8 + h + 1]), ["bc"], [PS(b2), ("e", es2)])
                for j in range(5):
                    if j < 4:
                        va_, e_, tokv, on_, et = vat[l][:, j, hp * 128:(hp + 1) * 128], eT[:, es, j * 64:(j + 1) * 64], ("va", l, j), onesb[:], ("e", es)
                    else:
                        va_, e_, tokv, on_, et = vat[oth][0:64, s, hp * 128:(hp + 1) * 128], eT[0:64, es2, 0:64], ("va", oth, s), onesb[0:64, :], ("e", es2)
                    P.add("pe", lambda j=j, va_=va_, e_=e_, col=col: nc.tensor.matmul(
                        psb[bo][:, col:col + 64], va_, e_, start=(j == 0), stop=(j == 4)), [tokv, et], [PS(bo)])
                    P.add("pe", lambda j=j, on_=on_, e_=e_, col=col: nc.tensor.matmul(
                        psb[bZ][:, col:col + 64], on_, e_, start=(j == 0), stop=(j == 4)), [et, "c"], [PS(bZ)])
                if h % 2 == 1:
                    piece_conv()
            for hf in range(2):
                lo, hi = hf * 64, hf * 64 + 64
                zv = psb[bZ][lo:hi, :].rearrange("p (hp f q) -> p hp f q", hp=4, f=2)[:, :, hf, :]
                ov = psb[bo][lo:hi, :].rearrange("p (hp f q) -> p hp f q", hp=4, f=2)[:, :, hf, :]
                tv = tA[1][lo:hi, 0:256].rearrange("p (hp q) -> p hp q", hp=4)
                P.add("act", lambda tv=tv, zv=zv: nc.scalar.activation(out=tv, in_=zv, func=AF.Ln), [], [PS(bZ), ("t", 1, "A")])
                P.add("act", lambda tv=tv: nc.scalar.activation(out=tv, in_=tv, func=AF.Exp, scale=-1.0), [], [("t", 1, "A")])
                P.add("dve", lambda tv=tv, ov=ov, lo=lo, hi=hi: nc.vector.tensor_tensor(
                    out=aT[lo:hi, 8:12, qc0:qc0 + 64], in0=ov, in1=tv, op=ALU.mult),
                    [("t", 1, "A")], [PS(bo)] + [("B", c) for c in range(8, 12)])

    groups = []
    for sq_ in range(NPS):
        for i in range(4):
            groups.append(dict(kind="p", seq=sq_, i=i, T=512))
    if NSS:
        groups.append(dict(kind="s", T=NSS * 64, nseq=NSS))
    def grp_rows(grp, dram_p, dram_s):
        if grp["kind"] == "p":
            return dram_p[grp["seq"], grp["i"] * 512:(grp["i"] + 1) * 512, :]
        return dram_s[0:grp["T"], :]

    prefetch_x(grp_rows(groups[0], xp, xs), groups[0]["T"])
    for gi_, grp in enumerate(groups):
        T = grp["T"]
        load_x(T)
        nxt = groups[gi_ + 1] if gi_ + 1 < len(groups) else None
        for l in range(NL):
            ffn(l, 1, T)
            mix(l, grp)
            pre = None
            if l == NL - 1 and nxt is not None:
                pre = (lambda nxt=nxt: prefetch_x(grp_rows(nxt, xp, xs), nxt["T"]))
            ffn(l, 2, T, pre=pre)
        store_y(grp_rows(grp, yp, ys), T)
    nw = P.emit_all()
    return nc, dict(n_ops=len(P.ops), n_wait=nw, dbg=dbg_outs)


def _constants():
    ident = np.eye(128, dtype=np.float32)
    anti = np.ascontiguousarray(ident[::-1])
    blk = np.zeros((128, 128), np.float32)
    blk[:64, :64] = 1.0
    blk[64:, 64:] = 1.0
    rperm = np.zeros((128, 128), np.float32)
    for m in range(128):
        if (m % 64) < 32:
            rperm[m + 32, m] = -1.0
        else:
            rperm[m - 32, m] = 1.0
    inv = (10000.0 ** (-np.arange(32, dtype=np.float32) * 2.0 / 64)).astype(np.float32)
    fidx = np.arange(128) % 32

    def tab(pos):
        ang = pos.astype(np.float32)[None, :] * inv[fidx][:, None]
        return np.cos(ang).astype(np.float32), np.sin(ang).astype(np.float32)
    cosp, sinp = tab(np.arange(SEQ))
    coss, sins = tab(np.tile(PAST + np.arange(64), 4))
    return dict(c_ident=ident, c_anti=anti, c_blk=blk, c_rperm=rperm, c_cosp=cosp, c_sinp=sinp,
                c_coss=coss, c_sins=sins)


_CACHE = {}


def kernel(x_prompt, x_sample, cache_a_k, cache_a_v, cache_b_k, cache_b_v,
           g_ffn1, w1_gate, w1_up, w1_down, g_mix, w_in, g_qa, g_ka, g_qb, g_kb,
           rel_bias, lam_q1, lam_k1, lam_q2, lam_k2, g_sub, w_out,
           g_ffn2, w2_gate, w2_up, w2_down):
    f = lambda a: np.ascontiguousarray(np.asarray(a, dtype=np.float32))
    NL = 2
    if "nc" not in _CACHE:
        _CACHE["nc"] = build()[0]
    nc = _CACHE["nc"]
    consts = _constants()
    shared = dict(w1_gate=f(w1_gate), w1_up=f(w1_up), w1_down=f(w1_down), w_in=f(w_in), w_out=f(w_out),
                  w2_gate=f(w2_gate), w2_up=f(w2_up), w2_down=f(w2_down),
                  g_ffn1=f(g_ffn1), g_mix=f(g_mix), g_ffn2=f(g_ffn2), g_qa=f(g_qa), g_ka=f(g_ka),
                  g_qb=f(g_qb), g_kb=f(g_kb), g_sub=f(g_sub), rel_bias=f(rel_bias),
                  lam_q1=f(lam_q1), lam_k1=f(lam_k1), lam_q2=f(lam_q2), lam_k2=f(lam_k2))
    shared.update(consts)
    xpn, xsn = f(x_prompt), f(x_sample)
    cak, cav = f(cache_a_k).reshape(NL, 32, 512, 512), f(cache_a_v).reshape(NL, 32, 512, 512)
    cbk, cbv = f(cache_b_k).reshape(NL, 32, PAST, 512), f(cache_b_v).reshape(NL, 32, PAST, 512)
    in_maps = []
    for c in range(NCORES):
        m = dict(shared)
        m["xp"] = xpn[2 * c:2 * c + 2]
        m["xs"] = xsn[4 * c:4 * c + 4].reshape(256, D_MODEL)
        m["cak"] = np.ascontiguousarray(cak[:, 4 * c:4 * c + 4])
        m["cav"] = np.ascontiguousarray(cav[:, 4 * c:4 * c + 4])
        m["cbk"] = np.ascontiguousarray(cbk[:, 4 * c:4 * c + 4])
        m["cbv"] = np.ascontiguousarray(cbv[:, 4 * c:4 * c + 4])
        in_maps.append(m)
    res = run_bass_kernel_spmd(nc, in_maps, core_ids=list(range(NCORES)))
    R = res.results
    yp = np.concatenate([r["yp"] for r in R], axis=0)
    ys = np.concatenate([r["ys"].reshape(4, 64, D_MODEL) for r in R], axis=0)
    cat1 = lambda k: np.concatenate([r[k] for r in R], axis=1)
    pak = cat1("pak").reshape(NL, 16, 512, 8, 64)
    pav = cat1("pav").reshape(NL, 16, 512, 8, 64)
    pbk = cat1("pbk").reshape(NL, 16, SEQ, 4, 128)
    pbv = cat1("pbv").reshape(NL, 16, SEQ, 4, 128)
    sak = cat1("sak").reshape(NL, 32, 64, 8, 64)
    sav = cat1("sav").reshape(NL, 32, 64, 8, 64)
    sbk = cat1("sbk").reshape(NL, 32, 64, 4, 128)
    sbv = cat1("sbv").reshape(NL, 32, 64, 4, 128)
    return (yp.astype(np.float32), ys.astype(np.float32), pak, pav, pbk, pbv, sak, sav, sbk, sbv)
```

```python
import math
import types
import numpy as np
import concourse.bass as bass
import concourse.mybir as mybir
from concourse.bass_utils import run_bass_kernel_spmd

F32 = mybir.dt.float32
BF16 = mybir.dt.bfloat16
AF = mybir.ActivationFunctionType
ALU = mybir.AluOpType

D_MODEL = 1024
D_FF = 2816
NFC = 22
SEQ = 2048
PAST = 2048
NCORES = 8
EPS = 1e-6
WT = 2048
NSLOT = 4


def _freeze(fn):
    if fn.__closure__ is None:
        return fn
    cells = []
    for c in fn.__closure__:
        try:
            cells.append(types.CellType(c.cell_contents))
        except ValueError:
            cells.append(c)
    return types.FunctionType(fn.__code__, fn.__globals__, fn.__name__, fn.__defaults__, tuple(cells))


class Prog:
    def __init__(self, nc, same_engine_sync=True):
        self.nc = nc
        self.eng = {"pe": nc.tensor, "act": nc.scalar, "dve": nc.vector,
                    "pool": nc.gpsimd, "sp": nc.sync}
        self.ops = []
        self.last_w = {}
        self.readers = {}
        self.same_engine_sync = same_engine_sync
        self.dma_fill = {}

    def add(self, eng, emit, reads=(), writes=(), dma_key=None):
        idx = len(self.ops)
        deps = set()
        for r in reads:
            lw = self.last_w.get(r)
            if lw is not None:
                deps.add(lw)
            self.readers.setdefault(r, []).append(idx)
        for w in writes:
            lw = self.last_w.get(w)
            if lw is not None:
                deps.add(lw)
            rs = self.readers.get(w)
            if rs:
                deps.update(rs)
            self.last_w[w] = idx
            self.readers[w] = []
        deps.discard(idx)
        fill = None
        if dma_key is not None:
            fill = self.dma_fill.get(dma_key, 0) + 1
            self.dma_fill[dma_key] = fill
        self.ops.append([eng, _freeze(emit), deps, dma_key, fill, False, 0])
        return idx

    def emit_all(self):
        nc = self.nc
        ops = self.ops
        pruned = []
        for j, (eng, emit, deps, key, fill, _, _) in enumerate(ops):
            best = {}
            for i in deps:
                e_i, _, _, k_i, _, _, _ = ops[i]
                sid = ("dma", k_i) if k_i is not None else e_i
                if k_i is None and e_i == eng and key is None:
                    if eng == "pe" or not self.same_engine_sync:
                        continue
                if sid not in best or best[sid] < i:
                    best[sid] = i
            pruned.append(sorted(best.values()))
            for i in best.values():
                ops[i][5] = True
        cnt = {}
        for op in ops:
            if op[3] is None and op[5]:
                cnt[op[0]] = cnt.get(op[0], 0) + 1
                op[6] = cnt[op[0]]
        esem = {e: nc.alloc_semaphore("s_" + e) for e in ("pe", "act", "dve", "pool")}
        dsem = {}
        for n, k in enumerate(self.dma_fill):
            dsem[k] = nc.alloc_semaphore("d%d" % n)
        know = {e: {} for e in self.eng}
        snap = {}
        n_wait = 0
        for j, (eng, emit, deps, key, fill, mark, c) in enumerate(ops):
            E = self.eng[eng]
            K = know[eng]
            for i in pruned[j]:
                e_i, _, _, k_i, f_i, _, c_i = ops[i]
                if k_i is not None:
                    sem, val, sk = dsem[k_i], 16 * f_i, ("d", k_i)
                else:
                    sem, val, sk = esem[e_i], c_i, e_i
                if K.get(sk, 0) >= val:
                    continue
                E.wait_ge(sem, val)
                n_wait += 1
                K[sk] = val
                si = snap.get(i)
                if si:
                    for k2, v2 in si.items():
                        if K.get(k2, 0) < v2:
                            K[k2] = v2
            if key is not None or mark:
                snap[j] = dict(K)
            inst = emit()
            if key is not None:
                inst.then_inc(dsem[key], 16)
            elif mark:
                inst.then_inc(esem[eng], 1)
        for k, f in self.dma_fill.items():
            if know["sp"].get(("d", k), 0) < 16 * f:
                nc.sync.wait_ge(dsem[k], 16 * f)
        return n_wait


def _weight_tiles(NL):
    cat = {}
    for l in range(NL):
        for f in (1, 2):
            for ft in range(11):
                cat[(l, "g%d" % f, ft)] = len(cat)
                cat[(l, "u%d" % f, ft)] = len(cat)
            for half in range(2):
                for t6 in range(6):
                    cat[(l, "d%d" % f, half * 6 + t6)] = len(cat)
            if f == 1:
                for ct in range(12):
                    cat[(l, "in", ct)] = len(cat)
                for ot in range(4):
                    cat[(l, "out", ot)] = len(cat)
    return cat


def build(NPS=2, NSS=4, NL=2, same_engine_sync=True, debug=None):
    nc = bass.Bass("TRN2", target_bir_lowering=False)
    P = Prog(nc, same_engine_sync)
    dbg_outs = {}

    def din(name, shape, dt=F32):
        return nc.dram_tensor(name, list(shape), dt, kind="ExternalInput").ap()

    def dout(name, shape, dt=F32):
        return nc.dram_tensor(name, list(shape), dt, kind="ExternalOutput").ap()

    xp = din("xp", [max(NPS, 1), SEQ, D_MODEL])
    xs = din("xs", [max(NSS, 1) * 64, D_MODEL])
    cak = din("cak", [NL, max(NSS, 1), 512, 512])
    cav = din("cav", [NL, max(NSS, 1), 512, 512])
    cbk = din("cbk", [NL, max(NSS, 1), PAST, 512])
    cbv = din("cbv", [NL, max(NSS, 1), PAST, 512])
    W = {}
    for f in (1, 2):
        W["g%d" % f] = din("w%d_gate" % f, [NL, D_MODEL, D_FF])
        W["u%d" % f] = din("w%d_up" % f, [NL, D_MODEL, D_FF])
        W["d%d" % f] = din("w%d_down" % f, [NL, D_FF, D_MODEL])
    W["in"] = din("w_in", [NL, D_MODEL, 3072])
    W["out"] = din("w_out", [NL, D_MODEL, D_MODEL])
    g_ffn1 = din("g_ffn1", [NL, D_MODEL])
    g_mix = din("g_mix", [NL, D_MODEL])
    g_ffn2 = din("g_ffn2", [NL, D_MODEL])
    g_qa = din("g_qa", [NL, 64])
    g_ka = din("g_ka", [NL, 64])
    g_qb = din("g_qb", [NL, 64])
    g_kb = din("g_kb", [NL, 64])
    g_sub = din("g_sub", [NL, 128])
    rel_bias = din("rel_bias", [NL, 8, 257])
    lam_in = {k: din(k, [NL, 64]) for k in ("lam_q1", "lam_k1", "lam_q2", "lam_k2")}
    c_ident = din("c_ident", [128, 128])
    c_anti = din("c_anti", [128, 128])
    c_blk = din("c_blk", [128, 128])
    c_rperm = din("c_rperm", [128, 128])
    c_cosp = din("c_cosp", [128, SEQ])
    c_sinp = din("c_sinp", [128, SEQ])
    c_coss = din("c_coss", [128, 256])
    c_sins = din("c_sins", [128, 256])

    yp = dout("yp", [max(NPS, 1), SEQ, D_MODEL])
    ys = dout("ys", [max(NSS, 1) * 64, D_MODEL])
    pak = dout("pak", [NL, max(NPS, 1), 512, 512])
    pav = dout("pav", [NL, max(NPS, 1), 512, 512])
    pbk = dout("pbk", [NL, max(NPS, 1), SEQ, 512])
    pbv = dout("pbv", [NL, max(NPS, 1), SEQ, 512])
    sak = dout("sak", [NL, max(NSS, 1), 64, 512])
    sav = dout("sav", [NL, max(NSS, 1), 64, 512])
    sbk = dout("sbk", [NL, max(NSS, 1), 64, 512])
    sbv = dout("sbv", [NL, max(NSS, 1), 64, 512])

    cat = _weight_tiles(NL)
    wsc = nc.dram_tensor("wsc", [len(cat), 128, WT], BF16, kind="Internal").ap()
    tpad = nc.dram_tensor("tpad", [NL * 8, 384], F32, kind="Internal").ap()

    A = nc.alloc_sbuf_tensor
    ident = A("ident", [128, 128], F32)
    identb = A("identb", [128, 128], BF16)
    antib = A("antib", [128, 128], BF16)
    onesb = A("onesb", [128, 128], BF16)
    blkb = A("blkb", [128, 128], BF16)
    rpermb = A("rpermb", [128, 128], BF16)
    maskB = A("maskB", [128, 64], BF16)
    maskA0 = A("maskA0", [128, 128], BF16)
    epsb = A("epsb", [128, 1], F32)
    lneps = A("lneps", [128, 1], F32)
    gx = A("gx", [128, 3 * NL * 8], F32)
    gh = A("gh", [128, 4 * NL], F32)
    gsub = A("gsub", [128, NL], F32)
    nlam = A("nlam", [128, NL], F32)
    bconst = A("bconst", [128, NL * 8], F32)
    btile = A("btile", [128, NL * 8 * 2, 128], BF16)
    xT = A("xT", [128, 8, 512], F32)
    hT = A("hT", [128, 8, 512], BF16)
    BIG = A("BIG", [128, NFC * 512], BF16)
    wring = A("wring", [128, NSLOT, WT], BF16)
    kbT = [A("kbT%d" % l, [128, 4, 2048], BF16) for l in range(2)]
    vbt = [A("vb%d" % l, [128, 16, 512], BF16) for l in range(2)]
    kaT = [A("kaT%d" % l, [128, 4, 1024], BF16) for l in range(2)]
    vat = [A("va%d" % l, [128, 8, 512], BF16) for l in range(2)]
    eT = A("eT", [128, 6, 512], BF16)
    tA = [A("tA%d" % i, [128, 512], F32) for i in range(2)]
    tB = [A("tB%d" % i, [128, 512], F32) for i in range(2)]
    tC = [A("tC%d" % i, [128, 512], F32) for i in range(2)]
    tz = [A("tz%d" % i, [128, 512], BF16) for i in range(2)]
    tq = [A("tq%d" % i, [128, 512], BF16) for i in range(2)]
    cosT = A("cosT", [128, 512], F32)
    sinT = A("sinT", [128, 512], F32)
    stg = A("stg", [128, 4, 512], F32)
    lamt = A("lamt", [128, 4, 64], F32)
    lamr = A("lamr", [128, 4], F32)
    rstdT = A("rstdT", [128, 512], F32)
    rtok = A("rtok", [128, 4], F32)
    epsv = lamt[:].rearrange("p a b -> p (a b)").bitcast(BF16)[:, 0:512]

    aT = BIG[:].rearrange("p (c t) -> p c t", c=NFC)
    xstg = BIG[:, 0:8192].bitcast(F32).rearrange("p (a b) -> p a b", a=4)
    sq8 = BIG[:, 0:4096].rearrange("p (c t) -> p c t", c=8)
    qpad = BIG[:, 0:8192].rearrange("p (c t) -> p c t", c=16)

    psb = [nc.alloc_psum_tensor("psb%d" % b, [128, 512], F32) for b in range(8)]
    rot_state = [0]

    def rot():
        b = rot_state[0] % 4
        rot_state[0] += 1
        return b

    def PS(b):
        return ("ps", b)

    def Bk(lo, hi):
        return [("B", c) for c in range(lo, hi)]

    ncd = nc.allow_non_contiguous_dma

    def dma(q, out, in_, reads, writes, key, nonc=False):
        E = nc.sync if q == "sp" else (nc.gpsimd if q == "pool" else nc.scalar)

        def emit():
            if nonc:
                with ncd(reason="tiny setup transfer"):
                    return E.dma_start(out=out, in_=in_)
            return E.dma_start(out=out, in_=in_)
        P.add(q, emit, reads, writes, dma_key=key)

    ukey = [0]

    def once_key():
        ukey[0] += 1
        return ("once", ukey[0] % 8)

    def dbg(name, ap, shape, reads):
        if debug is None or name not in debug:
            return
        o = dout("dbg_" + name, shape, ap.dtype)
        dbg_outs[name] = o
        dma("sp", o, ap, reads, [], ("dbg", name))

    def load_cast(dst_b, src):
        dma("sp", tA[0][:, 0:128], src, [], [("t", 0, "A")], ("once", 0))
        P.add("dve", lambda: nc.vector.tensor_copy(out=dst_b[:], in_=tA[0][:, 0:128]),
              [("t", 0, "A")], ["c"])

    dma("sp", ident[:], c_ident[:, :], [], ["c"], ("once", 1))
    load_cast(identb, c_ident[:, :])
    load_cast(antib, c_anti[:, :])
    load_cast(blkb, c_blk[:, :])
    load_cast(rpermb, c_rperm[:, :])
    P.add("pool", lambda: nc.gpsimd.memset(onesb[:], 1.0), [], ["c"])
    P.add("pool", lambda: nc.gpsimd.memset(epsb[:], EPS), [], ["c"])
    P.add("pool", lambda: nc.gpsimd.memset(lneps[:], float(math.log(EPS))), [], ["c"])
    P.add("pool", lambda: nc.gpsimd.memset(maskB[:], 0.0), [], ["c"])
    P.add("pool", lambda: nc.gpsimd.memset(maskB[64:128, :], -30000.0), [], ["c"])
    P.add("pool", lambda: nc.gpsimd.memset(maskA0[:], 0.0), [], ["c"])
    P.add("pool", lambda: nc.gpsimd.memset(maskA0[0:64, 64:128], -30000.0), [], ["c"])
    for wi, gsrc in enumerate((g_ffn1, g_mix, g_ffn2)):
        for l in range(NL):
            o = (wi * NL + l) * 8
            dma("sp", gx[:, o:o + 8], gsrc[l].rearrange("(c p) -> p c", p=128), [], ["g"], ("once", 2), nonc=True)
    for gi, gsrc in enumerate((g_qa, g_ka, g_qb, g_kb)):
        for l in range(NL):
            for hf in range(2):
                dma("sp", gh[hf * 64:(hf + 1) * 64, gi * NL + l:gi * NL + l + 1],
                    gsrc[l].rearrange("(p o) -> p o", o=1), [], ["g"], ("once", 3), nonc=True)
    for l in range(NL):
        lam_init = 0.8 - 0.6 * math.exp(-0.3 * l)
        dma("sp", tA[1][:, l:l + 1], g_sub[l].rearrange("(p o) -> p o", o=1), [], [("t", 1, "A")], ("once", 4), nonc=True)
        P.add("dve", lambda l=l, li=lam_init: nc.vector.tensor_scalar(
            out=gsub[:, l:l + 1], in0=tA[1][:, l:l + 1], scalar1=float(1.0 - li), scalar2=None, op0=ALU.mult),
            [("t", 1, "A")], ["g"])
        for k, nm in enumerate(("lam_q1", "lam_k1", "lam_q2", "lam_k2")):
            dma("sp", lamt[:, k, :], lam_in[nm][l:l + 1, :].partition_broadcast(128), [], ["lamt"], ("once", 5))
        for k in range(2):
            P.add("dve", lambda k=k: nc.vector.tensor_tensor(out=lamt[:, 2 * k, :], in0=lamt[:, 2 * k, :],
                                                             in1=lamt[:, 2 * k + 1, :], op=ALU.mult),
                  ["lamt"], ["lamt"])
            P.add("dve", lambda k=k: nc.vector.reduce_sum(out=lamr[:, k:k + 1], in_=lamt[:, 2 * k, :],
                                                          axis=mybir.AxisListType.X), ["lamt"], ["lamr"])
        P.add("act", lambda: nc.scalar.activation(out=lamr[:, 2:4], in_=lamr[:, 0:2], func=AF.Exp), ["lamr"], ["lamr"])
        P.add("dve", lambda l=l, li=lam_init: nc.vector.scalar_tensor_tensor(
            out=nlam[:, l:l + 1], in0=lamr[:, 3:4], scalar=float(-li), in1=lamr[:, 2:3], op0=ALU.add, op1=ALU.subtract),
            ["lamr"], ["g"])

    def setup_bias(l):
        dma("sp", bconst[:, l * 8:(l + 1) * 8],
            bass.AP(tensor=rel_bias.tensor, offset=l * 8 * 257 + 256, ap=[[0, 128], [257, 8]]),
            [], ["bc"], ("once", 6), nonc=True)
        dma("sp", tB[0][0:8, 0:257], rel_bias[l], [], [("t", 0, "B")], ("once", 7))
        P.add("dve", lambda: nc.vector.tensor_copy(out=tB[0][0:8, 257:384],
                                                   in_=tB[0][0:8, 256:257].to_broadcast([8, 127])),
              [("t", 0, "B")], [("t", 0, "B")])
        dma("sp", tpad[l * 8:(l + 1) * 8, :], tB[0][0:8, 0:384], [("t", 0, "B")], ["tpad"], ("tpadw", l))
        for h in range(8):
            for di, Dv in enumerate((128, 0)):
                s = (h * 2 + di) % 2
                src = bass.AP(tensor=tpad.tensor, offset=(l * 8 + h) * 384 + Dv + 1, ap=[[1, 128], [1, 128]])
                dma("sp", tC[s][:, 0:128], src, ["tpad"], [("t", s, "C")], ("btl", s))
                bi = (l * 8 + h) * 2 + di
                P.add("dve", lambda s=s, bi=bi, l=l, h=h: nc.vector.tensor_scalar(
                    out=btile[:, bi, :], in0=tC[s][:, 0:128], scalar1=bconst[:, l * 8 + h:l * 8 + h + 1],
                    scalar2=8.0, op0=ALU.subtract, op1=ALU.mult), [("t", s, "C"), "bc"], ["bt"])
                if di == 1:
                    P.add("dve", lambda bi=bi: nc.vector.memset(btile[0:64, bi, 0:64], -30000.0), [], ["bt"])

    cat_list = list(cat.items())
    cv_done = [0]
    CV_LEAD = 32
    CV_KEYS = 8

    def convert_upto(n_hi):
        while cv_done[0] < min(n_hi, len(cat_list)):
            n = cv_done[0]
            cv_done[0] += 1
            (l, name, idx), tid = cat_list[n]
            if name[0] in "gu" or name in ("in", "out"):
                src = W[name][l][:, idx * 256:(idx + 1) * 256].rearrange("(kc p) f -> p kc f", p=128)
                dst = wsc[tid].rearrange("p (kc f) -> p kc f", kc=8)
            else:
                half, t6 = idx // 6, idx % 6
                nfc = min(4, NFC - 4 * t6)
                src = W[name][l][t6 * 512:t6 * 512 + nfc * 128, half * 512:(half + 1) * 512].rearrange(
                    "(fc p) d -> p fc d", p=128)
                dst = wsc[tid][:, 0:nfc * 512].rearrange("p (fc d) -> p fc d", fc=nfc)
            dma("pool", dst, src, [], [("ws", tid), ("cvk", n % CV_KEYS)], ("cv", n % CV_KEYS))

    wcount = [0]

    def wload(l, name, idx, nel=WT):
        tid = cat[(l, name, idx)]
        convert_upto(tid + 1 + CV_LEAD)
        s = wcount[0] % NSLOT
        wcount[0] += 1
        dma("sp", wring[:, s, 0:nel], wsc[tid][:, 0:nel], [("ws", tid)], [("w", s)], ("w", s))
        return s

    nstate = {"bank": None, "pend": []}
    sqbuf = [(tq[0], ("t", 0, "q")), (tq[1], ("t", 1, "q")), (tz[0], ("t", 0, "z")), (tz[1], ("t", 1, "z"))]

    def norm_flush():
        for (b, dc, T) in nstate["pend"]:
            sb, stok = sqbuf[dc % 4]
            P.add("pe", lambda: nc.tensor.matmul(psb[b][:, 0:T], onesb[:], sb[:, 0:T],
                                                 start=(dc == 0), stop=(dc == 7)), [stok, "c"], [PS(b)])
        nstate["pend"] = []

    def norm_feed(dc, T, gcol, bank=None):
        if gcol is None:
            return
        if dc == 0:
            nstate["bank"] = rot() if bank is None else bank
        b = nstate["bank"]
        sb, stok = sqbuf[dc % 4]
        P.add("act", lambda: nc.scalar.activation(out=sb[:, 0:T], in_=xT[:, dc, 0:T], func=AF.Square),
              [("xT", dc)], [stok])
        P.add("dve", lambda: nc.vector.tensor_scalar(
            out=hT[:, dc, 0:T], in0=xT[:, dc, 0:T], scalar1=gx[:, gcol + dc:gcol + dc + 1], scalar2=None,
            op0=ALU.mult), [("xT", dc), "g"], [("hT", dc)])
        nstate["pend"].append((b, dc, T))

    def norm_finish(T, for_mix=False, blocks=None):
        norm_flush()
        b = nstate["bank"]
        P.add("act", lambda: nc.scalar.activation(out=rstdT[:, 0:T], in_=psb[b][:, 0:T], func=AF.Ln,
                                                  scale=1.0 / D_MODEL, bias=epsb[:, 0:1]), ["c"], [PS(b), "rstd"])
        if for_mix:
            P.add("act", lambda: nc.scalar.activation(out=epsv[:, 0:T], in_=rstdT[:, 0:T], func=AF.Exp,
                                                      bias=lneps[:, 0:1]), ["rstd", "c"], ["epsv", "lamt"])
        P.add("act", lambda: nc.scalar.activation(out=rstdT[:, 0:T], in_=rstdT[:, 0:T], func=AF.Exp, scale=-0.5),
              [], ["rstd"])
        if for_mix:
            nstate["rtok"] = (T, blocks)

    def rtok_part():
        T, blocks = nstate["rtok"]
        if True:
            bt = rot8()
            for k, (c0, n) in enumerate(blocks):
                P.add("pe", lambda k=k, c0=c0, n=n: nc.tensor.transpose(
                    psb[bt][0:n, k * 128:(k + 1) * 128], rstdT[:, c0:c0 + n], ident[:]), ["rstd", "c"], [PS(bt)])
            nb = len(blocks)
            n0 = blocks[0][1]
            P.add("dve", lambda: nc.vector.tensor_copy(
                out=rtok[0:n0, 0:nb], in_=psb[bt][0:n0, 0:nb * 128].rearrange("p (k f) -> p k f", k=nb)[:, :, 0]),
                [], [PS(bt), "rtok"])

    def ffn(l, f, T, pre=None):
        if f == 1:
            gnext = (1 * NL + l) * 8
        else:
            gnext = (0 * NL + l + 1) * 8 if l + 1 < NL else None
        HT = [("hT", c) for c in range(8)]
        for ft in range(11):
            sg = wload(l, "g%d" % f, ft)
            su = wload(l, "u%d" % f, ft)
            wg = wring[:, sg, :].rearrange("p (kc f) -> p kc f", kc=8)
            wu = wring[:, su, :].rearrange("p (kc f) -> p kc f", kc=8)
            for fl in range(2):
                fc = 2 * ft + fl
                bg, bu = rot(), rot()
                for kc in range(8):
                    P.add("pe", lambda kc=kc, bg=bg, wg=wg, fl=fl: nc.tensor.matmul(
                        psb[bg][:, 0:T], wg[:, kc, fl * 128:(fl + 1) * 128], hT[:, kc, 0:T],
                        start=(kc == 0), stop=(kc == 7)), [("w", sg), ("hT", kc)], [PS(bg)])
                for kc in range(8):
                    P.add("pe", lambda kc=kc, bu=bu, wu=wu, fl=fl: nc.tensor.matmul(
                        psb[bu][:, 0:T], wu[:, kc, fl * 128:(fl + 1) * 128], hT[:, kc, 0:T],
                        start=(kc == 0), stop=(kc == 7)), [("w", su), ("hT", kc)], [PS(bu)])
                ts = fc % 2
                if fc == 0:
                    norm_finish(T)
                P.add("dve", lambda bg=bg, ts=ts: nc.vector.tensor_tensor(
                    out=tA[ts][:, 0:T], in0=psb[bg][:, 0:T], in1=rstdT[:, 0:T], op=ALU.mult),
                    ["rstd"], [PS(bg), ("t", ts, "A")])
                P.add("act", lambda ts=ts: nc.scalar.activation(out=tA[ts][:, 0:T], in_=tA[ts][:, 0:T],
                                                                func=AF.Silu), [], [("t", ts, "A")])
                P.add("dve", lambda bu=bu, ts=ts, fc=fc: nc.vector.tensor_tensor(
                    out=aT[:, fc, 0:T], in0=psb[bu][:, 0:T], in1=tA[ts][:, 0:T], op=ALU.mult),
                    [("t", ts, "A")], [PS(bu), ("B", fc)])
        if pre is not None:
            pre()
        for half in range(2):
            for t6 in range(6):
                nfc = min(4, NFC - 4 * t6)
                sd = wload(l, "d%d" % f, half * 6 + t6, nel=nfc * 512)
                wd = wring[:, sd, 0:nfc * 512].rearrange("p (fc d) -> p fc d", fc=nfc)
                if half == 1 and t6 == 2:
                    norm_flush()
                for fl in range(nfc):
                    fc = 4 * t6 + fl
                    for dcl in range(4):
                        ab = (4 + dcl) if half == 0 else dcl
                        P.add("pe", lambda fl=fl, fc=fc, dcl=dcl, wd=wd, ab=ab: nc.tensor.matmul(
                            psb[ab][:, 0:T], wd[:, fl, dcl * 128:(dcl + 1) * 128], aT[:, fc, 0:T],
                            start=(fc == 0), stop=(fc == NFC - 1)), [("w", sd), ("B", fc)], [PS(ab)])
            for dcl in range(4):
                dc = half * 4 + dcl
                tb_ = tB[dcl % 2]
                ab = (4 + dcl) if half == 0 else dcl
                P.add("dve", lambda dcl=dcl, tb_=tb_, ab=ab: nc.vector.tensor_tensor(
                    out=tb_[:, 0:T], in0=psb[ab][:, 0:T], in1=rstdT[:, 0:T], op=ALU.mult),
                    ["rstd"], [PS(ab), ("t", dcl % 2, "B")])
                P.add("dve", lambda dc=dc, tb_=tb_: nc.vector.scalar_tensor_tensor(
                    out=xT[:, dc, 0:T], in0=tb_[:, 0:T], scalar=0.5, in1=xT[:, dc, 0:T],
                    op0=ALU.mult, op1=ALU.add), [("t", dcl % 2, "B")], [("xT", dc)])
            for dcl in range(4):
                norm_feed(half * 4 + dcl, T, gnext, bank=4)

    stgx = stg[:].rearrange("p a b -> p (a b)").rearrange("p (t d) -> p t d", t=2)
    XPIECE = {2: [(tA[0], ("t", 0, "A")), (tA[1], ("t", 1, "A"))],
              3: [(tC[0], ("t", 0, "C")), (tC[1], ("t", 1, "C"))]}

    def prefetch_x(src_rows, T):
        ntt = T // 128
        dma("sp", stgx[:, 0:2, :], src_rows[0:256, :].rearrange("(t p) d -> p t d", p=128), [],
            [("stg", k) for k in range(4)], "xin0")
        for tt in range(2, ntt):
            for hf in range(2):
                buf, tok = XPIECE[tt][hf]
                dma("sp", buf[:, :], src_rows[tt * 128:(tt + 1) * 128, hf * 512:(hf + 1) * 512], [], [tok],
                    "xin%d" % (1 + (tt - 2) * 2 + hf))

    def load_x(T):
        ntt = T // 128
        for dc in range(8):
            b = rot()
            for tt in range(ntt):
                if tt < 2:
                    src, toks = stgx[:, tt, dc * 128:(dc + 1) * 128], [("stg", 2 * tt), ("stg", 2 * tt + 1)]
                else:
                    buf, tok = XPIECE[tt][dc // 4]
                    src, toks = buf[:, (dc % 4) * 128:(dc % 4 + 1) * 128], [tok]
                P.add("pe", lambda tt=tt, src=src: nc.tensor.transpose(
                    psb[b][:, tt * 128:(tt + 1) * 128], src, ident[:]), toks + ["c"], [PS(b)])
            norm_flush()
            if dc % 2 == 0:
                P.add("act", lambda: nc.scalar.copy(out=xT[:, dc, 0:T], in_=psb[b][:, 0:T]),
                      [], [PS(b), ("xT", dc)])
            else:
                P.add("dve", lambda: nc.vector.tensor_copy(out=xT[:, dc, 0:T], in_=psb[b][:, 0:T]),
                      [], [PS(b), ("xT", dc)])
            norm_feed(dc, T, 0, bank=7)

    def store_y(dst_rows, T):
        ntt = T // 128
        for tt in range(ntt):
            for hf in range(2):
                b = rot()
                for d4 in range(4):
                    dc = hf * 4 + d4
                    P.add("pe", lambda dc=dc, d4=d4, tt=tt, b=b: nc.tensor.transpose(
                        psb[b][:, d4 * 128:(d4 + 1) * 128], xT[:, dc, tt * 128:(tt + 1) * 128], ident[:]),
                        [("xT", dc), "c"], [PS(b)])
                if hf == 0:
                    P.add("act", lambda tt=tt, hf=hf, b=b: nc.scalar.copy(
                        out=xstg[:, tt, hf * 512:(hf + 1) * 512], in_=psb[b][:, :]), [], [PS(b)] + Bk(0, 16))
                else:
                    P.add("dve", lambda tt=tt, hf=hf, b=b: nc.vector.tensor_copy(
                        out=xstg[:, tt, hf * 512:(hf + 1) * 512], in_=psb[b][:, :]), [], [PS(b)] + Bk(0, 16))
        dma("pool", dst_rows.rearrange("(t p) d -> p t d", p=128), xstg[:, 0:ntt, :], Bk(0, 16), [], "yst")

    def store_rows(src_f32, ntok_p, dst, reads, slot):
        dma("pool", dst, src_f32, reads, [], ("stg", slot))

    rot8_state = [0]

    def rot8():
        b = rot8_state[0] % 8
        rot8_state[0] += 1
        return b

    def mix(l, grp):
        T = grp["T"]
        ntt = T // 128
        isP = grp["kind"] == "p"
        if isP:
            vblocks = [(tb * 128, 128) for tb in range(ntt)]
        else:
            vblocks = [(tb * 64, 64) for tb in range(grp["nseq"])]
        norm_finish(T, for_mix=True, blocks=vblocks)
        if isP:
            i = grp["i"]
            sq_ = grp["seq"]
            tok0 = i * 512
            dma("sp", cosT[:, 0:T], c_cosp[:, i * 512:(i + 1) * 512], [], ["cs"], "cs")
            dma("sp", sinT[:, 0:T], c_sinp[:, i * 512:(i + 1) * 512], [], ["cs"], "cs")
        else:
            dma("sp", cosT[:, 0:T], c_coss[:, 0:T], [], ["cs"], "cs")
            dma("sp", sinT[:, 0:T], c_sins[:, 0:T], [], ["cs"], "cs")
        P.add("pool", lambda: nc.gpsimd.memset(BIG[:, 0:8192], 0.0), [], Bk(0, 16))
        oth = 1 - l
        wtl = {}

        def get_w(gidx):
            if gidx not in wtl:
                s0 = wload(l, "in", 2 * gidx)
                s1 = wload(l, "in", 2 * gidx + 1)
                wtl[gidx] = ([wring[:, s0, :].rearrange("p (kc f) -> p kc f", kc=8),
                              wring[:, s1, :].rearrange("p (kc f) -> p kc f", kc=8)], [s0, s1])
            return wtl[gidx]

        GIDX = {"qa": 0, "ka": 1, "va": 2, "qb": 3, "kb": 4, "vb": 5}
        jobs = []

        def v_job(kind, tb):
            isBv = kind == "vb"
            bsz = 128 if isP else 64

            def Pst():
                wv, ws = get_w(GIDX[kind])
                b = rot8()
                for half in range(2):
                    for kc in range(8):
                        P.add("pe", lambda kc=kc, half=half: nc.tensor.matmul(
                            psb[b][0:bsz, half * 256:(half + 1) * 256], hT[:, kc, tb * bsz:(tb + 1) * bsz],
                            wv[half][:, kc, :], start=(kc == 0), stop=(kc == 7)),
                            [("w", ws[half]), ("hT", kc)], [PS(b)])
                slot = tb % 4
                P.add("act", lambda: nc.scalar.activation(out=stg[0:bsz, slot, :], in_=psb[b][0:bsz, :], func=AF.Copy,
                                                          scale=rtok[0:bsz, tb:tb + 1]), ["rtok"], [PS(b), ("stg", slot)])
                if isP:
                    kt = (i * 4 + tb)
                    if isBv:
                        dstt, tok = vbt[l][:, kt, :], ("vb", l, kt)
                    else:
                        dstt, tok = vat[l][:, kt % 8, :], ("va", l, kt % 8)
                    P.add("pool", lambda: nc.gpsimd.tensor_copy(out=dstt, in_=stg[:, slot, :]), [("stg", slot)], [tok])
                    if isBv:
                        store_rows(stg[:, slot, :], 128, pbv[l, sq_, tok0 + tb * 128:tok0 + (tb + 1) * 128, :],
                                   [("stg", slot)], slot)
                    elif i == 3:
                        store_rows(stg[:, slot, :], 128, pav[l, sq_, tb * 128:(tb + 1) * 128, :],
                                   [("stg", slot)], slot)
                else:
                    if isBv:
                        dstt, tok = vbt[oth][0:64, tb, :], ("vb", oth, tb)
                    else:
                        dstt, tok = vat[oth][0:64, tb, :], ("va", oth, tb)
                    P.add("pool", lambda: nc.gpsimd.tensor_copy(out=dstt, in_=stg[0:64, slot, :]), [("stg", slot)], [tok])
                    store_rows(stg[0:64, slot, :], 64, (sbv if isBv else sav)[l, tb, :, :], [("stg", slot)], slot)
            return [Pst, None, None, None]

        def qk_job(kind, c, ts):
            isB = kind[1] == "b"
            gcol = {"qa": 0, "ka": 1, "qb": 2, "kb": 3}[kind] * NL + l
            GC = gh[:, gcol:gcol + 1]
            need_rows = (kind == "kb") or (kind == "ka" and (not isP or grp["i"] == 3))
            st = {}
            FT = ("t", ts, "B")
            if kind == "ka":
                if isP:
                    dstk, ktok = kaT[l][:, c, (i % 2) * 512:(i % 2) * 512 + T], ("ka", l, c)
                else:
                    dstk, ktok = kaT[oth][:, c, 0:T], ("ka", oth, c)
            elif kind == "kb":
                if isP:
                    dstk, ktok = kbT[l][:, c, tok0:tok0 + T], ("kb", l, c)
                else:
                    dstk, ktok = kbT[oth][:, c, 0:T], ("kb", oth, c)
            qbase = 0 if kind == "qa" else 8

            def Pst():
                wv, ws = get_w(GIDX[kind])
                half, cl = c // 2, c % 2
                bz = rot8()
                st["bz"] = bz
                for kc in range(8):
                    P.add("pe", lambda kc=kc: nc.tensor.matmul(
                        psb[bz][:, 0:T], wv[half][:, kc, cl * 128:(cl + 1) * 128], hT[:, kc, 0:T],
                        start=(kc == 0), stop=(kc == 7)), [("w", ws[half]), ("hT", kc)], [PS(bz)])
                P.add("act", lambda: nc.scalar.activation(out=tq[ts][:, 0:T], in_=psb[bz][:, 0:T], func=AF.Square),
                      [], [PS(bz), ("t", ts, "q")])
                if isB:
                    P.add("act", lambda: nc.scalar.activation(out=tz[ts][:, 0:T], in_=psb[bz][:, 0:T], func=AF.Copy,
                                                              scale=GC), ["g"], [PS(bz), ("t", ts, "z")])

            def Bst():
                bz = st["bz"]
                if isB:
                    P.add("dve", lambda: nc.vector.scalar_tensor_tensor(
                        out=tB[ts][:, 0:T], in0=psb[bz][:, 0:T], scalar=GC, in1=cosT[:, 0:T],
                        op0=ALU.mult, op1=ALU.mult), ["cs", "g"], [PS(bz), FT])
                bn = rot8()
                P.add("pe", lambda: nc.tensor.matmul(psb[bn][:, 0:T], blkb[:], tq[ts][:, 0:T], start=True, stop=False),
                      [("t", ts, "q"), "c"], [PS(bn)])
                P.add("pe", lambda: nc.tensor.matmul(psb[bn][:, 0:T], blkb[:], epsv[:, 0:T], start=False, stop=True),
                      ["epsv", "c"], [PS(bn)])
                P.add("act", lambda: nc.scalar.activation(out=tA[ts][:, 0:T], in_=psb[bn][:, 0:T], func=AF.Ln,
                                                          scale=1.0 / 64), [], [PS(bn), ("t", ts, "A")])
                P.add("act", lambda: nc.scalar.activation(out=tA[ts][:, 0:T], in_=tA[ts][:, 0:T], func=AF.Exp, scale=-0.5),
                      [], [("t", ts, "A")])
                if isB:
                    br = rot8()
                    P.add("pe", lambda: nc.tensor.matmul(psb[br][:, 0:T], rpermb[:], tz[ts][:, 0:T], start=True, stop=True),
                          [("t", ts, "z"), "c"], [PS(br)])
                    P.add("dve", lambda: nc.vector.tensor_tensor(out=tC[ts][:, 0:T], in0=psb[br][:, 0:T], in1=sinT[:, 0:T],
                                                                 op=ALU.mult), ["cs"], [PS(br), ("t", ts, "C")])
                elif kind == "qa":
                    for hf in range(2):
                        lo, hi = hf * 64, hf * 64 + 64
                        P.add("dve", lambda lo=lo, hi=hi, hf=hf: nc.vector.scalar_tensor_tensor(
                            out=qpad[lo:hi, 2 * c + hf, 0:T], in0=psb[bz][lo:hi, 0:T], scalar=gh[lo:hi, gcol:gcol + 1],
                            in1=tA[ts][lo:hi, 0:T], op0=ALU.mult, op1=ALU.mult),
                            [("t", ts, "A"), "g"], [PS(bz), ("B", 2 * c + hf)])
                elif need_rows:
                    P.add("dve", lambda: nc.vector.scalar_tensor_tensor(
                        out=tB[ts][:, 0:T], in0=psb[bz][:, 0:T], scalar=GC, in1=tA[ts][:, 0:T],
                        op0=ALU.mult, op1=ALU.mult), [("t", ts, "A"), "g"], [PS(bz), FT])
                    P.add("pool", lambda: nc.gpsimd.tensor_copy(out=dstk, in_=tB[ts][:, 0:T]), [FT], [ktok])
                else:
                    P.add("dve", lambda: nc.vector.scalar_tensor_tensor(
                        out=dstk, in0=psb[bz][:, 0:T], scalar=GC, in1=tA[ts][:, 0:T],
                        op0=ALU.mult, op1=ALU.mult), [("t", ts, "A"), "g"], [PS(bz), ktok])

            def Rst():
                P.add("dve", lambda: nc.vector.tensor_tensor(out=tB[ts][:, 0:T], in0=tB[ts][:, 0:T], in1=tC[ts][:, 0:T],
                                                             op=ALU.add), [("t", ts, "C")], [FT])
                if kind == "qb":
                    for m in range(2):
                        lo, hi = m * 64, m * 64 + 64
                        P.add("dve", lambda lo=lo, hi=hi, m=m: nc.vector.tensor_tensor(
                            out=qpad[lo:hi, 8 + 2 * c + m, 0:T], in0=tB[ts][lo:hi, 0:T], in1=tA[ts][lo:hi, 0:T],
                            op=ALU.mult), [FT, ("t", ts, "A")], [("B", 8 + 2 * c + m)])
                else:
                    P.add("dve", lambda: nc.vector.tensor_tensor(out=tB[ts][:, 0:T], in0=tB[ts][:, 0:T], in1=tA[ts][:, 0:T],
                                                                 op=ALU.mult), [("t", ts, "A")], [FT])
                    P.add("pool", lambda: nc.gpsimd.tensor_copy(out=dstk, in_=tB[ts][:, 0:T]), [FT], [ktok])

            def Xst():
                fin = tB[ts]
                bt = rot8()
                for tt in range(ntt):
                    P.add("pe", lambda tt=tt: nc.tensor.transpose(
                        psb[bt][:, tt * 128:(tt + 1) * 128], fin[:, tt * 128:(tt + 1) * 128], ident[:]),
                        [FT, "c"], [PS(bt)])
                P.add("dve", lambda: nc.vector.tensor_copy(
                    out=stg[:, 0:ntt, c * 128:(c + 1) * 128],
                    in_=psb[bt][:, 0:T].rearrange("p (t f) -> p t f", t=ntt)),
                    [], [PS(bt)] + [("stg", s_) for s_ in range(ntt)])
                if c == 3:
                    SR = [("stg", s_) for s_ in range(ntt)]
                    if isP:
                        if kind == "kb":
                            dst = pbk[l, sq_, tok0:tok0 + T, :].rearrange("(t p) f -> p t f", p=128)
                        else:
                            dst = pak[l, sq_, :, :].rearrange("(t p) f -> p t f", p=128)
                        dma("pool", dst, stg[:, 0:ntt, :], SR, [], ("stg", 0))
                    else:
                        dd = sbk if kind == "kb" else sak
                        for tt in range(ntt):
                            dst = dd[l, 2 * tt:2 * tt + 2, :, :].rearrange("s j f -> (s j) f")
                            dma("pool", dst, stg[:, tt, :], [("stg", tt)], [], ("stg", tt))
            return [Pst, Bst, Rst if isB else None, Xst if need_rows else None]

        nblk = ntt if isP else grp["nseq"]
        nq = 0
        for kind in ("qa", "va", "qb", "vb", "ka", "kb"):
            if kind[0] == "v":
                for tb in range(nblk):
                    jobs.append(v_job(kind, tb))
            else:
                for c in range(4):
                    jobs.append(qk_job(kind, c, nq % 2))
                    nq += 1
        nj = len(jobs)
        for step in range(nj + 3):
            for k in (3, 2, 0, 1):
                ci = step - k
                if 0 <= ci < nj and jobs[ci][k] is not None:
                    jobs[ci][k]()
            if step == 1:
                rtok_part()
        if isP:
            attn_prompt(l, grp)
        else:
            attn_sample(l, grp)
        for ot in range(4):
            so = wload(l, "out", ot)
            wo = wring[:, so, :].rearrange("p (ec f) -> p ec f", ec=8)
            for dcl in range(2):
                dc = 2 * ot + dcl
                b = rot()
                for n_, ec in enumerate((4, 5, 6, 7, 0, 1, 2, 3)):
                    oc = 8 + ec if ec < 4 else 12 + ec
                    P.add("pe", lambda n_=n_, ec=ec, oc=oc, dcl=dcl, b=b, wo=wo: nc.tensor.matmul(
                        psb[b][:, 0:T], wo[:, ec, dcl * 128:(dcl + 1) * 128], aT[:, oc, 0:T],
                        start=(n_ == 0), stop=(n_ == 7)), [("w", so), ("B", oc)], [PS(b)])
                norm_flush()
                P.add("dve", lambda dc=dc, b=b: nc.vector.tensor_tensor(
                    out=xT[:, dc, 0:T], in0=psb[b][:, 0:T], in1=xT[:, dc, 0:T], op=ALU.add),
                    [], [PS(b), ("xT", dc)])
                norm_feed(dc, T, (2 * NL + l) * 8, bank=7)

    ering = [0]

    def eslot():
        s = ering[0] % 6
        ering[0] += 1
        return s

    def zero_block(ap, etok):
        P.add("act", lambda: nc.scalar.mul(out=ap, in_=ap, mul=0.0), [], [etok])

    def attn_prompt(l, grp):
        i = grp["i"]
        T = 512
        nk = 4 * (i + 1)
        LA = 3
        tiles = [(h, m, j) for h in range(4) for m in range(2) for j in range(nk)]
        bo = [4, 6]
        bZ = [5, 7]
        pend = {}
        deferred = []

        def b_norm_m(h, m):
            P.add("act", lambda: nc.scalar.activation(out=tA[m][:, 0:T], in_=psb[bZ[m]][:, 0:T], func=AF.Ln),
                  [], [PS(bZ[m]), ("t", m, "A")])
            P.add("act", lambda: nc.scalar.activation(out=tA[m][:, 0:T], in_=tA[m][:, 0:T], func=AF.Exp, scale=-1.0),
                  [], [("t", m, "A")])
            P.add("dve", lambda: nc.vector.tensor_tensor(
                out=tB[m][:, 0:T], in0=psb[bo[m]][:, 0:T], in1=tA[m][:, 0:T], op=ALU.mult),
                [("t", m, "A")], [PS(bo[m]), ("t", m, "B")])

        def b_combine(h):
            P.add("dve", lambda: nc.vector.scalar_tensor_tensor(
                out=tC[0][:, 0:T], in0=tB[1][:, 0:T], scalar=nlam[:, l:l + 1], in1=tB[0][:, 0:T],
                op0=ALU.mult, op1=ALU.add), [("t", 0, "B"), ("t", 1, "B"), "g"], [("t", 0, "C")])
            P.add("act", lambda: nc.scalar.activation(out=tq[0][:, 0:T], in_=tC[0][:, 0:T], func=AF.Square),
                  [("t", 0, "C")], [("t", 0, "q")])

        def b_subnorm(h):
            bn = rot()
            P.add("pe", lambda: nc.tensor.matmul(psb[bn][:, 0:T], onesb[:], tq[0][:, 0:T], start=True, stop=True),
                  [("t", 0, "q"), "c"], [PS(bn)])
            P.add("act", lambda: nc.scalar.activation(out=tC[1][:, 0:T], in_=psb[bn][:, 0:T], func=AF.Ln,
                                                      scale=1.0 / 128, bias=epsb[:, 0:1]), ["c"], [PS(bn), ("t", 1, "C")])
            P.add("act", lambda: nc.scalar.activation(out=tC[1][:, 0:T], in_=tC[1][:, 0:T], func=AF.Exp, scale=-0.5),
                  [], [("t", 1, "C")])
            P.add("dve", lambda: nc.vector.scalar_tensor_tensor(
                out=aT[:, 16 + h, 0:T], in0=tC[0][:, 0:T], scalar=gsub[:, l:l + 1], in1=tC[1][:, 0:T],
                op0=ALU.mult, op1=ALU.mult), [("t", 0, "C"), ("t", 1, "C"), "g"], [("B", 16 + h)])

        def b1(idx):
            h, m, j = tiles[idx]
            qv = qpad[:, 8 + 2 * h + m, :]
            QT = ("B", 8 + 2 * h + m)
            r = j - 4 * i
            q0 = 128 * r if r > 0 else 0
            bs = rot()
            P.add("pe", lambda: nc.tensor.matmul(
                psb[bs][:, q0:512], kbT[l][:, h, j * 128:(j + 1) * 128], qv[:, q0:512],
                start=True, stop=(r < 0)), [("kb", l, h), QT], [PS(bs)])
            if r >= 0:
                P.add("pe", lambda: nc.tensor.matmul(psb[bs][:, q0:q0 + 64], identb[:], maskB[:],
                                                     start=False, stop=True), ["c"], [PS(bs)])
            es = eslot()
            P.add("act", lambda: nc.scalar.activation(
                out=eT[:, es, q0:512], in_=psb[bs][:, q0:512], func=AF.Exp, scale=0.125),
                [], [PS(bs), ("e", es)])
            pend[idx] = (es, q0)

        def b2(idx):
            h, m, j = tiles[idx]
            es, q0 = pend.pop(idx)
            P.add("pe", lambda: nc.tensor.matmul(
                psb[bo[m]][:, q0:512], vbt[l][:, j, h * 128:(h + 1) * 128], eT[:, es, q0:512],
                start=(j == 0), stop=(j == nk - 1)), [("vb", l, j), ("e", es)], [PS(bo[m])])
            P.add("pe", lambda: nc.tensor.matmul(
                psb[bZ[m]][:, q0:512], onesb[:], eT[:, es, q0:512],
                start=(j == 0), stop=(j == nk - 1)), [("e", es), "c"], [PS(bZ[m])])
            if j == nk - 1:
                deferred.append([idx + 2, b_norm_m, (h, m)])
                if m == 1:
                    deferred.append([idx + 3, b_combine, (h,)])
                    deferred.append([idx + 6, b_subnorm, (h,)])

        def flush(upto):
            while deferred and deferred[0][0] <= upto:
                _, fn, args = deferred.pop(0)
                fn(*args)

        nt = len(tiles)
        for idx in range(nt + LA):
            if idx < nt:
                b1(idx)
            if idx - LA >= 0:
                b2(idx - LA)
                flush(idx - LA)
        units = [(h, u) for h in range(8) for u in range(4)]
        pendA = {}

        def a1(idx):
            h, u = units[idx]
            hp = h // 2
            bi0 = (l * 8 + h) * 2
            ug = 4 * i + u
            t4 = [t for t in range(4) if ug - 4 + t >= 0]
            BC = bconst[:, l * 8 + h:l * 8 + h + 1]
            qv = qpad[:, h, u * 128:(u + 1) * 128]
            j = ug
            kcol = ((j // 4) % 2) * 512 + (j % 4) * 128
            b2_ = rot()
            P.add("pe", lambda: nc.tensor.matmul(psb[b2_][:, 0:128], antib[:], btile[:, bi0 + 1, :],
                                                 start=True, stop=False), ["bt", "c"], [PS(b2_)])
            P.add("pe", lambda: nc.tensor.matmul(
                psb[b2_][:, 0:128], kaT[l][:, hp, kcol:kcol + 128], qv, start=False, stop=True),
                [("ka", l, hp), ("B", h)], [PS(b2_)])
            es2 = eslot()
            P.add("act", lambda: nc.scalar.activation(
                out=eT[:, es2, 0:128], in_=psb[b2_][:, 0:128], func=AF.Exp, scale=0.125, bias=BC),
                ["bc"], [PS(b2_), ("e", es2)])
            es = None
            if t4:
                b1_ = rot()
                for t in t4:
                    j = ug - 4 + t
                    kcol = ((j // 4) % 2) * 512 + (j % 4) * 128
                    first = True
                    if t == 3:
                        P.add("pe", lambda t=t: nc.tensor.matmul(
                            psb[b1_][:, t * 128:(t + 1) * 128], antib[:], btile[:, bi0, :], start=True, stop=False),
                            ["bt", "c"], [PS(b1_)])
                        first = False
                    if t == 0:
                        P.add("pe", lambda t=t: nc.tensor.matmul(
                            psb[b1_][:, 0:128], identb[:], maskA0[:], start=True, stop=False), ["c"], [PS(b1_)])
                        first = False
                    P.add("pe", lambda t=t, kcol=kcol, first=first: nc.tensor.matmul(
                        psb[b1_][:, t * 128:(t + 1) * 128], kaT[l][:, hp, kcol:kcol + 128], qv,
                        start=first, stop=True), [("ka", l, hp), ("B", h)], [PS(b1_)])
                es = eslot()
                c0 = t4[0] * 128
                P.add("act", lambda: nc.scalar.activation(
                    out=eT[:, es, c0:512], in_=psb[b1_][:, c0:512], func=AF.Exp, scale=0.125, bias=BC),
                    ["bc"], [PS(b1_), ("e", es)])
            pendA[idx] = [(4, es2, 0)] + [(t, es, t * 128) for t in t4]

        def a2(idx):
            h, u = units[idx]
            hp, hf = h // 2, h % 2
            ug = 4 * i + u
            bo_, bZ_ = (4, 5) if h % 2 == 0 else (6, 7)
            seq_mm = pendA.pop(idx)
            for n_, (t, e_, ecol) in enumerate(seq_mm):
                j = ug - 4 + t
                P.add("pe", lambda n_=n_, j=j, e_=e_, ecol=ecol: nc.tensor.matmul(
                    psb[bo_][:, u * 128:(u + 1) * 128], vat[l][:, j % 8, hp * 128:(hp + 1) * 128],
                    eT[:, e_, ecol:ecol + 128], start=(n_ == 0), stop=(n_ == len(seq_mm) - 1)),
                    [("va", l, j % 8), ("e", e_)], [PS(bo_)])
                P.add("pe", lambda n_=n_, e_=e_, ecol=ecol: nc.tensor.matmul(
                    psb[bZ_][:, u * 128:(u + 1) * 128], onesb[:], eT[:, e_, ecol:ecol + 128],
                    start=(n_ == 0), stop=(n_ == len(seq_mm) - 1)), [("e", e_), "c"], [PS(bZ_)])
            if u == 3:
                deferred.append([idx + 1, a_finish, (h,)])

        def a_finish(h):
            hp, hf = h // 2, h % 2
            bo_, bZ_ = (4, 5) if h % 2 == 0 else (6, 7)
            ts = h % 2
            lo, hi = hf * 64, hf * 64 + 64
            P.add("act", lambda: nc.scalar.activation(
                out=tA[ts][lo:hi, 0:T], in_=psb[bZ_][lo:hi, 0:T], func=AF.Ln), [], [PS(bZ_), ("t", ts, "A")])
            P.add("act", lambda: nc.scalar.activation(
                out=tA[ts][lo:hi, 0:T], in_=tA[ts][lo:hi, 0:T], func=AF.Exp, scale=-1.0), [], [("t", ts, "A")])
            P.add("dve", lambda: nc.vector.tensor_tensor(
                out=aT[lo:hi, 8 + hp, 0:T], in0=psb[bo_][lo:hi, 0:T], in1=tA[ts][lo:hi, 0:T], op=ALU.mult),
                [("t", ts, "A")], [PS(bo_), ("B", 8 + hp)])

        nu = len(units)
        for idx in range(nu + 1):
            if idx < nu:
                a1(idx)
            if idx >= 1:
                a2(idx - 1)
                if idx >= 2:
                    flush(idx - 1)
            if idx == 1:
                flush(10 ** 9)
        flush(10 ** 9)

    def attn_sample(l, grp):
        T = grp["T"]
        nseq = grp["nseq"]
        oth = 1 - l
        cstg = [vbt[oth][:, 4 + 4 * s_:8 + 4 * s_, :].rearrange("p a b -> p (a b)").bitcast(F32).rearrange(
            "p (k f) -> p k f", k=2) for s_ in range(3)]
        CT = [[("vb", oth, 4 + 4 * s_ + k) for k in range(4)] for s_ in range(3)]
        ev = [0]
        vc = [0]

        def evac(out, in_, reads, writes):
            ev[0] += 1
            if ev[0] % 2 == 0:
                P.add("act", lambda: nc.scalar.copy(out=out, in_=in_), reads, writes)
            else:
                P.add("dve", lambda: nc.vector.tensor_copy(out=out, in_=in_), reads, writes)

        def KBS(h, half):
            return ("kbs", l, h, half)

        pieces = []
        for s_ in range(nseq):
            for half in range(2):
                for pc in range(4):
                    pieces.append((s_, cbk, half * 4 + pc, True, True))
                for pc in range(4):
                    pieces.append((s_, cbv, half * 4 + pc, False, True))
            for pc in range(2):
                pieces.append((s_, cak, pc, True, False))
            for pc in range(2):
                pieces.append((s_, cav, pc, False, False))
        pstate = {"dma": 0, "conv": 0}

        def piece_dma():
            n = pstate["dma"]
            if n >= len(pieces):
                return
            pstate["dma"] += 1
            s_, csrc, pc, isK, isBc = pieces[n]
            sl = n % 3
            dma("sp", cstg[sl], csrc[l, s_, pc * 256:(pc + 1) * 256, :].rearrange("(k p) f -> p k f", p=128),
                [], CT[sl], ("cst", sl))

        def piece_conv():
            n = pstate["conv"]
            if n >= len(pieces):
                return
            while pstate["dma"] < min(n + 3, len(pieces)):
                piece_dma()
            pstate["conv"] += 1
            s_, csrc, pc, isK, isBc = pieces[n]
            sl = n % 3
            for k in range(2):
                kt = pc * 2 + k
                if isK:
                    b = rot()
                    for c in range(4):
                        P.add("pe", lambda c=c: nc.tensor.transpose(
                            psb[b][:, c * 128:(c + 1) * 128], cstg[sl][:, k, c * 128:(c + 1) * 128], ident[:]),
                            CT[sl] + ["c"], [PS(b)])
                    if isBc:
                        dst = kbT[l][:, :, kt * 128:(kt + 1) * 128]
                        toks = [KBS(c, kt // 8) for c in range(4)]
                        if s_ == 0:
                            toks = toks + [("kb", l, c) for c in range(4)]
                    else:
                        dst = kaT[l][:, :, kt * 128:(kt + 1) * 128]
                        toks = [("ka", l, c) for c in range(4)]
                    evac(dst, psb[b][:, :].rearrange("p (c k) -> p c k", c=4), [], [PS(b)] + toks)
                else:
                    if isBc:
                        dst, tok = vbt[l][:, kt, :], ("vb", l, kt)
                    else:
                        dst, tok = vat[l][:, kt, :], ("va", l, kt)
                    vc[0] += 1
                    if vc[0] % 3 == 0:
                        P.add("pool", lambda: nc.gpsimd.tensor_copy(out=dst, in_=cstg[sl][:, k, :]), CT[sl], [tok])
                    elif vc[0] % 3 == 1:
                        P.add("dve", lambda: nc.vector.tensor_copy(out=dst, in_=cstg[sl][:, k, :]), CT[sl], [tok])
                    else:
                        P.add("act", lambda: nc.scalar.copy(out=dst, in_=cstg[sl][:, k, :]), CT[sl], [tok])

        for _ in range(8):
            piece_conv()

        sunits = [(h, m) for h in range(4) for m in range(2)]
        BO = [4, 6]
        BZ = [5, 7]

        for s in range(nseq):
            qc0 = s * 64
            for half in range(2):
                bo, bZ = BO[half], BZ[half]
                spend = {}

                def sb1(n):
                    h, m = sunits[n]
                    qv = qpad[:, 8 + 2 * h + m, qc0:qc0 + 64]
                    QT = ("B", 8 + 2 * h + m)
                    bs = rot()
                    for k8 in range(8):
                        j = half * 8 + k8
                        P.add("pe", lambda j=j, k8=k8: nc.tensor.matmul(
                            psb[bs][:, k8 * 64:(k8 + 1) * 64], kbT[l][:, h, j * 128:(j + 1) * 128], qv,
                            start=True, stop=True), [KBS(h, half), QT], [PS(bs)])
                    es = eslot()
                    P.add("act", lambda: nc.scalar.activation(
                        out=eT[:, es, :], in_=psb[bs][:, :], func=AF.Exp, scale=0.125), [], [PS(bs), ("e", es)])
                    es3 = None
                    if half == 1:
                        bs3 = rot()
                        P.add("pe", lambda: nc.tensor.matmul(
                            psb[bs3][0:64, 0:64], kbT[oth][:, h, qc0:qc0 + 64], qv, start=True, stop=True),
                            [("kb", oth, h), QT], [PS(bs3)])
                        es3 = eslot()
                        P.add("act", lambda: nc.scalar.activation(
                            out=eT[0:64, es3, 0:64], in_=psb[bs3][0:64, 0:64], func=AF.Exp, scale=0.125),
                            [], [PS(bs3), ("e", es3)])
                    spend[n] = (es, es3)

                def sb2(n):
                    h, m = sunits[n]
                    col = (2 * h + m) * 64
                    es, es3 = spend.pop(n)
                    nt = 9 if half == 1 else 8
                    for jj in range(nt):
                        if jj < 8:
                            j = half * 8 + jj
                            va_, e_, tokv = vbt[l][:, j, h * 128:(h + 1) * 128], eT[:, es, jj * 64:(jj + 1) * 64], ("vb", l, j)
                            on_ = onesb[:]
                            et = ("e", es)
                        else:
                            va_, e_, tokv = vbt[oth][0:64, s, h * 128:(h + 1) * 128], eT[0:64, es3, 0:64], ("vb", oth, s)
                            on_ = onesb[0:64, :]
                            et = ("e", es3)
                        P.add("pe", lambda jj=jj, va_=va_, e_=e_: nc.tensor.matmul(
                            psb[bo][:, col:col + 64], va_, e_, start=(jj == 0), stop=(jj == nt - 1)),
                            [tokv, et], [PS(bo)])
                        P.add("pe", lambda jj=jj, on_=on_, e_=e_: nc.tensor.matmul(
                            psb[bZ][:, col:col + 64], on_, e_, start=(jj == 0), stop=(jj == nt - 1)),
                            [et, "c"], [PS(bZ)])

                for n in range(len(sunits) + 1):
                    if n < len(sunits):
                        sb1(n)
                    if n >= 1:
                        sb2(n - 1)
                        piece_conv()
            P.add("dve", lambda: nc.vector.tensor_copy(out=tA[0][:, :], in_=psb[BZ[0]][:, :]), [], [PS(BZ[0]), ("t", 0, "A")])
            P.add("dve", lambda: nc.vector.tensor_tensor(out=tA[0][:, :], in0=psb[BZ[1]][:, :], in1=tA[0][:, :], op=ALU.add),
                  [], [PS(BZ[1]), ("t", 0, "A")])
            P.add("act", lambda: nc.scalar.activation(out=tA[0][:, :], in_=tA[0][:, :], func=AF.Ln), [], [("t", 0, "A")])
            P.add("act", lambda: nc.scalar.activation(out=tA[0][:, :], in_=tA[0][:, :], func=AF.Exp, scale=-1.0), [], [("t", 0, "A")])
            P.add("dve", lambda: nc.vector.tensor_copy(out=tB[0][:, :], in_=psb[BO[0]][:, :]), [], [PS(BO[0]), ("t", 0, "B")])
            P.add("dve", lambda: nc.vector.tensor_tensor(out=tB[0][:, :], in0=psb[BO[1]][:, :], in1=tB[0][:, :], op=ALU.add),
                  [], [PS(BO[1]), ("t", 0, "B")])
            P.add("dve", lambda: nc.vector.tensor_tensor(out=tB[0][:, :], in0=tB[0][:, :], in1=tA[0][:, :], op=ALU.mult),
                  [("t", 0, "A")], [("t", 0, "B")])
            onv = tB[0][:, :].rearrange("p (h m q) -> p h m q", h=4, m=2)
            obv = tC[0][:, 0:256].rearrange("p (h q) -> p h q", h=4)
            P.add("dve", lambda: nc.vector.scalar_tensor_tensor(
                out=obv, in0=onv[:, :, 1, :], scalar=nlam[:, l:l + 1], in1=onv[:, :, 0, :],
                op0=ALU.mult, op1=ALU.add), [("t", 0, "B"), "g"], [("t", 0, "C")])
            P.add("act", lambda: nc.scalar.activation(out=tq[0][:, 0:256], in_=tC[0][:, 0:256], func=AF.Square),
                  [("t", 0, "C")], [("t", 0, "q")])
            bn = rot()
            P.add("pe", lambda bn=bn: nc.tensor.matmul(psb[bn][:, 0:256], onesb[:], tq[0][:, 0:256], start=True, stop=True),
                  [("t", 0, "q"), "c"], [PS(bn)])
            P.add("act", lambda bn=bn: nc.scalar.activation(out=tC[1][:, 0:256], in_=psb[bn][:, 0:256], func=AF.Ln,
                                                            scale=1.0 / 128, bias=epsb[:, 0:1]), ["c"], [PS(bn), ("t", 1, "C")])
            P.add("act", lambda: nc.scalar.activation(out=tC[1][:, 0:256], in_=tC[1][:, 0:256], func=AF.Exp, scale=-0.5),
                  [], [("t", 1, "C")])
            P.add("dve", lambda: nc.vector.scalar_tensor_tensor(
                out=aT[:, 16:20, qc0:qc0 + 64], in0=obv, scalar=gsub[:, l:l + 1],
                in1=tC[1][:, 0:256].rearrange("p (h q) -> p h q", h=4), op0=ALU.mult, op1=ALU.mult),
                [("t", 0, "C"), ("t", 1, "C"), "g"], [("B", c) for c in range(16, 20)])
            bo, bZ = 6, 7
            for h in range(8):
                hp, hf = h // 2, h % 2
                bi0 = (l * 8 + h) * 2
                qv = qpad[:, h, qc0:qc0 + 64]
                col = h * 64
                b1 = rot()
                for t in range(4):
                    first = True
                    if t == 3:
                        P.add("pe", lambda b1=b1, t=t: nc.tensor.matmul(
                            psb[b1][:, t * 64:(t + 1) * 64], antib[:], btile[:, bi0, 0:64], start=True, stop=False),
                            ["bt", "c"], [PS(b1)])
                        first = False
                    P.add("pe", lambda b1=b1, t=t, first=first, qv=qv: nc.tensor.matmul(
                        psb[b1][:, t * 64:(t + 1) * 64], kaT[l][:, hp, t * 128:(t + 1) * 128], qv,
                        start=first, stop=True), [("ka", l, hp), ("B", h)], [PS(b1)])
                es = eslot()
                P.add("act", lambda b1=b1, es=es: nc.scalar.activation(
                    out=eT[:, es, 0:256], in_=psb[b1][:, 0:256], func=AF.Exp, scale=0.125,
                    bias=bconst[:, l * 8 + h:l * 8 + h + 1]), ["bc"], [PS(b1), ("e", es)])
                b2 = rot()
                P.add("pe", lambda b2=b2: nc.tensor.matmul(
                    psb[b2][0:64, 0:64], antib[64:128, 0:64], btile[64:128, bi0 + 1, 0:64], start=True, stop=False),
                    ["bt", "c"], [PS(b2)])
                P.add("pe", lambda b2=b2, qv=qv: nc.tensor.matmul(
                    psb[b2][0:64, 0:64], kaT[oth][:, hp, qc0:qc0 + 64], qv, start=False, stop=True),
                    [("ka", oth, hp), ("B", h)], [PS(b2)])
                es2 = eslot()
                P.add("act", lambda b2=b2, es2=es2: nc.scalar.activation(
                    out=eT[0:64, es2, 0:64], in_=psb[b2][0:64, 0:64], func=AF.Exp, scale=0.125,
                    bias=bconst[0:64, l * 8 + h:l * 8 + h + 1]), ["bc"], [PS(b2), ("e", es2)])
                for j in range(5):
                    if j < 4:
                        va_, e_, tokv, on_, et = vat[l][:, j, hp * 128:(hp + 1) * 128], eT[:, es, j * 64:(j + 1) * 64], ("va", l, j), onesb[:], ("e", es)
                    else:
                        va_, e_, tokv, on_, et = vat[oth][0:64, s, hp * 128:(hp + 1) * 128], eT[0:64, es2, 0:64], ("va", oth, s), onesb[0:64, :], ("e", es2)
                    P.add("pe", lambda j=j, va_=va_, e_=e_, col=col: nc.tensor.matmul(
                        psb[bo][:, col:col + 64], va_, e_, start=(j == 0), stop=(j == 4)), [tokv, et], [PS(bo)])
                    P.add("pe", lambda j=j, on_=on_, e_=e_, col=col: nc.tensor.matmul(
                        psb[bZ][:, col:col + 64], on_, e_, start=(j == 0), stop=(j == 4)), [et, "c"], [PS(bZ)])
                if h % 2 == 1:
                    piece_conv()
            for hf in range(2):
                lo, hi = hf * 64, hf * 64 + 64
                zv = psb[bZ][lo:hi, :].rearrange("p (hp f q) -> p hp f q", hp=4, f=2)[:, :, hf, :]
                ov = psb[bo][lo:hi, :].rearrange("p (hp f q) -> p hp f q", hp=4, f=2)[:, :, hf, :]
                tv = tA[1][lo:hi, 0:256].rearrange("p (hp q) -> p hp q", hp=4)
                P.add("act", lambda tv=tv, zv=zv: nc.scalar.activation(out=tv, in_=zv, func=AF.Ln), [], [PS(bZ), ("t", 1, "A")])
                P.add("act", lambda tv=tv: nc.scalar.activation(out=tv, in_=tv, func=AF.Exp, scale=-1.0), [], [("t", 1, "A")])
                P.add("dve", lambda tv=tv, ov=ov, lo=lo, hi=hi: nc.vector.tensor_tensor(
                    out=aT[lo:hi, 8:12, qc0:qc0 + 64], in0=ov, in1=tv, op=ALU.mult),
                    [("t", 1, "A")], [PS(bo)] + [("B", c) for c in range(8, 12)])

    groups = []
    for sq_ in range(NPS):
        for i in range(4):
            groups.append(dict(kind="p", seq=sq_, i=i, T=512))
    if NSS:
        groups.append(dict(kind="s", T=NSS * 64, nseq=NSS))
    def grp_rows(grp, dram_p, dram_s):
        if grp["kind"] == "p":
            return dram_p[grp["seq"], grp["i"] * 512:(grp["i"] + 1) * 512, :]
        return dram_s[0:grp["T"], :]

    prefetch_x(grp_rows(groups[0], xp, xs), groups[0]["T"])
    for gi_, grp in enumerate(groups):
        T = grp["T"]
        load_x(T)
        nxt = groups[gi_ + 1] if gi_ + 1 < len(groups) else None
        for l in range(NL):
            ffn(l, 1, T)
            if gi_ == 0:
                setup_bias(l)
            mix(l, grp)
            pre = None
            if l == NL - 1 and nxt is not None:
                pre = (lambda nxt=nxt: prefetch_x(grp_rows(nxt, xp, xs), nxt["T"]))
            ffn(l, 2, T, pre=pre)
        store_y(grp_rows(grp, yp, ys), T)
    nw = P.emit_all()
    return nc, dict(n_ops=len(P.ops), n_wait=nw, dbg=dbg_outs)


def _constants():
    ident = np.eye(128, dtype=np.float32)
    anti = np.ascontiguousarray(ident[::-1])
    blk = np.zeros((128, 128), np.float32)
    blk[:64, :64] = 1.0
    blk[64:, 64:] = 1.0
    rperm = np.zeros((128, 128), np.float32)
    for m in range(128):
        if (m % 64) < 32:
            rperm[m + 32, m] = -1.0
        else:
            rperm[m - 32, m] = 1.0
    inv = (10000.0 ** (-np.arange(32, dtype=np.float32) * 2.0 / 64)).astype(np.float32)
    fidx = np.arange(128) % 32

    def tab(pos):
        ang = pos.astype(np.float32)[None, :] * inv[fidx][:, None]
        return np.cos(ang).astype(np.float32), np.sin(ang).astype(np.float32)
    cosp, sinp = tab(np.arange(SEQ))
    coss, sins = tab(np.tile(PAST + np.arange(64), 4))
    return dict(c_ident=ident, c_anti=anti, c_blk=blk, c_rperm=rperm, c_cosp=cosp, c_sinp=sinp,
                c_coss=coss, c_sins=sins)


_CACHE = {}


def kernel(x_prompt, x_sample, cache_a_k, cache_a_v, cache_b_k, cache_b_v,
           g_ffn1, w1_gate, w1_up, w1_down, g_mix, w_in, g_qa, g_ka, g_qb, g_kb,
           rel_bias, lam_q1, lam_k1, lam_q2, lam_k2, g_sub, w_out,
           g_ffn2, w2_gate, w2_up, w2_down):
    f = lambda a: np.ascontiguousarray(np.asarray(a, dtype=np.float32))
    NL = 2
    if "nc" not in _CACHE:
        _CACHE["nc"] = build()[0]
    nc = _CACHE["nc"]
    consts = _constants()
    shared = dict(w1_gate=f(w1_gate), w1_up=f(w1_up), w1_down=f(w1_down), w_in=f(w_in), w_out=f(w_out),
                  w2_gate=f(w2_gate), w2_up=f(w2_up), w2_down=f(w2_down),
                  g_ffn1=f(g_ffn1), g_mix=f(g_mix), g_ffn2=f(g_ffn2), g_qa=f(g_qa), g_ka=f(g_ka),
                  g_qb=f(g_qb), g_kb=f(g_kb), g_sub=f(g_sub), rel_bias=f(rel_bias),
                  lam_q1=f(lam_q1), lam_k1=f(lam_k1), lam_q2=f(lam_q2), lam_k2=f(lam_k2))
    shared.update(consts)
    xpn, xsn = f(x_prompt), f(x_sample)
    cak, cav = f(cache_a_k).reshape(NL, 32, 512, 512), f(cache_a_v).reshape(NL, 32, 512, 512)
    cbk, cbv = f(cache_b_k).reshape(NL, 32, PAST, 512), f(cache_b_v).reshape(NL, 32, PAST, 512)
    in_maps = []
    for c in range(NCORES):
        m = dict(shared)
        m["xp"] = xpn[2 * c:2 * c + 2]
        m["xs"] = xsn[4 * c:4 * c + 4].reshape(256, D_MODEL)
        m["cak"] = np.ascontiguousarray(cak[:, 4 * c:4 * c + 4])
        m["cav"] = np.ascontiguousarray(cav[:, 4 * c:4 * c + 4])
        m["cbk"] = np.ascontiguousarray(cbk[:, 4 * c:4 * c + 4])
        m["cbv"] = np.ascontiguousarray(cbv[:, 4 * c:4 * c + 4])
        in_maps.append(m)
    res = run_bass_kernel_spmd(nc, in_maps, core_ids=list(range(NCORES)))
    R = res.results
    yp = np.concatenate([r["yp"] for r in R], axis=0)
    ys = np.concatenate([r["ys"].reshape(4, 64, D_MODEL) for r in R], axis=0)
    cat1 = lambda k: np.concatenate([r[k] for r in R], axis=1)
    pak = cat1("pak").reshape(NL, 16, 512, 8, 64)
    pav = cat1("pav").reshape(NL, 16, 512, 8, 64)
    pbk = cat1("pbk").reshape(NL, 16, SEQ, 4, 128)
    pbv = cat1("pbv").reshape(NL, 16, SEQ, 4, 128)
    sak = cat1("sak").reshape(NL, 32, 64, 8, 64)
    sav = cat1("sav").reshape(NL, 32, 64, 8, 64)
    sbk = cat1("sbk").reshape(NL, 32, 64, 4, 128)
    sbv = cat1("sbv").reshape(NL, 32, 64, 4, 128)
    return (yp.astype(np.float32), ys.astype(np.float32), pak, pav, pbk, pbv, sak, sav, sbk, sbv)
```

```python
import math
import types
import numpy as np
import concourse.bass as bass
import concourse.mybir as mybir
from concourse.bass_utils import run_bass_kernel_spmd

F32 = mybir.dt.float32
BF16 = mybir.dt.bfloat16
AF = mybir.ActivationFunctionType
ALU = mybir.AluOpType

D_MODEL = 1024
D_FF = 2816
NFC = 22
SEQ = 2048
PAST = 2048
NCORES = 8
EPS = 1e-6
WT = 2048
NSLOT = 4


def _freeze(fn):
    if fn.__closure__ is None:
        return fn
    cells = []
    for c in fn.__closure__:
        try:
            cells.append(types.CellType(c.cell_contents))
        except ValueError:
            cells.append(c)
    return types.FunctionType(fn.__code__, fn.__globals__, fn.__name__, fn.__defaults__, tuple(cells))


class Prog:
    def __init__(self, nc, same_engine_sync=True):
        self.nc = nc
        self.eng = {"pe": nc.tensor, "act": nc.scalar, "dve": nc.vector,
                    "pool": nc.gpsimd, "sp": nc.sync}
        self.ops = []
        self.last_w = {}
        self.readers = {}
        self.same_engine_sync = same_engine_sync
        self.dma_fill = {}

    def add(self, eng, emit, reads=(), writes=(), dma_key=None):
        idx = len(self.ops)
        deps = set()
        for r in reads:
            lw = self.last_w.get(r)
            if lw is not None:
                deps.add(lw)
            self.readers.setdefault(r, []).append(idx)
        for w in writes:
            lw = self.last_w.get(w)
            if lw is not None:
                deps.add(lw)
            rs = self.readers.get(w)
            if rs:
                deps.update(rs)
            self.last_w[w] = idx
            self.readers[w] = []
        deps.discard(idx)
        fill = None
        if dma_key is not None:
            fill = self.dma_fill.get(dma_key, 0) + 1
            self.dma_fill[dma_key] = fill
        self.ops.append([eng, _freeze(emit), deps, dma_key, fill, False, 0])
        return idx

    def emit_all(self):
        nc = self.nc
        ops = self.ops
        pruned = []
        for j, (eng, emit, deps, key, fill, _, _) in enumerate(ops):
            best = {}
            for i in deps:
                e_i, _, _, k_i, _, _, _ = ops[i]
                sid = ("dma", k_i) if k_i is not None else e_i
                if k_i is None and e_i == eng and key is None:
                    if eng == "pe" or not self.same_engine_sync:
                        continue
                if sid not in best or best[sid] < i:
                    best[sid] = i
            pruned.append(sorted(best.values()))
            for i in best.values():
                ops[i][5] = True
        cnt = {}
        for op in ops:
            if op[3] is None and op[5]:
                cnt[op[0]] = cnt.get(op[0], 0) + 1
                op[6] = cnt[op[0]]
        esem = {e: nc.alloc_semaphore("s_" + e) for e in ("pe", "act", "dve", "pool")}
        dsem = {}
        for n, k in enumerate(self.dma_fill):
            dsem[k] = nc.alloc_semaphore("d%d" % n)
        know = {e: {} for e in self.eng}
        snap = {}
        n_wait = 0
        for j, (eng, emit, deps, key, fill, mark, c) in enumerate(ops):
            E = self.eng[eng]
            K = know[eng]
            for i in pruned[j]:
                e_i, _, _, k_i, f_i, _, c_i = ops[i]
                if k_i is not None:
                    sem, val, sk = dsem[k_i], 16 * f_i, ("d", k_i)
                else:
                    sem, val, sk = esem[e_i], c_i, e_i
                if K.get(sk, 0) >= val:
                    continue
                E.wait_ge(sem, val)
                n_wait += 1
                K[sk] = val
                si = snap.get(i)
                if si:
                    for k2, v2 in si.items():
                        if K.get(k2, 0) < v2:
                            K[k2] = v2
            if key is not None or mark:
                snap[j] = dict(K)
            inst = emit()
            if key is not None:
                inst.then_inc(dsem[key], 16)
            elif mark:
                inst.then_inc(esem[eng], 1)
        for k, f in self.dma_fill.items():
            if know["sp"].get(("d", k), 0) < 16 * f:
                nc.sync.wait_ge(dsem[k], 16 * f)
        return n_wait


def _weight_tiles(NL):
    cat = {}
    for l in range(NL):
        for f in (1, 2):
            for ft in range(11):
                cat[(l, "g%d" % f, ft)] = len(cat)
                cat[(l, "u%d" % f, ft)] = len(cat)
            for half in range(2):
                for t6 in range(6):
                    cat[(l, "d%d" % f, half * 6 + t6)] = len(cat)
            if f == 1:
                for ct in range(12):
                    cat[(l, "in", ct)] = len(cat)
                for ot in range(4):
                    cat[(l, "out", ot)] = len(cat)
    return cat


def build(NPS=2, NSS=4, NL=2, same_engine_sync=True, debug=None):
    nc = bass.Bass("TRN2", target_bir_lowering=False)
    P = Prog(nc, same_engine_sync)
    dbg_outs = {}

    def din(name, shape, dt=F32):
        return nc.dram_tensor(name, list(shape), dt, kind="ExternalInput").ap()

    def dout(name, shape, dt=F32):
        return nc.dram_tensor(name, list(shape), dt, kind="ExternalOutput").ap()

    xp = din("xp", [max(NPS, 1), SEQ, D_MODEL])
    xs = din("xs", [max(NSS, 1) * 64, D_MODEL])
    cak = din("cak", [NL, max(NSS, 1), 512, 512])
    cav = din("cav", [NL, max(NSS, 1), 512, 512])
    cbk = din("cbk", [NL, max(NSS, 1), PAST, 512])
    cbv = din("cbv", [NL, max(NSS, 1), PAST, 512])
    W = {}
    for f in (1, 2):
        W["g%d" % f] = din("w%d_gate" % f, [NL, D_MODEL, D_FF])
        W["u%d" % f] = din("w%d_up" % f, [NL, D_MODEL, D_FF])
        W["d%d" % f] = din("w%d_down" % f, [NL, D_FF, D_MODEL])
    W["in"] = din("w_in", [NL, D_MODEL, 3072])
    W["out"] = din("w_out", [NL, D_MODEL, D_MODEL])
    g_ffn1 = din("g_ffn1", [NL, D_MODEL])
    g_mix = din("g_mix", [NL, D_MODEL])
    g_ffn2 = din("g_ffn2", [NL, D_MODEL])
    g_qa = din("g_qa", [NL, 64])
    g_ka = din("g_ka", [NL, 64])
    g_qb = din("g_qb", [NL, 64])
    g_kb = din("g_kb", [NL, 64])
    g_sub = din("g_sub", [NL, 128])
    rel_bias = din("rel_bias", [NL, 8, 257])
    lam_in = {k: din(k, [NL, 64]) for k in ("lam_q1", "lam_k1", "lam_q2", "lam_k2")}
    c_ident = din("c_ident", [128, 128])
    c_anti = din("c_anti", [128, 128])
    c_blk = din("c_blk", [128, 128])
    c_rperm = din("c_rperm", [128, 128])
    c_cosp = din("c_cosp", [128, SEQ])
    c_sinp = din("c_sinp", [128, SEQ])
    c_coss = din("c_coss", [128, 256])
    c_sins = din("c_sins", [128, 256])

    yp = dout("yp", [max(NPS, 1), SEQ, D_MODEL])
    ys = dout("ys", [max(NSS, 1) * 64, D_MODEL])
    pak = dout("pak", [NL, max(NPS, 1), 512, 512])
    pav = dout("pav", [NL, max(NPS, 1), 512, 512])
    pbk = dout("pbk", [NL, max(NPS, 1), SEQ, 512])
    pbv = dout("pbv", [NL, max(NPS, 1), SEQ, 512])
    sak = dout("sak", [NL, max(NSS, 1), 64, 512])
    sav = dout("sav", [NL, max(NSS, 1), 64, 512])
    sbk = dout("sbk", [NL, max(NSS, 1), 64, 512])
    sbv = dout("sbv", [NL, max(NSS, 1), 64, 512])

    cat = _weight_tiles(NL)
    wsc = nc.dram_tensor("wsc", [len(cat), 128, WT], BF16, kind="Internal").ap()
    tpad = nc.dram_tensor("tpad", [NL * 8, 384], F32, kind="Internal").ap()

    A = nc.alloc_sbuf_tensor
    ident = A("ident", [128, 128], F32)
    identb = A("identb", [128, 128], BF16)
    antib = A("antib", [128, 128], BF16)
    onesb = A("onesb", [128, 128], BF16)
    blkb = A("blkb", [128, 128], BF16)
    rpermb = A("rpermb", [128, 128], BF16)
    maskB = A("maskB", [128, 64], BF16)
    maskA0 = A("maskA0", [128, 128], BF16)
    epsb = A("epsb", [128, 1], F32)
    lneps = A("lneps", [128, 1], F32)
    gx = A("gx", [128, 3 * NL * 8], F32)
    gh = A("gh", [128, 4 * NL], F32)
    gsub = A("gsub", [128, NL], F32)
    nlam = A("nlam", [128, NL], F32)
    bconst = A("bconst", [128, NL * 8], F32)
    btile = A("btile", [128, NL * 8 * 2, 128], BF16)
    xT = A("xT", [128, 8, 512], F32)
    hT = A("hT", [128, 8, 512], BF16)
    BIG = A("BIG", [128, NFC * 512], BF16)
    wring = A("wring", [128, NSLOT, WT], BF16)
    kbT = [A("kbT%d" % l, [128, 4, 2048], BF16) for l in range(2)]
    vbt = [A("vb%d" % l, [128, 16, 512], BF16) for l in range(2)]
    kaT = [A("kaT%d" % l, [128, 4, 1024], BF16) for l in range(2)]
    vat = [A("va%d" % l, [128, 8, 512], BF16) for l in range(2)]
    eT = A("eT", [128, 6, 512], BF16)
    tA = [A("tA%d" % i, [128, 512], F32) for i in range(2)]
    tB = [A("tB%d" % i, [128, 512], F32) for i in range(2)]
    tC = [A("tC%d" % i, [128, 512], F32) for i in range(2)]
    tz = [A("tz%d" % i, [128, 512], BF16) for i in range(2)]
    tq = [A("tq%d" % i, [128, 512], BF16) for i in range(2)]
    cosT = A("cosT", [128, 512], F32)
    sinT = A("sinT", [128, 512], F32)
    stg = A("stg", [128, 4, 512], F32)
    lamt = A("lamt", [128, 4, 64], F32)
    lamr = A("lamr", [128, 4], F32)
    rstdT = A("rstdT", [128, 512], F32)
    rtok = A("rtok", [128, 4], F32)
    epsv = lamt[:].rearrange("p a b -> p (a b)").bitcast(BF16)[:, 0:512]

    aT = BIG[:].rearrange("p (c t) -> p c t", c=NFC)
    xstg = BIG[:, 0:8192].bitcast(F32).rearrange("p (a b) -> p a b", a=4)
    sq8 = BIG[:, 0:4096].rearrange("p (c t) -> p c t", c=8)
    qpad = BIG[:, 0:8192].rearrange("p (c t) -> p c t", c=16)

    psb = [nc.alloc_psum_tensor("psb%d" % b, [128, 512], F32) for b in range(8)]
    rot_state = [0]

    def rot():
        b = rot_state[0] % 4
        rot_state[0] += 1
        return b

    def PS(b):
        return ("ps", b)

    def Bk(lo, hi):
        return [("B", c) for c in range(lo, hi)]

    ncd = nc.allow_non_contiguous_dma

    def dma(q, out, in_, reads, writes, key, nonc=False):
        E = nc.sync if q == "sp" else (nc.gpsimd if q == "pool" else nc.scalar)

        def emit():
            if nonc:
                with ncd(reason="tiny setup transfer"):
                    return E.dma_start(out=out, in_=in_)
            return E.dma_start(out=out, in_=in_)
        P.add(q, emit, reads, writes, dma_key=key)

    ukey = [0]

    def once_key():
        ukey[0] += 1
        return ("once", ukey[0] % 8)

    def dbg(name, ap, shape, reads):
        if debug is None or name not in debug:
            return
        o = dout("dbg_" + name, shape, ap.dtype)
        dbg_outs[name] = o
        dma("sp", o, ap, reads, [], ("dbg", name))

    def load_cast(dst_b, src):
        dma("sp", tA[0][:, 0:128], src, [], [("t", 0, "A")], ("once", 0))
        P.add("dve", lambda: nc.vector.tensor_copy(out=dst_b[:], in_=tA[0][:, 0:128]),
              [("t", 0, "A")], ["c"])

    dma("sp", ident[:], c_ident[:, :], [], ["c"], ("once", 1))
    load_cast(identb, c_ident[:, :])
    load_cast(antib, c_anti[:, :])
    load_cast(blkb, c_blk[:, :])
    load_cast(rpermb, c_rperm[:, :])
    P.add("pool", lambda: nc.gpsimd.memset(onesb[:], 1.0), [], ["c"])
    P.add("pool", lambda: nc.gpsimd.memset(epsb[:], EPS), [], ["c"])
    P.add("pool", lambda: nc.gpsimd.memset(lneps[:], float(math.log(EPS))), [], ["c"])
    P.add("pool", lambda: nc.gpsimd.memset(maskB[:], 0.0), [], ["c"])
    P.add("pool", lambda: nc.gpsimd.memset(maskB[64:128, :], -30000.0), [], ["c"])
    P.add("pool", lambda: nc.gpsimd.memset(maskA0[:], 0.0), [], ["c"])
    P.add("pool", lambda: nc.gpsimd.memset(maskA0[0:64, 64:128], -30000.0), [], ["c"])
    for wi, gsrc in enumerate((g_ffn1, g_mix, g_ffn2)):
        for l in range(NL):
            o = (wi * NL + l) * 8
            dma("sp", gx[:, o:o + 8], gsrc[l].rearrange("(c p) -> p c", p=128), [], ["g"], ("once", 2), nonc=True)
    for gi, gsrc in enumerate((g_qa, g_ka, g_qb, g_kb)):
        for l in range(NL):
            for hf in range(2):
                dma("sp", gh[hf * 64:(hf + 1) * 64, gi * NL + l:gi * NL + l + 1],
                    gsrc[l].rearrange("(p o) -> p o", o=1), [], ["g"], ("once", 3), nonc=True)
    for l in range(NL):
        lam_init = 0.8 - 0.6 * math.exp(-0.3 * l)
        dma("sp", tA[1][:, l:l + 1], g_sub[l].rearrange("(p o) -> p o", o=1), [], [("t", 1, "A")], ("once", 4), nonc=True)
        P.add("dve", lambda l=l, li=lam_init: nc.vector.tensor_scalar(
            out=gsub[:, l:l + 1], in0=tA[1][:, l:l + 1], scalar1=float(1.0 - li), scalar2=None, op0=ALU.mult),
            [("t", 1, "A")], ["g"])
        for k, nm in enumerate(("lam_q1", "lam_k1", "lam_q2", "lam_k2")):
            dma("sp", lamt[:, k, :], lam_in[nm][l:l + 1, :].partition_broadcast(128), [], ["lamt"], ("once", 5))
        for k in range(2):
            P.add("dve", lambda k=k: nc.vector.tensor_tensor(out=lamt[:, 2 * k, :], in0=lamt[:, 2 * k, :],
                                                             in1=lamt[:, 2 * k + 1, :], op=ALU.mult),
                  ["lamt"], ["lamt"])
            P.add("dve", lambda k=k: nc.vector.reduce_sum(out=lamr[:, k:k + 1], in_=lamt[:, 2 * k, :],
                                                          axis=mybir.AxisListType.X), ["lamt"], ["lamr"])
        P.add("act", lambda: nc.scalar.activation(out=lamr[:, 2:4], in_=lamr[:, 0:2], func=AF.Exp), ["lamr"], ["lamr"])
        P.add("dve", lambda l=l, li=lam_init: nc.vector.scalar_tensor_tensor(
            out=nlam[:, l:l + 1], in0=lamr[:, 3:4], scalar=float(-li), in1=lamr[:, 2:3], op0=ALU.add, op1=ALU.subtract),
            ["lamr"], ["g"])

    def setup_bias(l):
        dma("sp", bconst[:, l * 8:(l + 1) * 8],
            bass.AP(tensor=rel_bias.tensor, offset=l * 8 * 257 + 256, ap=[[0, 128], [257, 8]]),
            [], ["bc"], ("once", 6), nonc=True)
        dma("sp", tB[0][0:8, 0:257], rel_bias[l], [], [("t", 0, "B")], ("once", 7))
        P.add("dve", lambda: nc.vector.tensor_copy(out=tB[0][0:8, 257:384],
                                                   in_=tB[0][0:8, 256:257].to_broadcast([8, 127])),
              [("t", 0, "B")], [("t", 0, "B")])
        dma("sp", tpad[l * 8:(l + 1) * 8, :], tB[0][0:8, 0:384], [("t", 0, "B")], ["tpad"], ("tpadw", l))
        for h in range(8):
            for di, Dv in enumerate((128, 0)):
                s = (h * 2 + di) % 2
                src = bass.AP(tensor=tpad.tensor, offset=(l * 8 + h) * 384 + Dv + 1, ap=[[1, 128], [1, 128]])
                dma("pool", tC[s][:, 0:128], src, ["tpad"], [("t", s, "C")], ("btl", s))
                bi = (l * 8 + h) * 2 + di
                P.add("dve", lambda s=s, bi=bi, l=l, h=h: nc.vector.tensor_scalar(
                    out=btile[:, bi, :], in0=tC[s][:, 0:128], scalar1=bconst[:, l * 8 + h:l * 8 + h + 1],
                    scalar2=8.0, op0=ALU.subtract, op1=ALU.mult), [("t", s, "C"), "bc"], ["bt"])
                if di == 1:
                    P.add("dve", lambda bi=bi: nc.vector.memset(btile[0:64, bi, 0:64], -30000.0), [], ["bt"])

    cat_list = list(cat.items())
    cv_done = [0]
    CV_LEAD = 32
    CV_KEYS = 8

    def convert_upto(n_hi):
        while cv_done[0] < min(n_hi, len(cat_list)):
            n = cv_done[0]
            cv_done[0] += 1
            (l, name, idx), tid = cat_list[n]
            if name[0] in "gu" or name in ("in", "out"):
                src = W[name][l][:, idx * 256:(idx + 1) * 256].rearrange("(kc p) f -> p kc f", p=128)
                dst = wsc[tid].rearrange("p (kc f) -> p kc f", kc=8)
            else:
                half, t6 = idx // 6, idx % 6
                nfc = min(4, NFC - 4 * t6)
                src = W[name][l][t6 * 512:t6 * 512 + nfc * 128, half * 512:(half + 1) * 512].rearrange(
                    "(fc p) d -> p fc d", p=128)
                dst = wsc[tid][:, 0:nfc * 512].rearrange("p (fc d) -> p fc d", fc=nfc)
            dma("pool", dst, src, [], [("ws", tid), ("cvk", n % CV_KEYS)], ("cv", n % CV_KEYS))

    wcount = [0]

    def wload(l, name, idx, nel=WT):
        tid = cat[(l, name, idx)]
        convert_upto(tid + 1 + CV_LEAD)
        s = wcount[0] % NSLOT
        wcount[0] += 1
        dma("sp", wring[:, s, 0:nel], wsc[tid][:, 0:nel], [("ws", tid)], [("w", s)], ("w", s))
        return s

    nstate = {"bank": None, "pend": []}
    sqbuf = [(tq[0], ("t", 0, "q")), (tq[1], ("t", 1, "q")), (tz[0], ("t", 0, "z")), (tz[1], ("t", 1, "z"))]

    def norm_flush():
        for (b, dc, T) in nstate["pend"]:
            sb, stok = sqbuf[dc % 4]
            P.add("pe", lambda: nc.tensor.matmul(psb[b][:, 0:T], onesb[:], sb[:, 0:T],
                                                 start=(dc == 0), stop=(dc == 7)), [stok, "c"], [PS(b)])
        nstate["pend"] = []

    def norm_feed(dc, T, gcol, bank=None):
        if gcol is None:
            return
        if dc == 0:
            nstate["bank"] = rot() if bank is None else bank
        b = nstate["bank"]
        sb, stok = sqbuf[dc % 4]
        P.add("act", lambda: nc.scalar.activation(out=sb[:, 0:T], in_=xT[:, dc, 0:T], func=AF.Square),
              [("xT", dc)], [stok])
        P.add("dve", lambda: nc.vector.tensor_scalar(
            out=hT[:, dc, 0:T], in0=xT[:, dc, 0:T], scalar1=gx[:, gcol + dc:gcol + dc + 1], scalar2=None,
            op0=ALU.mult), [("xT", dc), "g"], [("hT", dc)])
        nstate["pend"].append((b, dc, T))

    def norm_finish(T, for_mix=False, blocks=None):
        norm_flush()
        b = nstate["bank"]
        P.add("act", lambda: nc.scalar.activation(out=rstdT[:, 0:T], in_=psb[b][:, 0:T], func=AF.Ln,
                                                  scale=1.0 / D_MODEL, bias=epsb[:, 0:1]), ["c"], [PS(b), "rstd"])
        if for_mix:
            P.add("act", lambda: nc.scalar.activation(out=epsv[:, 0:T], in_=rstdT[:, 0:T], func=AF.Exp,
                                                      bias=lneps[:, 0:1]), ["rstd", "c"], ["epsv", "lamt"])
        P.add("act", lambda: nc.scalar.activation(out=rstdT[:, 0:T], in_=rstdT[:, 0:T], func=AF.Exp, scale=-0.5),
              [], ["rstd"])
        if for_mix:
            nstate["rtok"] = (T, blocks)

    def rtok_part():
        T, blocks = nstate["rtok"]
        if True:
            bt = rot8()
            for k, (c0, n) in enumerate(blocks):
                P.add("pe", lambda k=k, c0=c0, n=n: nc.tensor.transpose(
                    psb[bt][0:n, k * 128:(k + 1) * 128], rstdT[:, c0:c0 + n], ident[:]), ["rstd", "c"], [PS(bt)])
            nb = len(blocks)
            n0 = blocks[0][1]
            P.add("dve", lambda: nc.vector.tensor_copy(
                out=rtok[0:n0, 0:nb], in_=psb[bt][0:n0, 0:nb * 128].rearrange("p (k f) -> p k f", k=nb)[:, :, 0]),
                [], [PS(bt), "rtok"])

    def ffn(l, f, T, pre=None):
        if f == 1:
            gnext = (1 * NL + l) * 8
        else:
            gnext = (0 * NL + l + 1) * 8 if l + 1 < NL else None
        HT = [("hT", c) for c in range(8)]
        for ft in range(11):
            sg = wload(l, "g%d" % f, ft)
            su = wload(l, "u%d" % f, ft)
            wg = wring[:, sg, :].rearrange("p (kc f) -> p kc f", kc=8)
            wu = wring[:, su, :].rearrange("p (kc f) -> p kc f", kc=8)
            for fl in range(2):
                fc = 2 * ft + fl
                bg, bu = rot(), rot()
                for kc in range(8):
                    P.add("pe", lambda kc=kc, bg=bg, wg=wg, fl=fl: nc.tensor.matmul(
                        psb[bg][:, 0:T], wg[:, kc, fl * 128:(fl + 1) * 128], hT[:, kc, 0:T],
                        start=(kc == 0), stop=(kc == 7)), [("w", sg), ("hT", kc)], [PS(bg)])
                for kc in range(8):
                    P.add("pe", lambda kc=kc, bu=bu, wu=wu, fl=fl: nc.tensor.matmul(
                        psb[bu][:, 0:T], wu[:, kc, fl * 128:(fl + 1) * 128], hT[:, kc, 0:T],
                        start=(kc == 0), stop=(kc == 7)), [("w", su), ("hT", kc)], [PS(bu)])
                ts = fc % 2
                if fc == 0:
                    norm_finish(T)
                P.add("dve", lambda bg=bg, ts=ts: nc.vector.tensor_tensor(
                    out=tA[ts][:, 0:T], in0=psb[bg][:, 0:T], in1=rstdT[:, 0:T], op=ALU.mult),
                    ["rstd"], [PS(bg), ("t", ts, "A")])
                P.add("act", lambda ts=ts: nc.scalar.activation(out=tA[ts][:, 0:T], in_=tA[ts][:, 0:T],
                                                                func=AF.Silu), [], [("t", ts, "A")])
                P.add("dve", lambda bu=bu, ts=ts, fc=fc: nc.vector.tensor_tensor(
                    out=aT[:, fc, 0:T], in0=psb[bu][:, 0:T], in1=tA[ts][:, 0:T], op=ALU.mult),
                    [("t", ts, "A")], [PS(bu), ("B", fc)])
        if pre is not None:
            pre()
        for half in range(2):
            for t6 in range(6):
                nfc = min(4, NFC - 4 * t6)
                sd = wload(l, "d%d" % f, half * 6 + t6, nel=nfc * 512)
                wd = wring[:, sd, 0:nfc * 512].rearrange("p (fc d) -> p fc d", fc=nfc)
                if half == 1 and t6 == 2:
                    norm_flush()
                for fl in range(nfc):
                    fc = 4 * t6 + fl
                    for dcl in range(4):
                        ab = (4 + dcl) if half == 0 else dcl
                        P.add("pe", lambda fl=fl, fc=fc, dcl=dcl, wd=wd, ab=ab: nc.tensor.matmul(
                            psb[ab][:, 0:T], wd[:, fl, dcl * 128:(dcl + 1) * 128], aT[:, fc, 0:T],
                            start=(fc == 0), stop=(fc == NFC - 1)), [("w", sd), ("B", fc)], [PS(ab)])
            for dcl in range(4):
                dc = half * 4 + dcl
                tb_ = tB[dcl % 2]
                ab = (4 + dcl) if half == 0 else dcl
                P.add("dve", lambda dcl=dcl, tb_=tb_, ab=ab: nc.vector.tensor_tensor(
                    out=tb_[:, 0:T], in0=psb[ab][:, 0:T], in1=rstdT[:, 0:T], op=ALU.mult),
                    ["rstd"], [PS(ab), ("t", dcl % 2, "B")])
                P.add("dve", lambda dc=dc, tb_=tb_: nc.vector.scalar_tensor_tensor(
                    out=xT[:, dc, 0:T], in0=tb_[:, 0:T], scalar=0.5, in1=xT[:, dc, 0:T],
                    op0=ALU.mult, op1=ALU.add), [("t", dcl % 2, "B")], [("xT", dc)])
            for dcl in range(4):
                norm_feed(half * 4 + dcl, T, gnext, bank=4)

    stgx = stg[:].rearrange("p a b -> p (a b)").rearrange("p (t d) -> p t d", t=2)
    XPIECE = {2: [(tA[0], ("t", 0, "A")), (tA[1], ("t", 1, "A"))],
              3: [(tC[0], ("t", 0, "C")), (tC[1], ("t", 1, "C"))]}

    def prefetch_x(src_rows, T):
        ntt = T // 128
        dma("sp", stgx[:, 0:2, :], src_rows[0:256, :].rearrange("(t p) d -> p t d", p=128), [],
            [("stg", k) for k in range(4)], "xin0")
        for tt in range(2, ntt):
            for hf in range(2):
                buf, tok = XPIECE[tt][hf]
                dma("sp", buf[:, :], src_rows[tt * 128:(tt + 1) * 128, hf * 512:(hf + 1) * 512], [], [tok],
                    "xin%d" % (1 + (tt - 2) * 2 + hf))

    def load_x(T):
        ntt = T // 128
        for dc in range(8):
            b = rot()
            for tt in range(ntt):
                if tt < 2:
                    src, toks = stgx[:, tt, dc * 128:(dc + 1) * 128], [("stg", 2 * tt), ("stg", 2 * tt + 1)]
                else:
                    buf, tok = XPIECE[tt][dc // 4]
                    src, toks = buf[:, (dc % 4) * 128:(dc % 4 + 1) * 128], [tok]
                P.add("pe", lambda tt=tt, src=src: nc.tensor.transpose(
                    psb[b][:, tt * 128:(tt + 1) * 128], src, ident[:]), toks + ["c"], [PS(b)])
            norm_flush()
            if dc % 2 == 0:
                P.add("act", lambda: nc.scalar.copy(out=xT[:, dc, 0:T], in_=psb[b][:, 0:T]),
                      [], [PS(b), ("xT", dc)])
            else:
                P.add("dve", lambda: nc.vector.tensor_copy(out=xT[:, dc, 0:T], in_=psb[b][:, 0:T]),
                      [], [PS(b), ("xT", dc)])
            norm_feed(dc, T, 0, bank=7)

    def store_y(dst_rows, T):
        ntt = T // 128
        for tt in range(ntt):
            for hf in range(2):
                b = rot()
                for d4 in range(4):
                    dc = hf * 4 + d4
                    P.add("pe", lambda dc=dc, d4=d4, tt=tt, b=b: nc.tensor.transpose(
                        psb[b][:, d4 * 128:(d4 + 1) * 128], xT[:, dc, tt * 128:(tt + 1) * 128], ident[:]),
                        [("xT", dc), "c"], [PS(b)])
                if hf == 0:
                    P.add("act", lambda tt=tt, hf=hf, b=b: nc.scalar.copy(
                        out=xstg[:, tt, hf * 512:(hf + 1) * 512], in_=psb[b][:, :]), [], [PS(b)] + Bk(0, 16))
                else:
                    P.add("dve", lambda tt=tt, hf=hf, b=b: nc.vector.tensor_copy(
                        out=xstg[:, tt, hf * 512:(hf + 1) * 512], in_=psb[b][:, :]), [], [PS(b)] + Bk(0, 16))
        dma("pool", dst_rows.rearrange("(t p) d -> p t d", p=128), xstg[:, 0:ntt, :], Bk(0, 16), [], "yst")

    def store_rows(src_f32, ntok_p, dst, reads, slot):
        dma("pool", dst, src_f32, reads, [], ("stg", slot))

    rot8_state = [0]

    def rot8():
        b = rot8_state[0] % 8
        rot8_state[0] += 1
        return b

    def mix(l, grp):
        T = grp["T"]
        ntt = T // 128
        isP = grp["kind"] == "p"
        if isP:
            vblocks = [(tb * 128, 128) for tb in range(ntt)]
        else:
            vblocks = [(tb * 64, 64) for tb in range(grp["nseq"])]
        norm_finish(T, for_mix=True, blocks=vblocks)
        if isP:
            i = grp["i"]
            sq_ = grp["seq"]
            tok0 = i * 512
            dma("sp", cosT[:, 0:T], c_cosp[:, i * 512:(i + 1) * 512], [], ["cs"], "cs")
            dma("sp", sinT[:, 0:T], c_sinp[:, i * 512:(i + 1) * 512], [], ["cs"], "cs")
        else:
            dma("sp", cosT[:, 0:T], c_coss[:, 0:T], [], ["cs"], "cs")
            dma("sp", sinT[:, 0:T], c_sins[:, 0:T], [], ["cs"], "cs")
        P.add("pool", lambda: nc.gpsimd.memset(BIG[:, 0:8192], 0.0), [], Bk(0, 16))
        oth = 1 - l
        wtl = {}

        def get_w(gidx):
            if gidx not in wtl:
                s0 = wload(l, "in", 2 * gidx)
                s1 = wload(l, "in", 2 * gidx + 1)
                wtl[gidx] = ([wring[:, s0, :].rearrange("p (kc f) -> p kc f", kc=8),
                              wring[:, s1, :].rearrange("p (kc f) -> p kc f", kc=8)], [s0, s1])
            return wtl[gidx]

        GIDX = {"qa": 0, "ka": 1, "va": 2, "qb": 3, "kb": 4, "vb": 5}
        jobs = []

        def v_job(kind, tb):
            isBv = kind == "vb"
            bsz = 128 if isP else 64

            def Pst():
                wv, ws = get_w(GIDX[kind])
                b = rot8()
                for half in range(2):
                    for kc in range(8):
                        P.add("pe", lambda kc=kc, half=half: nc.tensor.matmul(
                            psb[b][0:bsz, half * 256:(half + 1) * 256], hT[:, kc, tb * bsz:(tb + 1) * bsz],
                            wv[half][:, kc, :], start=(kc == 0), stop=(kc == 7)),
                            [("w", ws[half]), ("hT", kc)], [PS(b)])
                slot = tb % 4
                P.add("act", lambda: nc.scalar.activation(out=stg[0:bsz, slot, :], in_=psb[b][0:bsz, :], func=AF.Copy,
                                                          scale=rtok[0:bsz, tb:tb + 1]), ["rtok"], [PS(b), ("stg", slot)])
                if isP:
                    kt = (i * 4 + tb)
                    if isBv:
                        dstt, tok = vbt[l][:, kt, :], ("vb", l, kt)
                    else:
                        dstt, tok = vat[l][:, kt % 8, :], ("va", l, kt % 8)
                    P.add("pool", lambda: nc.gpsimd.tensor_copy(out=dstt, in_=stg[:, slot, :]), [("stg", slot)], [tok])
                    if isBv:
                        store_rows(stg[:, slot, :], 128, pbv[l, sq_, tok0 + tb * 128:tok0 + (tb + 1) * 128, :],
                                   [("stg", slot)], slot)
                    elif i == 3:
                        store_rows(stg[:, slot, :], 128, pav[l, sq_, tb * 128:(tb + 1) * 128, :],
                                   [("stg", slot)], slot)
                else:
                    if isBv:
                        dstt, tok = vbt[oth][0:64, tb, :], ("vb", oth, tb)
                    else:
                        dstt, tok = vat[oth][0:64, tb, :], ("va", oth, tb)
                    P.add("pool", lambda: nc.gpsimd.tensor_copy(out=dstt, in_=stg[0:64, slot, :]), [("stg", slot)], [tok])
                    store_rows(stg[0:64, slot, :], 64, (sbv if isBv else sav)[l, tb, :, :], [("stg", slot)], slot)
            return [Pst, None, None, None]

        def qk_job(kind, c, ts):
            isB = kind[1] == "b"
            gcol = {"qa": 0, "ka": 1, "qb": 2, "kb": 3}[kind] * NL + l
            GC = gh[:, gcol:gcol + 1]
            need_rows = (kind == "kb") or (kind == "ka" and (not isP or grp["i"] == 3))
            st = {}
            FT = ("t", ts, "B")
            if kind == "ka":
                if isP:
                    dstk, ktok = kaT[l][:, c, (i % 2) * 512:(i % 2) * 512 + T], ("ka", l, c)
                else:
                    dstk, ktok = kaT[oth][:, c, 0:T], ("ka", oth, c)
            elif kind == "kb":
                if isP:
                    dstk, ktok = kbT[l][:, c, tok0:tok0 + T], ("kb", l, c)
                else:
                    dstk, ktok = kbT[oth][:, c, 0:T], ("kb", oth, c)
            qbase = 0 if kind == "qa" else 8

            def Pst():
                wv, ws = get_w(GIDX[kind])
                half, cl = c // 2, c % 2
                bz = rot8()
                st["bz"] = bz
                for kc in range(8):
                    P.add("pe", lambda kc=kc: nc.tensor.matmul(
                        psb[bz][:, 0:T], wv[half][:, kc, cl * 128:(cl + 1) * 128], hT[:, kc, 0:T],
                        start=(kc == 0), stop=(kc == 7)), [("w", ws[half]), ("hT", kc)], [PS(bz)])
                P.add("act", lambda: nc.scalar.activation(out=tq[ts][:, 0:T], in_=psb[bz][:, 0:T], func=AF.Square),
                      [], [PS(bz), ("t", ts, "q")])
                if isB:
                    P.add("act", lambda: nc.scalar.activation(out=tz[ts][:, 0:T], in_=psb[bz][:, 0:T], func=AF.Copy,
                                                              scale=GC), ["g"], [PS(bz), ("t", ts, "z")])

            def Bst():
                bz = st["bz"]
                if isB:
                    P.add("dve", lambda: nc.vector.scalar_tensor_tensor(
                        out=tB[ts][:, 0:T], in0=psb[bz][:, 0:T], scalar=GC, in1=cosT[:, 0:T],
                        op0=ALU.mult, op1=ALU.mult), ["cs", "g"], [PS(bz), FT])
                bn = rot8()
                P.add("pe", lambda: nc.tensor.matmul(psb[bn][:, 0:T], blkb[:], tq[ts][:, 0:T], start=True, stop=False),
                      [("t", ts, "q"), "c"], [PS(bn)])
                P.add("pe", lambda: nc.tensor.matmul(psb[bn][:, 0:T], blkb[:], epsv[:, 0:T], start=False, stop=True),
                      ["epsv", "c"], [PS(bn)])
                P.add("act", lambda: nc.scalar.activation(out=tA[ts][:, 0:T], in_=psb[bn][:, 0:T], func=AF.Ln,
                                                          scale=1.0 / 64), [], [PS(bn), ("t", ts, "A")])
                P.add("act", lambda: nc.scalar.activation(out=tA[ts][:, 0:T], in_=tA[ts][:, 0:T], func=AF.Exp, scale=-0.5),
                      [], [("t", ts, "A")])
                if isB:
                    br = rot8()
                    P.add("pe", lambda: nc.tensor.matmul(psb[br][:, 0:T], rpermb[:], tz[ts][:, 0:T], start=True, stop=True),
                          [("t", ts, "z"), "c"], [PS(br)])
                    P.add("dve", lambda: nc.vector.tensor_tensor(out=tC[ts][:, 0:T], in0=psb[br][:, 0:T], in1=sinT[:, 0:T],
                                                                 op=ALU.mult), ["cs"], [PS(br), ("t", ts, "C")])
                elif kind == "qa":
                    for hf in range(2):
                        lo, hi = hf * 64, hf * 64 + 64
                        P.add("dve", lambda lo=lo, hi=hi, hf=hf: nc.vector.scalar_tensor_tensor(
                            out=qpad[lo:hi, 2 * c + hf, 0:T], in0=psb[bz][lo:hi, 0:T], scalar=gh[lo:hi, gcol:gcol + 1],
                            in1=tA[ts][lo:hi, 0:T], op0=ALU.mult, op1=ALU.mult),
                            [("t", ts, "A"), "g"], [PS(bz), ("B", 2 * c + hf)])
                elif need_rows:
                    P.add("dve", lambda: nc.vector.scalar_tensor_tensor(
                        out=tB[ts][:, 0:T], in0=psb[bz][:, 0:T], scalar=GC, in1=tA[ts][:, 0:T],
                        op0=ALU.mult, op1=ALU.mult), [("t", ts, "A"), "g"], [PS(bz), FT])
                    P.add("pool", lambda: nc.gpsimd.tensor_copy(out=dstk, in_=tB[ts][:, 0:T]), [FT], [ktok])
                else:
                    P.add("dve", lambda: nc.vector.scalar_tensor_tensor(
                        out=dstk, in0=psb[bz][:, 0:T], scalar=GC, in1=tA[ts][:, 0:T],
                        op0=ALU.mult, op1=ALU.mult), [("t", ts, "A"), "g"], [PS(bz), ktok])

            def Rst():
                P.add("dve", lambda: nc.vector.tensor_tensor(out=tB[ts][:, 0:T], in0=tB[ts][:, 0:T], in1=tC[ts][:, 0:T],
                                                             op=ALU.add), [("t", ts, "C")], [FT])
                if kind == "qb":
                    for m in range(2):
                        lo, hi = m * 64, m * 64 + 64
                        P.add("dve", lambda lo=lo, hi=hi, m=m: nc.vector.tensor_tensor(
                            out=qpad[lo:hi, 8 + 2 * c + m, 0:T], in0=tB[ts][lo:hi, 0:T], in1=tA[ts][lo:hi, 0:T],
                            op=ALU.mult), [FT, ("t", ts, "A")], [("B", 8 + 2 * c + m)])
                else:
                    P.add("dve", lambda: nc.vector.tensor_tensor(out=tB[ts][:, 0:T], in0=tB[ts][:, 0:T], in1=tA[ts][:, 0:T],
                                                                 op=ALU.mult), [("t", ts, "A")], [FT])
                    P.add("pool", lambda: nc.gpsimd.tensor_copy(out=dstk, in_=tB[ts][:, 0:T]), [FT], [ktok])

            def Xst():
                fin = tB[ts]
                bt = rot8()
                for tt in range(ntt):
                    P.add("pe", lambda tt=tt: nc.tensor.transpose(
                        psb[bt][:, tt * 128:(tt + 1) * 128], fin[:, tt * 128:(tt + 1) * 128], ident[:]),
                        [FT, "c"], [PS(bt)])
                P.add("dve", lambda: nc.vector.tensor_copy(
                    out=stg[:, 0:ntt, c * 128:(c + 1) * 128],
                    in_=psb[bt][:, 0:T].rearrange("p (t f) -> p t f", t=ntt)),
                    [], [PS(bt)] + [("stg", s_) for s_ in range(ntt)])
                if c == 3:
                    SR = [("stg", s_) for s_ in range(ntt)]
                    if isP:
                        if kind == "kb":
                            dst = pbk[l, sq_, tok0:tok0 + T, :].rearrange("(t p) f -> p t f", p=128)
                        else:
                            dst = pak[l, sq_, :, :].rearrange("(t p) f -> p t f", p=128)
                        dma("pool", dst, stg[:, 0:ntt, :], SR, [], ("stg", 0))
                    else:
                        dd = sbk if kind == "kb" else sak
                        for tt in range(ntt):
                            dst = dd[l, 2 * tt:2 * tt + 2, :, :].rearrange("s j f -> (s j) f")
                            dma("pool", dst, stg[:, tt, :], [("stg", tt)], [], ("stg", tt))
            return [Pst, Bst, Rst if isB else None, Xst if need_rows else None]

        nblk = ntt if isP else grp["nseq"]
        nq = 0
        for kind in ("qa", "va", "qb", "vb", "ka", "kb"):
            if kind[0] == "v":
                for tb in range(nblk):
                    jobs.append(v_job(kind, tb))
            else:
                for c in range(4):
                    jobs.append(qk_job(kind, c, nq % 2))
                    nq += 1
        nj = len(jobs)
        for step in range(nj + 3):
            for k in (3, 2, 0, 1):
                ci = step - k
                if 0 <= ci < nj and jobs[ci][k] is not None:
                    jobs[ci][k]()
            if step == 1:
                rtok_part()
        if isP:
            attn_prompt(l, grp)
        else:
            attn_sample(l, grp)
        for ot in range(4):
            so = wload(l, "out", ot)
            wo = wring[:, so, :].rearrange("p (ec f) -> p ec f", ec=8)
            for dcl in range(2):
                dc = 2 * ot + dcl
                b = rot()
                for n_, ec in enumerate((4, 5, 6, 7, 0, 1, 2, 3)):
                    oc = 8 + ec if ec < 4 else 12 + ec
                    P.add("pe", lambda n_=n_, ec=ec, oc=oc, dcl=dcl, b=b, wo=wo: nc.tensor.matmul(
                        psb[b][:, 0:T], wo[:, ec, dcl * 128:(dcl + 1) * 128], aT[:, oc, 0:T],
                        start=(n_ == 0), stop=(n_ == 7)), [("w", so), ("B", oc)], [PS(b)])
                norm_flush()
                P.add("dve", lambda dc=dc, b=b: nc.vector.tensor_tensor(
                    out=xT[:, dc, 0:T], in0=psb[b][:, 0:T], in1=xT[:, dc, 0:T], op=ALU.add),
                    [], [PS(b), ("xT", dc)])
                norm_feed(dc, T, (2 * NL + l) * 8, bank=7)

    ering = [0]

    def eslot():
        s = ering[0] % 6
        ering[0] += 1
        return s

    def zero_block(ap, etok):
        P.add("act", lambda: nc.scalar.mul(out=ap, in_=ap, mul=0.0), [], [etok])

    def attn_prompt(l, grp):
        i = grp["i"]
        T = 512
        nk = 4 * (i + 1)
        LA = 3
        tiles = [(h, m, j) for h in range(4) for m in range(2) for j in range(nk)]
        bo = [4, 6]
        bZ = [5, 7]
        pend = {}
        deferred = []

        def b_norm_m(h, m):
            P.add("act", lambda: nc.scalar.activation(out=tA[m][:, 0:T], in_=psb[bZ[m]][:, 0:T], func=AF.Ln),
                  [], [PS(bZ[m]), ("t", m, "A")])
            P.add("act", lambda: nc.scalar.activation(out=tA[m][:, 0:T], in_=tA[m][:, 0:T], func=AF.Exp, scale=-1.0),
                  [], [("t", m, "A")])
            P.add("dve", lambda: nc.vector.tensor_tensor(
                out=tB[m][:, 0:T], in0=psb[bo[m]][:, 0:T], in1=tA[m][:, 0:T], op=ALU.mult),
                [("t", m, "A")], [PS(bo[m]), ("t", m, "B")])

        def b_combine(h):
            P.add("dve", lambda: nc.vector.scalar_tensor_tensor(
                out=tC[0][:, 0:T], in0=tB[1][:, 0:T], scalar=nlam[:, l:l + 1], in1=tB[0][:, 0:T],
                op0=ALU.mult, op1=ALU.add), [("t", 0, "B"), ("t", 1, "B"), "g"], [("t", 0, "C")])
            P.add("act", lambda: nc.scalar.activation(out=tq[0][:, 0:T], in_=tC[0][:, 0:T], func=AF.Square),
                  [("t", 0, "C")], [("t", 0, "q")])

        def b_subnorm(h):
            bn = rot()
            P.add("pe", lambda: nc.tensor.matmul(psb[bn][:, 0:T], onesb[:], tq[0][:, 0:T], start=True, stop=True),
                  [("t", 0, "q"), "c"], [PS(bn)])
            P.add("act", lambda: nc.scalar.activation(out=tC[1][:, 0:T], in_=psb[bn][:, 0:T], func=AF.Ln,
                                                      scale=1.0 / 128, bias=epsb[:, 0:1]), ["c"], [PS(bn), ("t", 1, "C")])
            P.add("act", lambda: nc.scalar.activation(out=tC[1][:, 0:T], in_=tC[1][:, 0:T], func=AF.Exp, scale=-0.5),
                  [], [("t", 1, "C")])
            P.add("dve", lambda: nc.vector.scalar_tensor_tensor(
                out=aT[:, 16 + h, 0:T], in0=tC[0][:, 0:T], scalar=gsub[:, l:l + 1], in1=tC[1][:, 0:T],
                op0=ALU.mult, op1=ALU.mult), [("t", 0, "C"), ("t", 1, "C"), "g"], [("B", 16 + h)])

        def b1(idx):
            h, m, j = tiles[idx]
            qv = qpad[:, 8 + 2 * h + m, :]
            QT = ("B", 8 + 2 * h + m)
            r = j - 4 * i
            q0 = 128 * r if r > 0 else 0
            bs = rot()
            P.add("pe", lambda: nc.tensor.matmul(
                psb[bs][:, q0:512], kbT[l][:, h, j * 128:(j + 1) * 128], qv[:, q0:512],
                start=True, stop=(r < 0)), [("kb", l, h), QT], [PS(bs)])
            if r >= 0:
                P.add("pe", lambda: nc.tensor.matmul(psb[bs][:, q0:q0 + 64], identb[:], maskB[:],
                                                     start=False, stop=True), ["c"], [PS(bs)])
            es = eslot()
            P.add("act", lambda: nc.scalar.activation(
                out=eT[:, es, q0:512], in_=psb[bs][:, q0:512], func=AF.Exp, scale=0.125),
                [], [PS(bs), ("e", es)])
            pend[idx] = (es, q0)

        def b2(idx):
            h, m, j = tiles[idx]
            es, q0 = pend.pop(idx)
            P.add("pe", lambda: nc.tensor.matmul(
                psb[bo[m]][:, q0:512], vbt[l][:, j, h * 128:(h + 1) * 128], eT[:, es, q0:512],
                start=(j == 0), stop=(j == nk - 1)), [("vb", l, j), ("e", es)], [PS(bo[m])])
            P.add("pe", lambda: nc.tensor.matmul(
                psb[bZ[m]][:, q0:512], onesb[:], eT[:, es, q0:512],
                start=(j == 0), stop=(j == nk - 1)), [("e", es), "c"], [PS(bZ[m])])
            if j == nk - 1:
                deferred.append([idx + 2, b_norm_m, (h, m)])
                if m == 1:
                    deferred.append([idx + 3, b_combine, (h,)])
                    deferred.append([idx + 6, b_subnorm, (h,)])

        def flush(upto):
            while deferred and deferred[0][0] <= upto:
                _, fn, args = deferred.pop(0)
                fn(*args)

        nt = len(tiles)
        for idx in range(nt + LA):
            if idx < nt:
                b1(idx)
            if idx - LA >= 0:
                b2(idx - LA)
                flush(idx - LA)
        units = [(h, u) for h in range(8) for u in range(4)]
        pendA = {}

        def a1(idx):
            h, u = units[idx]
            hp = h // 2
            bi0 = (l * 8 + h) * 2
            ug = 4 * i + u
            t4 = [t for t in range(4) if ug - 4 + t >= 0]
            BC = bconst[:, l * 8 + h:l * 8 + h + 1]
            qv = qpad[:, h, u * 128:(u + 1) * 128]
            j = ug
            kcol = ((j // 4) % 2) * 512 + (j % 4) * 128
            b2_ = rot()
            P.add("pe", lambda: nc.tensor.matmul(psb[b2_][:, 0:128], antib[:], btile[:, bi0 + 1, :],
                                                 start=True, stop=False), ["bt", "c"], [PS(b2_)])
            P.add("pe", lambda: nc.tensor.matmul(
                psb[b2_][:, 0:128], kaT[l][:, hp, kcol:kcol + 128], qv, start=False, stop=True),
                [("ka", l, hp), ("B", h)], [PS(b2_)])
            es2 = eslot()
            P.add("act", lambda: nc.scalar.activation(
                out=eT[:, es2, 0:128], in_=psb[b2_][:, 0:128], func=AF.Exp, scale=0.125, bias=BC),
                ["bc"], [PS(b2_), ("e", es2)])
            es = None
            if t4:
                b1_ = rot()
                for t in t4:
                    j = ug - 4 + t
                    kcol = ((j // 4) % 2) * 512 + (j % 4) * 128
                    first = True
                    if t == 3:
                        P.add("pe", lambda t=t: nc.tensor.matmul(
                            psb[b1_][:, t * 128:(t + 1) * 128], antib[:], btile[:, bi0, :], start=True, stop=False),
                            ["bt", "c"], [PS(b1_)])
                        first = False
                    if t == 0:
                        P.add("pe", lambda t=t: nc.tensor.matmul(
                            psb[b1_][:, 0:128], identb[:], maskA0[:], start=True, stop=False), ["c"], [PS(b1_)])
                        first = False
                    P.add("pe", lambda t=t, kcol=kcol, first=first: nc.tensor.matmul(
                        psb[b1_][:, t * 128:(t + 1) * 128], kaT[l][:, hp, kcol:kcol + 128], qv,
                        start=first, stop=True), [("ka", l, hp), ("B", h)], [PS(b1_)])
                es = eslot()
                c0 = t4[0] * 128
                P.add("act", lambda: nc.scalar.activation(
                    out=eT[:, es, c0:512], in_=psb[b1_][:, c0:512], func=AF.Exp, scale=0.125, bias=BC),
                    ["bc"], [PS(b1_), ("e", es)])
            pendA[idx] = [(4, es2, 0)] + [(t, es, t * 128) for t in t4]

        def a2(idx):
            h, u = units[idx]
            hp, hf = h // 2, h % 2
            ug = 4 * i + u
            bo_, bZ_ = (4, 5) if h % 2 == 0 else (6, 7)
            seq_mm = pendA.pop(idx)
            for n_, (t, e_, ecol) in enumerate(seq_mm):
                j = ug - 4 + t
                P.add("pe", lambda n_=n_, j=j, e_=e_, ecol=ecol: nc.tensor.matmul(
                    psb[bo_][:, u * 128:(u + 1) * 128], vat[l][:, j % 8, hp * 128:(hp + 1) * 128],
                    eT[:, e_, ecol:ecol + 128], start=(n_ == 0), stop=(n_ == len(seq_mm) - 1)),
                    [("va", l, j % 8), ("e", e_)], [PS(bo_)])
                P.add("pe", lambda n_=n_, e_=e_, ecol=ecol: nc.tensor.matmul(
                    psb[bZ_][:, u * 128:(u + 1) * 128], onesb[:], eT[:, e_, ecol:ecol + 128],
                    start=(n_ == 0), stop=(n_ == len(seq_mm) - 1)), [("e", e_), "c"], [PS(bZ_)])
            if u == 3:
                deferred.append([idx + 1, a_finish, (h,)])

        def a_finish(h):
            hp, hf = h // 2, h % 2
            bo_, bZ_ = (4, 5) if h % 2 == 0 else (6, 7)
            ts = h % 2
            lo, hi = hf * 64, hf * 64 + 64
            P.add("act", lambda: nc.scalar.activation(
                out=tA[ts][lo:hi, 0:T], in_=psb[bZ_][lo:hi, 0:T], func=AF.Ln), [], [PS(bZ_), ("t", ts, "A")])
            P.add("act", lambda: nc.scalar.activation(
                out=tA[ts][lo:hi, 0:T], in_=tA[ts][lo:hi, 0:T], func=AF.Exp, scale=-1.0), [], [("t", ts, "A")])
            P.add("dve", lambda: nc.vector.tensor_tensor(
                out=aT[lo:hi, 8 + hp, 0:T], in0=psb[bo_][lo:hi, 0:T], in1=tA[ts][lo:hi, 0:T], op=ALU.mult),
                [("t", ts, "A")], [PS(bo_), ("B", 8 + hp)])

        nu = len(units)
        for idx in range(nu + 1):
            if idx < nu:
                a1(idx)
            if idx >= 1:
                a2(idx - 1)
                if idx >= 2:
                    flush(idx - 1)
            if idx == 1:
                flush(10 ** 9)
        flush(10 ** 9)

    def attn_sample(l, grp):
        T = grp["T"]
        nseq = grp["nseq"]
        oth = 1 - l
        cstg = [vbt[oth][:, 4 + 4 * s_:8 + 4 * s_, :].rearrange("p a b -> p (a b)").bitcast(F32).rearrange(
            "p (k f) -> p k f", k=2) for s_ in range(3)]
        CT = [[("vb", oth, 4 + 4 * s_ + k) for k in range(4)] for s_ in range(3)]
        ev = [0]
        vc = [0]

        def evac(out, in_, reads, writes):
            ev[0] += 1
            if ev[0] % 2 == 0:
                P.add("act", lambda: nc.scalar.copy(out=out, in_=in_), reads, writes)
            else:
                P.add("dve", lambda: nc.vector.tensor_copy(out=out, in_=in_), reads, writes)

        def KBS(h, half):
            return ("kbs", l, h, half)

        pieces = []
        for s_ in range(nseq):
            for half in range(2):
                for pc in range(4):
                    pieces.append((s_, cbk, half * 4 + pc, True, True))
                for pc in range(4):
                    pieces.append((s_, cbv, half * 4 + pc, False, True))
            for pc in range(2):
                pieces.append((s_, cak, pc, True, False))
            for pc in range(2):
                pieces.append((s_, cav, pc, False, False))
        pstate = {"dma": 0, "conv": 0}

        def piece_dma():
            n = pstate["dma"]
            if n >= len(pieces):
                return
            pstate["dma"] += 1
            s_, csrc, pc, isK, isBc = pieces[n]
            sl = n % 3
            dma("sp", cstg[sl], csrc[l, s_, pc * 256:(pc + 1) * 256, :].rearrange("(k p) f -> p k f", p=128),
                [], CT[sl], ("cst", sl))

        def piece_conv():
            n = pstate["conv"]
            if n >= len(pieces):
                return
            while pstate["dma"] < min(n + 3, len(pieces)):
                piece_dma()
            pstate["conv"] += 1
            s_, csrc, pc, isK, isBc = pieces[n]
            sl = n % 3
            for k in range(2):
                kt = pc * 2 + k
                if isK:
                    b = rot()
                    for c in range(4):
                        P.add("pe", lambda c=c: nc.tensor.transpose(
                            psb[b][:, c * 128:(c + 1) * 128], cstg[sl][:, k, c * 128:(c + 1) * 128], ident[:]),
                            CT[sl] + ["c"], [PS(b)])
                    if isBc:
                        dst = kbT[l][:, :, kt * 128:(kt + 1) * 128]
                        toks = [KBS(c, kt // 8) for c in range(4)]
                        if s_ == 0:
                            toks = toks + [("kb", l, c) for c in range(4)]
                    else:
                        dst = kaT[l][:, :, kt * 128:(kt + 1) * 128]
                        toks = [("ka", l, c) for c in range(4)]
                    evac(dst, psb[b][:, :].rearrange("p (c k) -> p c k", c=4), [], [PS(b)] + toks)
                else:
                    if isBc:
                        dst, tok = vbt[l][:, kt, :], ("vb", l, kt)
                    else:
                        dst, tok = vat[l][:, kt, :], ("va", l, kt)
                    vc[0] += 1
                    if vc[0] % 3 == 0:
                        P.add("pool", lambda: nc.gpsimd.tensor_copy(out=dst, in_=cstg[sl][:, k, :]), CT[sl], [tok])
                    elif vc[0] % 3 == 1:
                        P.add("dve", lambda: nc.vector.tensor_copy(out=dst, in_=cstg[sl][:, k, :]), CT[sl], [tok])
                    else:
                        P.add("act", lambda: nc.scalar.copy(out=dst, in_=cstg[sl][:, k, :]), CT[sl], [tok])

        for _ in range(8):
            piece_conv()

        sunits = [(h, m) for h in range(4) for m in range(2)]
        BO = [4, 6]
        BZ = [5, 7]

        for s in range(nseq):
            qc0 = s * 64
            for half in range(2):
                bo, bZ = BO[half], BZ[half]
                spend = {}

                def sb1(n):
                    h, m = sunits[n]
                    qv = qpad[:, 8 + 2 * h + m, qc0:qc0 + 64]
                    QT = ("B", 8 + 2 * h + m)
                    bs = rot()
                    for k8 in range(8):
                        j = half * 8 + k8
                        P.add("pe", lambda j=j, k8=k8: nc.tensor.matmul(
                            psb[bs][:, k8 * 64:(k8 + 1) * 64], kbT[l][:, h, j * 128:(j + 1) * 128], qv,
                            start=True, stop=True), [KBS(h, half), QT], [PS(bs)])
                    es = eslot()
                    P.add("act", lambda: nc.scalar.activation(
                        out=eT[:, es, :], in_=psb[bs][:, :], func=AF.Exp, scale=0.125), [], [PS(bs), ("e", es)])
                    es3 = None
                    if half == 1:
                        bs3 = rot()
                        P.add("pe", lambda: nc.tensor.matmul(
                            psb[bs3][0:64, 0:64], kbT[oth][:, h, qc0:qc0 + 64], qv, start=True, stop=True),
                            [("kb", oth, h), QT], [PS(bs3)])
                        es3 = eslot()
                        P.add("act", lambda: nc.scalar.activation(
                            out=eT[0:64, es3, 0:64], in_=psb[bs3][0:64, 0:64], func=AF.Exp, scale=0.125),
                            [], [PS(bs3), ("e", es3)])
                    spend[n] = (es, es3)

                def sb2(n):
                    h, m = sunits[n]
                    col = (2 * h + m) * 64
                    es, es3 = spend.pop(n)
                    nt = 9 if half == 1 else 8
                    for jj in range(nt):
                        if jj < 8:
                            j = half * 8 + jj
                            va_, e_, tokv = vbt[l][:, j, h * 128:(h + 1) * 128], eT[:, es, jj * 64:(jj + 1) * 64], ("vb", l, j)
                            on_ = onesb[:]
                            et = ("e", es)
                        else:
                            va_, e_, tokv = vbt[oth][0:64, s, h * 128:(h + 1) * 128], eT[0:64, es3, 0:64], ("vb", oth, s)
                            on_ = onesb[0:64, :]
                            et = ("e", es3)
                        P.add("pe", lambda jj=jj, va_=va_, e_=e_: nc.tensor.matmul(
                            psb[bo][:, col:col + 64], va_, e_, start=(jj == 0), stop=(jj == nt - 1)),
                            [tokv, et], [PS(bo)])
                        P.add("pe", lambda jj=jj, on_=on_, e_=e_: nc.tensor.matmul(
                            psb[bZ][:, col:col + 64], on_, e_, start=(jj == 0), stop=(jj == nt - 1)),
                            [et, "c"], [PS(bZ)])

                for n in range(len(sunits) + 1):
                    if n < len(sunits):
                        sb1(n)
                    if n >= 1:
                        sb2(n - 1)
                        piece_conv()
            P.add("dve", lambda: nc.vector.tensor_copy(out=tA[0][:, :], in_=psb[BZ[0]][:, :]), [], [PS(BZ[0]), ("t", 0, "A")])
            P.add("dve", lambda: nc.vector.tensor_tensor(out=tA[0][:, :], in0=psb[BZ[1]][:, :], in1=tA[0][:, :], op=ALU.add),
                  [], [PS(BZ[1]), ("t", 0, "A")])
            P.add("act", lambda: nc.scalar.activation(out=tA[0][:, :], in_=tA[0][:, :], func=AF.Ln), [], [("t", 0, "A")])
            P.add("act", lambda: nc.scalar.activation(out=tA[0][:, :], in_=tA[0][:, :], func=AF.Exp, scale=-1.0), [], [("t", 0, "A")])
            P.add("dve", lambda: nc.vector.tensor_copy(out=tB[0][:, :], in_=psb[BO[0]][:, :]), [], [PS(BO[0]), ("t", 0, "B")])
            P.add("dve", lambda: nc.vector.tensor_tensor(out=tB[0][:, :], in0=psb[BO[1]][:, :], in1=tB[0][:, :], op=ALU.add),
                  [], [PS(BO[1]), ("t", 0, "B")])
            P.add("dve", lambda: nc.vector.tensor_tensor(out=tB[0][:, :], in0=tB[0][:, :], in1=tA[0][:, :], op=ALU.mult),
                  [("t", 0, "A")], [("t", 0, "B")])
            onv = tB[0][:, :].rearrange("p (h m q) -> p h m q", h=4, m=2)
            obv = tC[0][:, 0:256].rearrange("p (h q) -> p h q", h=4)
            P.add("dve", lambda: nc.vector.scalar_tensor_tensor(
                out=obv, in0=onv[:, :, 1, :], scalar=nlam[:, l:l + 1], in1=onv[:, :, 0, :],
                op0=ALU.mult, op1=ALU.add), [("t", 0, "B"), "g"], [("t", 0, "C")])
            P.add("act", lambda: nc.scalar.activation(out=tq[0][:, 0:256], in_=tC[0][:, 0:256], func=AF.Square),
                  [("t", 0, "C")], [("t", 0, "q")])
            bn = rot()
            P.add("pe", lambda bn=bn: nc.tensor.matmul(psb[bn][:, 0:256], onesb[:], tq[0][:, 0:256], start=True, stop=True),
                  [("t", 0, "q"), "c"], [PS(bn)])
            P.add("act", lambda bn=bn: nc.scalar.activation(out=tC[1][:, 0:256], in_=psb[bn][:, 0:256], func=AF.Ln,
                                                            scale=1.0 / 128, bias=epsb[:, 0:1]), ["c"], [PS(bn), ("t", 1, "C")])
            P.add("act", lambda: nc.scalar.activation(out=tC[1][:, 0:256], in_=tC[1][:, 0:256], func=AF.Exp, scale=-0.5),
                  [], [("t", 1, "C")])
            P.add("dve", lambda: nc.vector.scalar_tensor_tensor(
                out=aT[:, 16:20, qc0:qc0 + 64], in0=obv, scalar=gsub[:, l:l + 1],
                in1=tC[1][:, 0:256].rearrange("p (h q) -> p h q", h=4), op0=ALU.mult, op1=ALU.mult),
                [("t", 0, "C"), ("t", 1, "C"), "g"], [("B", c) for c in range(16, 20)])
            bo, bZ = 6, 7
            for h in range(8):
                hp, hf = h // 2, h % 2
                bi0 = (l * 8 + h) * 2
                qv = qpad[:, h, qc0:qc0 + 64]
                col = h * 64
                b1 = rot()
                for t in range(4):
                    first = True
                    if t == 3:
                        P.add("pe", lambda b1=b1, t=t: nc.tensor.matmul(
                            psb[b1][:, t * 64:(t + 1) * 64], antib[:], btile[:, bi0, 0:64], start=True, stop=False),
                            ["bt", "c"], [PS(b1)])
                        first = False
                    P.add("pe", lambda b1=b1, t=t, first=first, qv=qv: nc.tensor.matmul(
                        psb[b1][:, t * 64:(t + 1) * 64], kaT[l][:, hp, t * 128:(t + 1) * 128], qv,
                        start=first, stop=True), [("ka", l, hp), ("B", h)], [PS(b1)])
                es = eslot()
                P.add("act", lambda b1=b1, es=es: nc.scalar.activation(
                    out=eT[:, es, 0:256], in_=psb[b1][:, 0:256], func=AF.Exp, scale=0.125,
                    bias=bconst[:, l * 8 + h:l * 8 + h + 1]), ["bc"], [PS(b1), ("e", es)])
                b2 = rot()
                P.add("pe", lambda b2=b2: nc.tensor.matmul(
                    psb[b2][0:64, 0:64], antib[64:128, 0:64], btile[64:128, bi0 + 1, 0:64], start=True, stop=False),
                    ["bt", "c"], [PS(b2)])
                P.add("pe", lambda b2=b2, qv=qv: nc.tensor.matmul(
                    psb[b2][0:64, 0:64], kaT[oth][:, hp, qc0:qc0 + 64], qv, start=False, stop=True),
                    [("ka", oth, hp), ("B", h)], [PS(b2)])
                es2 = eslot()
                P.add("act", lambda b2=b2, es2=es2: nc.scalar.activation(
                    out=eT[0:64, es2, 0:64], in_=psb[b2][0:64, 0:64], func=AF.Exp, scale=0.125,
                    bias=bconst[0:64, l * 8 + h:l * 8 + h + 1]), ["bc"], [PS(b2), ("e", es2)])
                for j in range(5):
                    if j < 4:
                        va_, e_, tokv, on_, et = vat[l][:, j, hp * 128:(hp + 1) * 128], eT[:, es, j * 64:(j + 1) * 64], ("va", l, j), onesb[:], ("e", es)
                    else:
                        va_, e_, tokv, on_, et = vat[oth][0:64, s, hp * 128:(hp + 1) * 128], eT[0:64, es2, 0:64], ("va", oth, s), onesb[0:64, :], ("e", es2)
                    P.add("pe", lambda j=j, va_=va_, e_=e_, col=col: nc.tensor.matmul(
                        psb[bo][:, col:col + 64], va_, e_, start=(j == 0), stop=(j == 4)), [tokv, et], [PS(bo)])
                    P.add("pe", lambda j=j, on_=on_, e_=e_, col=col: nc.tensor.matmul(
                        psb[bZ][:, col:col + 64], on_, e_, start=(j == 0), stop=(j == 4)), [et, "c"], [PS(bZ)])
                if h % 2 == 1:
                    piece_conv()
            for hf in range(2):
                lo, hi = hf * 64, hf * 64 + 64
                zv = psb[bZ][lo:hi, :].rearrange("p (hp f q) -> p hp f q", hp=4, f=2)[:, :, hf, :]
                ov = psb[bo][lo:hi, :].rearrange("p (hp f q) -> p hp f q", hp=4, f=2)[:, :, hf, :]
                tv = tA[1][lo:hi, 0:256].rearrange("p (hp q) -> p hp q", hp=4)
                P.add("act", lambda tv=tv, zv=zv: nc.scalar.activation(out=tv, in_=zv, func=AF.Ln), [], [PS(bZ), ("t", 1, "A")])
                P.add("act", lambda tv=tv: nc.scalar.activation(out=tv, in_=tv, func=AF.Exp, scale=-1.0), [], [("t", 1, "A")])
                P.add("dve", lambda tv=tv, ov=ov, lo=lo, hi=hi: nc.vector.tensor_tensor(
                    out=aT[lo:hi, 8:12, qc0:qc0 + 64], in0=ov, in1=tv, op=ALU.mult),
                    [("t", 1, "A")], [PS(bo)] + [("B", c) for c in range(8, 12)])

    groups = []
    for sq_ in range(NPS):
        for i in range(4):
            groups.append(dict(kind="p", seq=sq_, i=i, T=512))
    if NSS:
        groups.append(dict(kind="s", T=NSS * 64, nseq=NSS))
    def grp_rows(grp, dram_p, dram_s):
        if grp["kind"] == "p":
            return dram_p[grp["seq"], grp["i"] * 512:(grp["i"] + 1) * 512, :]
        return dram_s[0:grp["T"], :]

    prefetch_x(grp_rows(groups[0], xp, xs), groups[0]["T"])
    for gi_, grp in enumerate(groups):
        T = grp["T"]
        load_x(T)
        nxt = groups[gi_ + 1] if gi_ + 1 < len(groups) else None
        for l in range(NL):
            ffn(l, 1, T)
            if gi_ == 0:
                setup_bias(l)
            mix(l, grp)
            pre = None
            if l == NL - 1 and nxt is not None:
                pre = (lambda nxt=nxt: prefetch_x(grp_rows(nxt, xp, xs), nxt["T"]))
            ffn(l, 2, T, pre=pre)
        store_y(grp_rows(grp, yp, ys), T)
    nw = P.emit_all()
    return nc, dict(n_ops=len(P.ops), n_wait=nw, dbg=dbg_outs)


def _constants():
    ident = np.eye(128, dtype=np.float32)
    anti = np.ascontiguousarray(ident[::-1])
    blk = np.zeros((128, 128), np.float32)
    blk[:64, :64] = 1.0
    blk[64:, 64:] = 1.0
    rperm = np.zeros((128, 128), np.float32)
    for m in range(128):
        if (m % 64) < 32:
            rperm[m + 32, m] = -1.0
        else:
            rperm[m - 32, m] = 1.0
    inv = (10000.0 ** (-np.arange(32, dtype=np.float32) * 2.0 / 64)).astype(np.float32)
    fidx = np.arange(128) % 32

    def tab(pos):
        ang = pos.astype(np.float32)[None, :] * inv[fidx][:, None]
        return np.cos(ang).astype(np.float32), np.sin(ang).astype(np.float32)
    cosp, sinp = tab(np.arange(SEQ))
    coss, sins = tab(np.tile(PAST + np.arange(64), 4))
    return dict(c_ident=ident, c_anti=anti, c_blk=blk, c_rperm=rperm, c_cosp=cosp, c_sinp=sinp,
                c_coss=coss, c_sins=sins)


_CACHE = {}


def kernel(x_prompt, x_sample, cache_a_k, cache_a_v, cache_b_k, cache_b_v,
           g_ffn1, w1_gate, w1_up, w1_down, g_mix, w_in, g_qa, g_ka, g_qb, g_kb,
           rel_bias, lam_q1, lam_k1, lam_q2, lam_k2, g_sub, w_out,
           g_ffn2, w2_gate, w2_up, w2_down):
    f = lambda a: np.ascontiguousarray(np.asarray(a, dtype=np.float32))
    NL = 2
    if "nc" not in _CACHE:
        _CACHE["nc"] = build()[0]
    nc = _CACHE["nc"]
    consts = _constants()
    shared = dict(w1_gate=f(w1_gate), w1_up=f(w1_up), w1_down=f(w1_down), w_in=f(w_in), w_out=f(w_out),
                  w2_gate=f(w2_gate), w2_up=f(w2_up), w2_down=f(w2_down),
                  g_ffn1=f(g_ffn1), g_mix=f(g_mix), g_ffn2=f(g_ffn2), g_qa=f(g_qa), g_ka=f(g_ka),
                  g_qb=f(g_qb), g_kb=f(g_kb), g_sub=f(g_sub), rel_bias=f(rel_bias),
                  lam_q1=f(lam_q1), lam_k1=f(lam_k1), lam_q2=f(lam_q2), lam_k2=f(lam_k2))
    shared.update(consts)
    xpn, xsn = f(x_prompt), f(x_sample)
    cak, cav = f(cache_a_k).reshape(NL, 32, 512, 512), f(cache_a_v).reshape(NL, 32, 512, 512)
    cbk, cbv = f(cache_b_k).reshape(NL, 32, PAST, 512), f(cache_b_v).reshape(NL, 32, PAST, 512)
    in_maps = []
    for c in range(NCORES):
        m = dict(shared)
        m["xp"] = xpn[2 * c:2 * c + 2]
        m["xs"] = xsn[4 * c:4 * c + 4].reshape(256, D_MODEL)
        m["cak"] = np.ascontiguousarray(cak[:, 4 * c:4 * c + 4])
        m["cav"] = np.ascontiguousarray(cav[:, 4 * c:4 * c + 4])
        m["cbk"] = np.ascontiguousarray(cbk[:, 4 * c:4 * c + 4])
        m["cbv"] = np.ascontiguousarray(cbv[:, 4 * c:4 * c + 4])
        in_maps.append(m)
    res = run_bass_kernel_spmd(nc, in_maps, core_ids=list(range(NCORES)))
    R = res.results
    yp = np.concatenate([r["yp"] for r in R], axis=0)
    ys = np.concatenate([r["ys"].reshape(4, 64, D_MODEL) for r in R], axis=0)
    cat1 = lambda k: np.concatenate([r[k] for r in R], axis=1)
    pak = cat1("pak").reshape(NL, 16, 512, 8, 64)
    pav = cat1("pav").reshape(NL, 16, 512, 8, 64)
    pbk = cat1("pbk").reshape(NL, 16, SEQ, 4, 128)
    pbv = cat1("pbv").reshape(NL, 16, SEQ, 4, 128)
    sak = cat1("sak").reshape(NL, 32, 64, 8, 64)
    sav = cat1("sav").reshape(NL, 32, 64, 8, 64)
    sbk = cat1("sbk").reshape(NL, 32, 64, 4, 128)
    sbv = cat1("sbv").reshape(NL, 32, 64, 4, 128)
    return (yp.astype(np.float32), ys.astype(np.float32), pak, pav, pbk, pbv, sak, sav, sbk, sbv)
```
